# Optimizing a Trainium2 kernel written in Bass

```python
import jax, jax.numpy as jnp
from jax import lax
import numpy as np

D_MODEL = 1024
BATCH = 8
SEQ = 4096
DEPTH = 1
DEC_BATCH = 32
DEC_SEQ = 4
PAST_LEN = 16384
PAGE_SIZE = 128

A_HEADS = 8
A_HD = 64
A_WIDTH = A_HEADS * A_HD
MOBA_BLOCK = 256
MOBA_TOPK = 3
MOBA_QCHUNK = 32
HG_HEADS = 4
HG_DK = 128
HG_DV = 128
HG_KWIDTH = HG_HEADS * HG_DK
HG_WIDTH = HG_HEADS * HG_DV
HG_CHUNK = 64
D_FF = 2816
CONV_W = 3
N_BRANCH = 2
IN_SIZES = [A_WIDTH, A_WIDTH, A_WIDTH, HG_KWIDTH, HG_KWIDTH, HG_WIDTH, HG_WIDTH, D_MODEL, D_MODEL]
IN_COLS = sum(IN_SIZES)
IN_SPLITS = [int(s) for s in np.cumsum(IN_SIZES)[:-1]]
EPS = 1e-6

kernel_name = "moba_hgrn2_gated_parallel_hybrid_step"


def rmsnorm(x, g):
    xf = x.astype(jnp.float32)
    y = xf * lax.rsqrt(jnp.mean(xf * xf, axis=-1, keepdims=True) + EPS)
    return (y * g.astype(jnp.float32)).astype(x.dtype)


def moba_attention(q, k_full, v_full, q_pos):
    B, Q, H, hd = q.shape
    nb = k_full.shape[1] // MOBA_BLOCK
    kblk = k_full.reshape(B, nb, MOBA_BLOCK, H, hd)
    vblk = v_full.reshape(B, nb, MOBA_BLOCK, H, hd)
    kmean = jnp.mean(kblk.astype(jnp.float32), axis=2)
    n_sel = min(MOBA_TOPK, nb)
    scale = A_HD ** -0.5
    qc = next(c for c in (MOBA_QCHUNK, 16, 8, 4, 2, 1) if Q % c == 0)
    n_chunks = Q // qc
    b_idx = jnp.arange(B)[:, None, None]
    h_idx = jnp.arange(H)[None, :, None]
    tpos = jnp.arange(MOBA_BLOCK, dtype=jnp.int32)

    def chunk(args):
        q_c, pos_c = args
        qf = q_c.astype(jnp.float32)
        own = pos_c[0] // MOBA_BLOCK
        gate = jnp.einsum('bqhd,bnhd->bhqn', qf, kmean)
        gate = jnp.where(jnp.arange(nb) < own, gate, -jnp.inf)
        _, sel = lax.top_k(gate, n_sel)
        scores, v_parts = [], []
        for j in range(n_sel):
            idx = sel[..., j]
            kb = kblk[b_idx, idx, :, h_idx, :]
            s = jnp.einsum('bqhd,bhqtd->bhqt', qf, kb.astype(jnp.float32)) * scale
            scores.append(jnp.where(j < own, s, -jnp.inf))
            v_parts.append(idx)
        k_own = lax.dynamic_slice_in_dim(k_full, own * MOBA_BLOCK, MOBA_BLOCK, axis=1)
        v_own = lax.dynamic_slice_in_dim(v_full, own * MOBA_BLOCK, MOBA_BLOCK, axis=1)
        s_own = jnp.einsum('bqhd,bthd->bhqt', qf, k_own.astype(jnp.float32)) * scale
        causal = (own * MOBA_BLOCK + tpos)[None, :] <= pos_c[:, None]
        scores.append(jnp.where(causal, s_own, -jnp.inf))
        p = jax.nn.softmax(jnp.concatenate(scores, axis=-1), axis=-1)
        out = jnp.einsum('bhqt,bthd->bqhd', p[..., n_sel * MOBA_BLOCK:], v_own.astype(jnp.float32))
        for j in range(n_sel):
            vb = vblk[b_idx, v_parts[j], :, h_idx, :]
            out = out + jnp.einsum('bhqt,bhqtd->bqhd', p[..., j * MOBA_BLOCK:(j + 1) * MOBA_BLOCK],
                                   vb.astype(jnp.float32))
        return out.astype(q.dtype)

    qs = q.reshape(B, n_chunks, qc, H, hd).transpose(1, 0, 2, 3, 4)
    ps = q_pos.reshape(n_chunks, qc)
    o = lax.map(chunk, (qs, ps))
    return o.transpose(1, 0, 2, 3, 4).reshape(B, Q, H * hd)


def hgrn2_recurrence(q, k, logf, v, S0):
    B, T, H, dk = q.shape
    dv = v.shape[-1]
    C = HG_CHUNK if T % HG_CHUNK == 0 else T
    n = T // C

    def split(a):
        return a.astype(jnp.float32).reshape(B, n, C, H, a.shape[-1]).transpose(1, 0, 3, 2, 4)

    tri = jnp.tril(jnp.ones((C, C), dtype=bool))

    def step(S, xs):
        qc, kc, gc, vc = xs
        b = jnp.cumsum(gc, axis=2)
        diff = jnp.where(tri[None, None, :, :, None], b[:, :, :, None, :] - b[:, :, None, :, :], -jnp.inf)
        A = jnp.einsum('bhtd,bhsd,bhtsd->bhts', qc, kc, jnp.exp(diff))
        o = jnp.einsum('bhts,bhsv->bhtv', A, vc) + jnp.einsum('bhtd,bhdv->bhtv', qc * jnp.exp(b), S)
        b_last = b[:, :, -1:, :]
        S_new = jnp.exp(b_last[:, :, 0, :])[..., None] * S + jnp.einsum('bhsd,bhsv->bhdv', kc * jnp.exp(b_last - b), vc)
        return S_new, o

    S_T, o = lax.scan(step, S0.astype(jnp.float32), (split(q), split(k), split(logf), split(v)))
    o = o.transpose(1, 0, 3, 2, 4).reshape(B, T, H, dv)
    return o, S_T


def decoder_layer(x, c, k_past, v_past, S0, conv_prev, lb,
                  norm1_g, norm2_g, w_ada, b_ada, w_in, hg_norm_g, w_a_out, w_b_out, w_o,
                  w_up, conv_w, conv_b, w_down):
    B, T, _ = x.shape
    pos0 = k_past.shape[1]
    mod = jnp.dot(jax.nn.silu(c), w_ada) + b_ada
    sh1, sc1, g1, sh2, sc2, g2 = jnp.split(mod[:, None, :], 6, axis=-1)

    h = rmsnorm(x, norm1_g) * (1 + sc1) + sh1
    proj = jnp.dot(h, w_in)
    aq, ak, av, hq, hf, hi, hg, ga, gb = jnp.split(proj, IN_SPLITS, axis=-1)

    aq = aq.reshape(B, T, A_HEADS, A_HD)
    ak = ak.reshape(B, T, A_HEADS, A_HD)
    av = av.reshape(B, T, A_HEADS, A_HD)
    L = pos0 + T
    nb = -(-L // MOBA_BLOCK)
    pad = jnp.zeros((B, nb * MOBA_BLOCK - L, A_HEADS, A_HD), ak.dtype)
    k_full = jnp.concatenate([k_past.astype(ak.dtype), ak, pad], axis=1)
    v_full = jnp.concatenate([v_past.astype(av.dtype), av, pad], axis=1)
    q_pos = pos0 + jnp.arange(T, dtype=jnp.int32)
    o_a = moba_attention(aq, k_full, v_full, q_pos)

    qh = jax.nn.silu(hq.astype(jnp.float32)).reshape(B, T, HG_HEADS, HG_DK)
    logf = jnp.logaddexp(jnp.log(lb), jnp.log1p(-lb) + jax.nn.log_sigmoid(hf.astype(jnp.float32)))
    kh = -jnp.expm1(logf)
    o_b, S_T = hgrn2_recurrence(qh, kh.reshape(B, T, HG_HEADS, HG_DK), logf.reshape(B, T, HG_HEADS, HG_DK),
                                hi.reshape(B, T, HG_HEADS, HG_DV), S0)
    o_b = rmsnorm(o_b, hg_norm_g.reshape(HG_HEADS, HG_DV)) * jax.nn.silu(hg.astype(jnp.float32)).reshape(B, T, HG_HEADS, HG_DV)
    o_b = o_b.reshape(B, T, HG_WIDTH).astype(x.dtype)

    y_a = jnp.dot(o_a, w_a_out)
    y_b = jnp.dot(o_b, w_b_out)
    mixed = jax.nn.sigmoid(ga) * y_a + jax.nn.sigmoid(gb) * y_b
    x = x + g1 * jnp.dot(mixed, w_o)

    h = rmsnorm(x, norm2_g) * (1 + sc2) + sh2
    u, v = jnp.split(jnp.dot(h, w_up), 2, axis=-1)
    u_ext = jnp.concatenate([conv_prev.astype(u.dtype), u], axis=1)
    u_c = conv_b + conv_w[0] * u_ext[:, 0:T]
    for j in range(1, CONV_W):
        u_c = u_c + conv_w[j] * u_ext[:, j:j + T]
    act = jax.nn.gelu(u_c, approximate=False) * v
    x = x + g2 * jnp.dot(act, w_down)
    new_conv = u_ext[:, -(CONV_W - 1):]
    return x, ak, av, S_T.astype(S0.dtype), new_conv


def setup_inputs(seed: int = 0) -> dict:
    key = jax.random.key(seed)
    ks = jax.random.split(key, 24)
    f32 = jnp.float32
    n_pages = PAST_LEN // PAGE_SIZE
    n_used = DEC_BATCH * n_pages
    n_phys = n_used + n_used // 4

    def nrm(k, shape, s):
        return jax.random.normal(k, shape, f32) * s

    return {
        "x_prompt": nrm(ks[0], (BATCH, SEQ, D_MODEL), 1.0),
        "x_sample": nrm(ks[1], (DEC_BATCH, DEC_SEQ, D_MODEL), 1.0),
        "cache_k": nrm(ks[2], (DEPTH, n_phys, PAGE_SIZE, A_HEADS, A_HD), 1.0),
        "cache_v": nrm(ks[3], (DEPTH, n_phys, PAGE_SIZE, A_HEADS, A_HD), 1.0),
        "state_hgrn": nrm(ks[4], (DEPTH, DEC_BATCH, HG_HEADS, HG_DK, HG_DV), 0.5),
        "state_conv": nrm(ks[5], (DEPTH, DEC_BATCH, CONV_W - 1, D_FF), 1.0),
        "page_table": jax.random.permutation(ks[6], n_phys)[:n_used].reshape(DEC_BATCH, n_pages).astype(jnp.int32),
        "c_prompt": nrm(ks[7], (BATCH, D_MODEL), 1.0),
        "c_sample": nrm(ks[8], (DEC_BATCH, D_MODEL), 1.0),
        "norm1_g": 1.0 + nrm(ks[9], (DEPTH, D_MODEL), 0.05),
        "norm2_g": 1.0 + nrm(ks[10], (DEPTH, D_MODEL), 0.05),
        "w_ada": nrm(ks[11], (DEPTH, D_MODEL, 6 * D_MODEL), 0.5 * D_MODEL ** -0.5),
        "b_ada": nrm(ks[12], (DEPTH, 6 * D_MODEL), 0.01),
        "w_in": nrm(ks[13], (DEPTH, D_MODEL, IN_COLS), D_MODEL ** -0.5),
        "hgrn_lb_logits": nrm(ks[14], (DEPTH + 1, HG_KWIDTH), 0.1),
        "hg_norm_g": 1.0 + nrm(ks[15], (DEPTH, HG_WIDTH), 0.05),
        "w_a_out": nrm(ks[16], (DEPTH, A_WIDTH, D_MODEL), A_WIDTH ** -0.5),
        "w_b_out": nrm(ks[17], (DEPTH, HG_WIDTH, D_MODEL), HG_WIDTH ** -0.5),
        "w_o": nrm(ks[18], (DEPTH, D_MODEL, D_MODEL), D_MODEL ** -0.5),
        "w_up": nrm(ks[19], (DEPTH, D_MODEL, 2 * D_FF), D_MODEL ** -0.5),
        "conv_w": nrm(ks[20], (DEPTH, CONV_W, D_FF), CONV_W ** -0.5),
        "conv_b": nrm(ks[21], (DEPTH, D_FF), 0.01),
        "w_down": nrm(ks[22], (DEPTH, D_FF, D_MODEL), D_FF ** -0.5),
        "final_g": 1.0 + nrm(ks[23], (D_MODEL,), 0.05),
    }


def reference(x_prompt, x_sample, cache_k, cache_v, state_hgrn, state_conv, page_table, c_prompt, c_sample,
              norm1_g, norm2_g, w_ada, b_ada, w_in, hgrn_lb_logits, hg_norm_g, w_a_out, w_b_out, w_o,
              w_up, conv_w, conv_b, w_down, final_g):
    Bp = x_prompt.shape[0]
    Bs = x_sample.shape[0]
    past_len = page_table.shape[1] * PAGE_SIZE
    lower = jnp.cumsum(jax.nn.softmax(hgrn_lb_logits.astype(jnp.float32), axis=0), axis=0)
    xp, xs = x_prompt, x_sample
    kp_l, vp_l, ks_l, vs_l, sp_l, ss_l, cp_l, cs_l = [], [], [], [], [], [], [], []
    for l in range(DEPTH):
        params = (norm1_g[l], norm2_g[l], w_ada[l], b_ada[l], w_in[l], hg_norm_g[l], w_a_out[l], w_b_out[l],
                  w_o[l], w_up[l], conv_w[l], conv_b[l], w_down[l])
        empty = jnp.zeros((Bp, 0, A_HEADS, A_HD), cache_k.dtype)
        xp, kp, vp, sp, cp = decoder_layer(
            xp, c_prompt, empty, empty,
            jnp.zeros((Bp, HG_HEADS, HG_DK, HG_DV), state_hgrn.dtype),
            jnp.zeros((Bp, CONV_W - 1, D_FF), state_conv.dtype), lower[l], *params)
        k_past = cache_k[l][page_table].reshape(Bs, past_len, A_HEADS, A_HD)
        v_past = cache_v[l][page_table].reshape(Bs, past_len, A_HEADS, A_HD)
        xs, kss, vss, ss, cs = decoder_layer(
            xs, c_sample, k_past, v_past, state_hgrn[l], state_conv[l], lower[l], *params)
        kp_l.append(kp); vp_l.append(vp); ks_l.append(kss); vs_l.append(vss)
        sp_l.append(sp); ss_l.append(ss); cp_l.append(cp); cs_l.append(cs)
    y_prompt = rmsnorm(xp, final_g)
    y_sample = rmsnorm(xs, final_g)
    return (y_prompt, y_sample, jnp.stack(kp_l), jnp.stack(vp_l), jnp.stack(ks_l), jnp.stack(vs_l),
            jnp.stack(sp_l), jnp.stack(ss_l), jnp.stack(cp_l), jnp.stack(cs_l))
```

```python
import contextlib
import numpy as np
import concourse.bass as bass
import concourse.mybir as mybir

F32 = mybir.dt.float32
BF16 = mybir.dt.bfloat16
I32 = mybir.dt.int32
U32 = mybir.dt.uint32
AF = mybir.ActivationFunctionType
ALU = mybir.AluOpType
AX = mybir.AxisListType

NDMA = 2


class Buf:
    def __init__(self, prog, name, ap, root=None, lo=0, hi=None):
        self.prog = prog
        self.name = name
        self.ap = ap
        self.root = root if root is not None else self
        self.lo = lo
        self.hi = hi if hi is not None else 1 << 60
        if root is None:
            self.writes = []
            self.reads = []
            self.init_ev = dict(prog.freed_events)

    def sub(self, lo, hi, ap=None):
        return Buf(self.prog, self.name, ap if ap is not None else self.ap, root=self.root, lo=lo, hi=hi)

    def __getitem__(self, key):
        return self.ap[key]


class Prog:
    def __init__(self, nc):
        self.nc = nc
        self.eng = {"pe": nc.tensor, "act": nc.scalar, "dve": nc.vector, "pool": nc.gpsimd, "sp": nc.sync}
        self.stack = contextlib.ExitStack()
        self.sem = {}
        for k in ("pe", "act", "dve", "pool"):
            self.sem[k] = self.stack.enter_context(nc.semaphore("s_" + k))
        for q in ("sp", "pool", "act"):
            for i in range(NDMA):
                self.sem[("d", q, i)] = self.stack.enter_context(nc.semaphore("d_%s%d" % (q, i)))
        self.count = {k: 0 for k in ("pe", "act", "dve", "pool")}
        self.dcount = {"sp": 0, "pool": 0, "act": 0}
        self.waited = {e: {} for e in ("pe", "act", "dve", "pool", "sp")}
        self.freed_events = {}
        self.scopes = []
        self.out_events = {}
        self.nwaits = 0
        self.rings = {}
        self.uid = 0

    @contextlib.contextmanager
    def scope(self):
        st = contextlib.ExitStack()
        bufs = []
        self.scopes.append((st, bufs, {}))
        try:
            yield
        finally:
            self.scopes.pop()
            for b in bufs:
                for (_, _, k, v) in b.writes + b.reads:
                    if self.freed_events.get(k, 0) < v:
                        self.freed_events[k] = v
                for k, v in b.init_ev.items():
                    if self.freed_events.get(k, 0) < v:
                        self.freed_events[k] = v
            st.close()

    def sb(self, name, shape, dtype):
        st, bufs = self.scopes[-1][:2] if self.scopes else (self.stack, [])
        self.uid += 1
        t = st.enter_context(self.nc.sbuf_tensor("%s_%d" % (name, self.uid), list(shape), dtype))
        b = Buf(self, name, t)
        if self.scopes:
            bufs.append(b)
        return b

    def tmp(self, name, shape, dtype, bufs=2):
        st, bufs_l, rings = self.scopes[-1]
        ent = rings.setdefault(name, [[], 0])
        if len(ent[0]) < bufs:
            self.uid += 1
            t = st.enter_context(self.nc.sbuf_tensor("%s_%d" % (name, self.uid), list(shape), dtype))
            b = Buf(self, name, t)
            bufs_l.append(b)
            ent[0].append(b)
            return b
        b = ent[0][ent[1] % bufs]
        ent[1] += 1
        return b

    def ps(self, name, shape, dtype):
        st, bufs = self.scopes[-1][:2] if self.scopes else (self.stack, [])
        t = st.enter_context(self.nc.psum_tensor(name, list(shape), dtype))
        b = Buf(self, name, t)
        b.excl = True
        if self.scopes:
            bufs.append(b)
        return b

    def dram(self, name, shape, dtype, kind="Internal"):
        t = self.nc.dram_tensor(name, list(shape), dtype, kind=kind)
        b = Buf(self, name, t.ap())
        b.init_ev = {}
        return b

    @staticmethod
    def _ov(a, b):
        return a[0] < b.hi and b.lo < a[1]

    def _collect(self, engine, reads, writes):
        raw = {}
        oth = {}

        def add(d, k, v):
            if d.get(k, 0) < v:
                d[k] = v

        for b in reads:
            r = b.root
            for k, v in r.init_ev.items():
                add(raw, k, v)
            for w in r.writes:
                if self._ov(w, b):
                    add(raw, w[2], w[3])
        for b in writes:
            r = b.root
            for k, v in r.init_ev.items():
                add(raw, k, v)
            for w in r.writes:
                if self._ov(w, b):
                    add(oth, w[2], w[3])
            for rd in r.reads:
                if self._ov(rd, b):
                    add(oth, rd[2], rd[3])
        return raw, oth

    def _emit_waits(self, engine, deps, is_dma, extra=None):
        raw, oth = deps
        e = self.eng[engine]
        wd = self.waited[engine]
        need = dict(raw)
        for k, v in oth.items():
            if k == engine and not is_dma:
                continue
            if need.get(k, 0) < v:
                need[k] = v
        if extra:
            for k, v in extra.items():
                if need.get(k, 0) < v:
                    need[k] = v
        for k, v in need.items():
            if wd.get(k, 0) >= v:
                continue
            e.wait_ge(self.sem[k], v)
            self.nwaits += 1
            wd[k] = v

    def _record(self, reads, writes, k, v):
        for b in reads:
            r = b.root
            r.reads = [x for x in r.reads if not (x[2] == k and x[0] == b.lo and x[1] == b.hi)]
            r.reads.append((b.lo, b.hi, k, v))
        for b in writes:
            r = b.root
            r.writes = [x for x in r.writes if not (b.lo <= x[0] and x[1] <= b.hi)]
            r.reads = [x for x in r.reads if not (b.lo <= x[0] and x[1] <= b.hi)]
            r.writes.append((b.lo, b.hi, k, v))
            if b.lo <= 0 and b.hi >= (1 << 60):
                r.init_ev = {}

    def op(self, engine, fn, r=(), w=()):
        ex = [b for b in r if getattr(b.root, "excl", False) and b not in w]
        if ex:
            w = list(w) + ex
        deps = self._collect(engine, r, w)
        self._emit_waits(engine, deps, False)
        ins = fn(self.eng[engine])
        self.count[engine] += 1
        v = self.count[engine]
        ins.then_inc(self.sem[engine], 1)
        self._record(r, w, engine, v)
        return ins

    def dma(self, queue, out, in_, r=(), w=(), is_output=False, **kw):
        n = self.dcount[queue]
        slot = n % NDMA
        k = ("d", queue, slot)
        deps = self._collect(queue, r, w)
        prev = 16 * (n // NDMA)
        self._emit_waits(queue, deps, True, extra={k: prev} if prev > 0 else None)
        ins = self.eng[queue].dma_start(out=out, in_=in_, **kw)
        v = prev + 16
        ins.then_inc(self.sem[k], 16)
        self.dcount[queue] += 1
        self._record(r, w, k, v)
        if is_output:
            self.out_events[k] = max(self.out_events.get(k, 0), v)
        return ins

    def idma(self, out, in_, in_offset, r=(), w=()):
        queue = "pool"
        n = self.dcount[queue]
        slot = n % NDMA
        k = ("d", queue, slot)
        deps = self._collect(queue, r, w)
        prev = 16 * (n // NDMA)
        self._emit_waits(queue, deps, True, extra={k: prev} if prev > 0 else None)
        ins = self.nc.gpsimd.indirect_dma_start(out=out, out_offset=None, in_=in_, in_offset=in_offset)
        v = prev + 16
        ins.then_inc(self.sem[k], 16)
        self.dcount[queue] += 1
        self._record(r, w, k, v)
        return ins

    def finish(self):
        e = self.eng["sp"]
        for q in ("sp", "pool", "act"):
            n = self.dcount[q]
            for slot in range(NDMA):
                cnt = (n - slot + NDMA - 1) // NDMA if n > slot else 0
                if cnt > 0:
                    e.wait_ge(self.sem[("d", q, slot)], 16 * cnt)
        for k in ("pe", "act", "dve", "pool"):
            if self.count[k] > 0:
                e.wait_ge(self.sem[k], self.count[k])
        self.stack.close()


from concourse.bass_utils import run_bass_kernel_spmd
import ml_dtypes

D = 1024
KD = 8
NIN = 5632
FF = 2816
KF = 22
EPS = 1e-6
MB = 8192.0
NEG = -1.0e30


def wchunks():
    L = []
    for c in range(7):
        L.append(("in%d" % c, "w_in", 0, 8, c * 512))
    for half in range(2):
        L.append(("in%d" % (7 + half), "w_in", 0, 8, (7 + half) * 512))
        L.append(("in%d" % (9 + half), "w_in", 0, 8, (9 + half) * 512))
        L.append(("wa%d" % half, "w_a_out", 0, 4, half * 512))
        L.append(("wb%d" % half, "w_b_out", 0, 4, half * 512))
    for half in range(2):
        L.append(("wo%d" % half, "w_o", 0, 8, half * 512))
    for c in range(11):
        L.append(("up%d" % c, "w_up", 0, 8, c * 512))
    for half in range(2):
        for kg, (k0, nk) in enumerate(((0, 8), (8, 8), (16, 6))):
            L.append(("dn%d_%d" % (half, kg), "w_down", k0, nk, half * 512))
    return L


class StopBuild(Exception):
    pass


def build(nc, cfg):
    stage = cfg.get("stage", 99)

    def ckpt(n):
        if stage <= n:
            raise StopBuild()
    T = cfg["T"]; NS = cfg["NS"]; NPG = cfg["NPG"]; NPHYS = cfg["NPHYS"]
    NSEQ = 1 + NS
    NST = T // 512
    NB_S = NPG // 2
    P = Prog(nc)

    def din(name, shape, dt=F32):
        return nc.dram_tensor(name, list(shape), dt, kind="ExternalInput").ap()

    def dout(name, shape, dt=F32):
        return nc.dram_tensor(name, list(shape), dt, kind="ExternalOutput").ap()

    xp = din("xp", [T, D]); xs = din("xs", [NS * 128, D])
    cT = din("cT", [128, KD, NSEQ])
    cache_k = din("cache_k", [NPHYS * 128, 512]); cache_v = din("cache_v", [NPHYS * 128, 512])
    ptab = din("ptab", [NS, NPG], I32)
    shg = din("shg", [NS, 4, 128, 128])
    convT = din("convT", [NS, 128, KF, 2])
    W = {"w_ada": din("w_ada", [D, 6 * D]), "w_in": din("w_in", [D, NIN]), "w_a_out": din("w_a_out", [512, D]),
         "w_b_out": din("w_b_out", [512, D]), "w_o": din("w_o", [D, D]), "w_up": din("w_up", [D, 2 * FF]),
         "w_down": din("w_down", [FF, D])}
    n1gT = din("n1gT", [128, KD]); n2gT = din("n2gT", [128, KD]); badaT = din("badaT", [128, 48])
    lbT = din("lbT", [128, 4, 2]); convwT = din("convwT", [128, KF, 3]); convbT = din("convbT", [128, KF])
    hgn_bc = din("hgn_bc", [128, 512]); fg_bc = din("fg_bc", [128, D])
    c_identb = din("c_identb", [128, 128], BF16); c_identf = din("c_identf", [128, 128])
    c_tri = din("c_tri", [128, 128]); c_causal = din("c_causal", [128, 128], BF16)
    c_esel = din("c_esel", [128, 17 * 128], BF16)
    c_cm4 = din("c_cm4", [4, 32]); c_eye4 = din("c_eye4", [4, 4], BF16)
    c_hmask = din("c_hmask", [32, 512]); c_selq = din("c_selq", [128, 4], BF16)

    yp = dout("yp", [T, D]); ys = dout("ys", [NS * 128, D])
    kp = dout("kp", [T, 512]); vp = dout("vp", [T, 512])
    ksn = dout("ksn", [NS * 128, 512]); vsn = dout("vsn", [NS * 128, 512])
    hgp = dout("hgp", [4, 128, 128]); hgs = dout("hgs", [NS, 4, 128, 128])
    cvp = dout("cvp", [128, KF, 2]); cvs = dout("cvs", [NS, 128, KF, 2])

    WCH = wchunks()
    NCH = len(WCH)
    wscr = [P.dram("wscr%d" % i, [128, 8 * 512], BF16) for i in range(NCH)]

    NSLOT = 4
    wslot = [P.sb("wslot%d" % i, [128, 8, 512], BF16) for i in range(NSLOT)]
    identb = P.sb("identb", [128, 128], BF16); identf = P.sb("identf", [128, 128], F32)
    tri = P.sb("tri", [128, 128], F32); causal = P.sb("causal", [128, 128], BF16)
    esel = P.sb("esel", [128, 17 * 128], BF16)
    cm4 = P.sb("cm4", [4, 32], F32); eye4 = P.sb("eye4", [4, 4], BF16)
    hmask = P.sb("hmask", [32, 512], F32); selq = P.sb("selq", [128, 4], BF16)
    onesf = P.sb("onesf", [128, 128], F32); onesb = P.sb("onesb", [128, 128], BF16)
    hgn = P.sb("hgn", [128, 512], F32); fg = P.sb("fg", [128, D], F32)
    modT = P.sb("modT", [128, 48, NSEQ], F32)
    s1 = P.sb("s1", [128, KD, NSEQ], F32); s2 = P.sb("s2", [128, KD, NSEQ], F32)
    lb = P.sb("lb", [128, 4], F32); oml = P.sb("oml", [128, 4], F32); noml = P.sb("noml", [128, 4], F32)
    cw = P.sb("cw", [128, KF, 3], F32); cb = P.sb("cb", [128, KF], F32)
    G1 = P.sb("G1", [128, D], F32); G2 = P.sb("G2", [128, D], F32)
    gstate = {"seq": None}
    pf = [P.ps("pf%d" % i, [128, 512], F32) for i in range(6)]
    pbf = [P.ps("pbf%d" % i, [128, 1024], BF16) for i in range(2)]
    rr = {"pf": 0, "pbf": 0}

    def nxt_pf(group=(0, 1, 2, 3)):
        i = group[rr["pf"] % len(group)]
        rr["pf"] += 1
        return pf[i]

    def nxt_pbf():
        i = rr["pbf"] % 2
        rr["pbf"] += 1
        return pbf[i]

    def mm(out, lhsT, rhs, start, stop, r, w):
        return P.op("pe", lambda e: e.matmul(out, lhsT, rhs, start=start, stop=stop, skip_group_check=True), r=r, w=w)

    def tr(out, in_, ident, r, w):
        return P.op("pe", lambda e: e.transpose(out, in_, ident), r=r, w=w)

    for sbt, dr in ((identb, c_identb), (identf, c_identf), (tri, c_tri), (causal, c_causal), (esel, c_esel),
                    (cm4, c_cm4), (eye4, c_eye4), (hmask, c_hmask), (selq, c_selq), (hgn, hgn_bc), (fg, fg_bc),
                    (cw, convwT), (cb, convbT)):
        P.dma("sp", sbt[:], dr, w=[sbt])
    P.op("dve", lambda e: e.memset(onesf[:], 1.0), w=[onesf])
    P.op("dve", lambda e: e.memset(onesb[:], 1.0), w=[onesb])

    def setup_phase():
      with P.scope():
          cTs = P.sb("cTs", [128, KD, NSEQ], F32); scb = P.sb("scb", [128, KD, 8], BF16)
          n1g = P.sb("n1g", [128, KD], F32); n2g = P.sb("n2g", [128, KD], F32); bada = P.sb("bada", [128, 48], F32)
          lbl = P.sb("lbl", [128, 4, 2], F32); lbd = P.sb("lbd", [128, 4], F32)
          P.dma("sp", cTs[:], cT, w=[cTs]); P.dma("sp", n1g[:], n1gT, w=[n1g]); P.dma("sp", n2g[:], n2gT, w=[n2g])
          P.dma("sp", bada[:], badaT, w=[bada]); P.dma("sp", lbl[:], lbT, w=[lbl])
          P.op("act", lambda e: e.activation(out=scb[:, :, 0:NSEQ], in_=cTs[:], func=AF.Silu), r=[cTs], w=[scb])
          P.op("dve", lambda e: e.tensor_tensor(out=lbd[:], in0=lbl[:, :, 0], in1=lbl[:, :, 1], op=ALU.subtract), r=[lbl], w=[lbd])
          P.op("act", lambda e: e.activation(out=lb[:], in_=lbd[:], func=AF.Sigmoid), r=[lbd], w=[lb])
          P.op("dve", lambda e: e.tensor_scalar(out=oml[:], in0=lb[:], scalar1=-1.0, scalar2=1.0, op0=ALU.mult, op1=ALU.add), r=[lb], w=[oml])
          P.op("dve", lambda e: e.tensor_scalar(out=noml[:], in0=oml[:], scalar1=-1.0, scalar2=None, op0=ALU.mult), r=[oml], w=[noml])
          if stage <= -3:
              raise StopBuild()
          mps = pf[0]
          for c12 in range(12):
              sl = wslot[c12 % NSLOT]
              P.dma("pool", sl[:], W["w_ada"][:, c12 * 512:(c12 + 1) * 512].rearrange("(k p) j -> p k j", p=128), w=[sl])
              for m in range(4):
                  ch = c12 * 4 + m
                  for k in range(KD):
                      mm(mps[:, ch * NSEQ:(ch + 1) * NSEQ], sl[:, k, m * 128:(m + 1) * 128], scb[:, k, 0:NSEQ],
                         start=(k == 0), stop=(k == KD - 1), r=[sl, scb], w=[mps])
          P.op("dve", lambda e: e.tensor_tensor(out=modT[:], in0=mps[:, 0:48 * NSEQ].rearrange("p (c s) -> p c s", s=NSEQ),
                                                in1=bada[:].unsqueeze(2).to_broadcast([128, 48, NSEQ]), op=ALU.add), r=[mps, bada], w=[modT])
          for (sd, ng, off) in ((s1, n1g, 8), (s2, n2g, 32)):
              P.op("dve", lambda e: e.tensor_scalar(out=sd[:], in0=modT[:, off:off + 8, :], scalar1=1.0, scalar2=None, op0=ALU.add), r=[modT], w=[sd])
              P.op("dve", lambda e: e.tensor_tensor(out=sd[:], in0=sd[:], in1=ng[:].unsqueeze(2).to_broadcast([128, KD, NSEQ]), op=ALU.mult), r=[sd, ng], w=[sd])
          if stage <= -2:
              raise StopBuild()
          for i, (nm, wn, k0, nk, c0) in enumerate(WCH[:(cfg.get('ncast', 999) if cfg.get('wmode', 'scratch') == 'scratch' else 0)]):
              sl = wslot[i % NSLOT]
              src = W[wn][k0 * 128:(k0 + nk) * 128, c0:c0 + 512].rearrange("(k p) j -> p k j", p=128)
              P.dma("pool", sl[:, 0:nk, :], src, w=[sl])
              P.dma("sp", wscr[i][:, 0:nk * 512], sl[:, 0:nk, :].rearrange("p k j -> p (k j)"), r=[sl], w=[wscr[i]])


    try:
        setup_phase()
    except StopBuild:
        P.finish()
        return P
    if stage <= -1:
        P.finish()
        return P

    ckpt_setup = True
    ws = {"pos": 0, "issued": 0}

    def w_issue_upto(n):
        while ws["issued"] < n:
            i = ws["issued"]
            ci = i % NCH
            nk = WCH[ci][3]
            sl = wslot[i % NSLOT]
            if cfg.get("wmode", "scratch") == "scratch":
                P.dma("sp", sl[:, 0:nk, :].rearrange("p k j -> p (k j)"), wscr[ci][:, 0:nk * 512], r=[wscr[ci]], w=[sl])
            else:
                (nm_, wn_, k0_, nk_, c0_) = WCH[ci]
                src_ = W[wn_][k0_ * 128:(k0_ + nk_) * 128, c0_:c0_ + 512].rearrange("(k p) j -> p k j", p=128)
                P.dma("pool", sl[:, 0:nk_, :], src_, w=[sl])
            ws["issued"] += 1

    def wget(name):
        i = ws["pos"]
        ci = i % NCH
        assert WCH[ci][0] == name, (WCH[ci][0], name)
        w_issue_upto(i + 3)
        ws["pos"] += 1
        return wslot[i % NSLOT]

    def build_G(seq):
        if gstate["seq"] == seq:
            return
        gstate["seq"] = seq
        for (G, off) in ((G1, 16), (G2, 40)):
            for k in range(KD):
                dg = P.tmp("dg", [128, 128], F32)
                P.op("dve", lambda e: e.tensor_scalar(out=dg[:], in0=identf[:], scalar1=modT[:, off + k, seq:seq + 1], scalar2=None, op0=ALU.mult), r=[identf, modT], w=[dg])
                pb = nxt_pf(group=(4, 5))
                mm(pb[:, 0:128], onesf[:, 0:128], dg[:], True, True, r=[onesf, dg], w=[pb])
                P.op("act", lambda e: e.copy(out=G[:, k * 128:(k + 1) * 128], in_=pb[:, 0:128]), r=[pb], w=[G])

    def rstd_from_ssq(ssq, n, cols, scale):
        P.op("act", lambda e: e.activation(out=ssq[:n, cols], in_=ssq[:n, cols], func=AF.Sqrt, scale=scale, bias=epsb[:n, 0:1]), r=[ssq, epsb], w=[ssq])
        P.op("dve", lambda e: e.reciprocal(out=ssq[:n, cols], in_=ssq[:n, cols]), r=[ssq], w=[ssq])

    epsb = P.sb("epsb", [128, 1], F32)
    P.op("dve", lambda e: e.memset(epsb[:], EPS), w=[epsb])
    nmb = P.sb("nmb", [128, 1], F32)
    P.op("dve", lambda e: e.memset(nmb[:], -MB / 8.0), w=[nmb])

    nrm_cnt = {"n": 0}

    def norm_to_hT(src_ap, src_bufs, n, seq, sc, shoff, hT, col0):
        nrm_cnt["n"] += 1
        xn = P.tmp("xn", [128, D], BF16)
        ssq = P.tmp("ssq", [128, 1], F32)
        P.op("act", lambda e: e.activation(out=xn[:n, :], in_=src_ap, func=AF.Square, accum_out=ssq[:n, 0:1]), r=src_bufs, w=[xn, ssq])
        rstd_from_ssq(ssq, n, slice(0, 1), 1.0 / D)
        P.op("dve", lambda e: e.tensor_scalar(out=xn[:n, :], in0=src_ap, scalar1=ssq[:n, 0:1], scalar2=None, op0=ALU.mult), r=src_bufs + [ssq], w=[xn])
        pb = nxt_pbf()
        for k in range(KD):
            tr(pb[:, k * 128:k * 128 + n], xn[:n, k * 128:(k + 1) * 128], identb[:n, :n], r=[xn, identb], w=[pb])
        for k in range(KD):
            if nrm_cnt["n"] % 2 == 0:
                P.op("dve", lambda e: e.tensor_scalar(out=hT[:, k, col0:col0 + n], in0=pb[:, k * 128:k * 128 + n], scalar1=sc[:, k, seq:seq + 1],
                                                      scalar2=modT[:, shoff + k, seq:seq + 1], op0=ALU.mult, op1=ALU.add), r=[pb, sc, modT], w=[hT])
            else:
                P.op("act", lambda e: e.activation(out=hT[:, k, col0:col0 + n], in_=pb[:, k * 128:k * 128 + n], func=AF.Identity,
                                                   scale=sc[:, k, seq:seq + 1], bias=modT[:, shoff + k, seq:seq + 1]), r=[pb, sc, modT], w=[hT])

    NSEG = max(1, NS)
    HS = P.sb("HS", [128, NSEG, KF, 2], F32)
    Sst = [P.sb("S%d" % h, [128, 128], F32) for h in range(4)]
    Sbf = [P.sb("Sbf%d" % h, [128, 128], BF16) for h in range(4)]

    def supertile(tiles, is_prompt, st_idx, attn_ctx):
        coff = []
        c = 0
        for t in tiles:
            coff.append(c); c += t[1]
        N = c
        NT = len(tiles)
        nseg = 1 if is_prompt else NT
        segn = N // nseg
        nr = 128 if is_prompt else 4
        nrs = segn if is_prompt else 4
        with P.scope():
            hT = P.sb("hT", [128, KD, N], BF16)
            x2 = P.sb("x2", [128, NT, D], F32)
            with P.scope():
                oaT = P.sb("oaT", [128, 4, N], BF16)
                obT = P.sb("obT", [128, 4, N], BF16)
                ckpt(0)
                with P.scope():
                    for i, (seq, n, xd, outs) in enumerate(tiles):
                        xt = P.tmp("xt", [128, D], F32)
                        P.dma("sp", xt[:n, :], xd, w=[xt])
                        norm_to_hT(xt[:n, :], [xt], n, seq, s1, 0, hT, coff[i])

                ckpt(1)

                def amm(sl, i, pb):
                    n = tiles[i][1]
                    for k in range(KD):
                        mm(pb[:n, :], hT[:, k, coff[i]:coff[i] + n], sl[:, k, :], k == 0, k == KD - 1, r=[sl, hT], w=[pb])

                with P.scope():
                    qT = P.sb("qT", [128, 8, N], BF16)
                    P.op("dve", lambda e: e.memset(qT[:], 0.0), w=[qT])
                    akT = P.sb("akT", [128, 4, N], BF16) if not is_prompt else None
                    vtok = P.sb("vtok", [128, NT, 512], BF16) if not is_prompt else None
                    sl = wget("in0")
                    for m in range(4):
                        pb = nxt_pf(); bmmh(sl, m, pb, hT, N)
                        P.op("act", lambda e: e.copy(out=qT[0:64, 2 * m, :], in_=pb[0:64, 0:N]), r=[pb], w=[qT])
                        P.op("act", lambda e: e.copy(out=qT[64:128, 2 * m + 1, :], in_=pb[64:128, 0:N]), r=[pb], w=[qT])
                    sl = wget("in1")
                    KTs = attn_ctx["KT"][st_idx] if is_prompt else akT
                    for m in range(4):
                        pb = nxt_pf(); bmmh(sl, m, pb, hT, N)
                        P.op("dve", lambda e: e.tensor_copy(out=KTs[:, m, :], in_=pb[:, 0:N]), r=[pb], w=[KTs])
                    for i, (seq, n, xd, outs) in enumerate(tiles):
                        pb = nxt_pf(); amm(sl, i, pb)
                        kst = P.tmp("kvst", [128, 512], F32)
                        P.op("act", lambda e: e.copy(out=kst[:n, :], in_=pb[:n, :]), r=[pb], w=[kst])
                        if not cfg.get("dbg_nokv"):
                            P.dma("pool", outs["k"], kst[:n, :], r=[kst], is_output=True)
                    if is_prompt:
                        kms = P.tmp("kms", [128, 4, 2], F32)
                        P.op("dve", lambda e: e.tensor_reduce(out=kms[:], in_=KTs[:].rearrange("p c (b t) -> p c b t", b=2), axis=AX.X, op=ALU.add), r=[KTs], w=[kms])
                        kmT = attn_ctx["kmT"]
                        P.op("act", lambda e: e.activation(out=kmT[:, :, 2 * st_idx:2 * st_idx + 2], in_=kms[:], func=AF.Copy, scale=1.0 / 256.0), r=[kms], w=[kmT])
                    sl = wget("in2")
                    for i, (seq, n, xd, outs) in enumerate(tiles):
                        pb = nxt_pf(); amm(sl, i, pb)
                        vst = P.tmp("kvst", [128, 512], F32)
                        P.op("act", lambda e: e.copy(out=vst[:n, :], in_=pb[:n, :]), r=[pb], w=[vst])
                        if not cfg.get("dbg_nokv"):
                            P.dma("pool", outs["v"], vst[:n, :], r=[vst], is_output=True)
                        if is_prompt:
                            Vs = attn_ctx["V"][st_idx]
                            P.op("dve", lambda e: e.tensor_copy(out=Vs[:, i, :, 0:64], in_=pb[:, :].rearrange("p (h d) -> p h d", h=8)), r=[pb], w=[Vs])
                        elif not cfg.get("dbg_novtok"):
                            P.op("dve", lambda e: e.tensor_copy(out=vtok[:n, i, :], in_=pb[:n, :]), r=[pb], w=[vtok])
                    ckpt(2)
                    if is_prompt:
                        attn_prompt(st_idx, attn_ctx, qT, oaT)
                    else:
                        for j, (seq, n, xd, outs) in enumerate(tiles):
                            attn_sample(j, qT, akT, vtok, oaT, coff[j])
                ckpt(3)
                with P.scope():
                    hi_t = P.sb("hi_t", [128, NT, 512], BF16)
                    hg_t = P.sb("hg_t", [128, NT, 512], BF16)
                    sl3 = wget("in3")
                    qts = [P.sb("qt%d" % h, [128, N], F32) for h in range(4)]
                    for h in range(4):
                        pb = nxt_pf(); bmmh(sl3, h, pb, hT, N)
                        P.op("act", lambda e: e.activation(out=qts[h][:], in_=pb[:, 0:N], func=AF.Silu), r=[pb], w=[qts[h]])
                    sl4 = wget("in4")
                    Qh = [P.sb("Qh%d" % h, [128, N], BF16) for h in range(4)]
                    Qm = [P.sb("Qm%d" % h, [128, N], BF16) for h in range(4)]
                    Km = [P.sb("Km%d" % h, [128, N], BF16) for h in range(4)]
                    Kl = [P.sb("Kl%d" % h, [128, N], BF16) for h in range(4)]
                    elast = [P.sb("el%d" % h, [128, NT], F32) for h in range(4)]
                    for h in range(4):
                        pb = nxt_pf(); bmmh(sl4, h, pb, hT, N)
                        sig = P.tmp("sig", [128, N], F32, bufs=1); kk = P.tmp("kk", [128, N], F32, bufs=1)
                        bb = P.tmp("bb", [128, N], F32, bufs=1); eb = P.tmp("eb", [128, N], F32, bufs=1)
                        P.op("act", lambda e: e.activation(out=sig[:], in_=pb[:, 0:N], func=AF.Sigmoid), r=[pb], w=[sig])
                        P.op("dve", lambda e: e.tensor_scalar(out=kk[:], in0=sig[:], scalar1=noml[:, h:h + 1], scalar2=oml[:, h:h + 1], op0=ALU.mult, op1=ALU.add), r=[sig, noml, oml], w=[kk])
                        P.op("act", lambda e: e.activation(out=sig[:], in_=sig[:], func=AF.Ln, scale=oml[:, h:h + 1], bias=lb[:, h:h + 1]), r=[sig, oml, lb], w=[sig])
                        if nr < 128:
                            P.op("dve", lambda e: e.memset(bb[:], 0.0), w=[bb])
                        for i, (seq, n, xd, outs) in enumerate(tiles):
                            cs = slice(coff[i], coff[i] + nr)
                            P.op("dve", lambda e: e.tensor_tensor_scan(out=bb[:, cs], data0=onesf[:, 0:nr], data1=sig[:, cs], initial=0.0, op0=ALU.mult, op1=ALU.add), r=[onesf, sig], w=[bb])
                        P.op("act", lambda e: e.activation(out=eb[:], in_=bb[:], func=AF.Exp), r=[bb], w=[eb])
                        P.op("dve", lambda e: e.tensor_tensor(out=qts[h][:], in0=qts[h][:], in1=eb[:], op=ALU.mult), r=[qts[h], eb], w=[qts[h]])
                        P.op("act", lambda e: e.copy(out=Qh[h][:], in_=qts[h][:]), r=[qts[h]], w=[Qh[h]])
                        P.op("dve", lambda e: e.reciprocal(out=bb[:], in_=eb[:]), r=[eb], w=[bb])
                        P.op("dve", lambda e: e.tensor_tensor(out=kk[:], in0=kk[:], in1=bb[:], op=ALU.mult), r=[kk, bb], w=[kk])
                        for i, (seq, n, xd, outs) in enumerate(tiles):
                            cs = slice(coff[i], coff[i] + n)
                            mid = coff[i] + nr // 2; last = coff[i] + nr - 1
                            P.op("pool", lambda e: e.tensor_copy(out=elast[h][:, i:i + 1], in_=eb[:, last:last + 1]), r=[eb], w=[elast[h]])
                            P.op("dve", lambda e: e.tensor_scalar(out=Qm[h][:, cs], in0=qts[h][:, cs], scalar1=bb[:, mid:mid + 1], scalar2=None, op0=ALU.mult), r=[qts[h], bb], w=[Qm[h]])
                            P.op("dve", lambda e: e.tensor_scalar(out=Km[h][:, cs], in0=kk[:, cs], scalar1=eb[:, mid:mid + 1], scalar2=None, op0=ALU.mult), r=[kk, eb], w=[Km[h]])
                            P.op("dve", lambda e: e.tensor_scalar(out=Kl[h][:, cs], in0=kk[:, cs], scalar1=eb[:, last:last + 1], scalar2=None, op0=ALU.mult), r=[kk, eb], w=[Kl[h]])
                            if nr < 128:
                                P.op("dve", lambda e: e.memset(Kl[h][:, coff[i] + nr:coff[i] + n], 0.0), w=[Kl[h]])
                    sl5 = wget("in5")
                    for i, (seq, n, xd, outs) in enumerate(tiles):
                        pb = nxt_pf(); amm(sl5, i, pb)
                        P.op("act", lambda e: e.copy(out=hi_t[:n, i, :], in_=pb[:n, :]), r=[pb], w=[hi_t])
                    sl6 = wget("in6")
                    for i, (seq, n, xd, outs) in enumerate(tiles):
                        pb = nxt_pf(); amm(sl6, i, pb)
                        P.op("act", lambda e: e.activation(out=hg_t[:n, i, :], in_=pb[:n, :], func=AF.Silu), r=[pb], w=[hg_t])
                    for i, (seq, n, xd, outs) in enumerate(tiles):
                        cs = slice(coff[i], coff[i] + n)
                        if not is_prompt:
                            for h in range(4):
                                P.dma("sp", Sst[h][:], shg[seq - 1, h], w=[Sst[h]])
                                P.op("act", lambda e: e.copy(out=Sbf[h][:], in_=Sst[h][:]), r=[Sst[h]], w=[Sbf[h]])
                        ob = P.tmp("ob", [128, 512], F32)
                        for h in range(4):
                            hs = slice(h * 128, (h + 1) * 128)
                            pa = nxt_pf()
                            mm(pa[:n, 0:n], Km[h][:, cs], Qm[h][:, cs], True, True, r=[Km[h], Qm[h]], w=[pa])
                            AT = P.tmp("AT", [128, 128], BF16)
                            P.op("dve", lambda e: e.tensor_tensor(out=AT[:n, :n], in0=pa[:n, 0:n], in1=tri[:n, :n], op=ALU.mult), r=[pa, tri], w=[AT])
                            pt_ = nxt_pbf()
                            tr(pt_[:n, 0:128], Kl[h][:, cs], identb[:, :], r=[Kl[h], identb], w=[pt_])
                            Kls = P.tmp("Kls", [128, 128], BF16)
                            P.op("act", lambda e: e.copy(out=Kls[:n, :], in_=pt_[:n, 0:128]), r=[pt_], w=[Kls])
                            po = nxt_pf()
                            mm(po[:n, 0:128], AT[:n, :n], hi_t[:n, i, hs], True, False, r=[AT, hi_t], w=[po])
                            mm(po[:n, 0:128], Qh[h][:, cs], Sbf[h][:], False, True, r=[Qh[h], Sbf[h]], w=[po])
                            P.op("act", lambda e: e.copy(out=ob[:n, hs], in_=po[:n, 0:128]), r=[po], w=[ob])
                            pS = nxt_pf()
                            mm(pS[:, 0:128], Kls[:n, :], hi_t[:n, i, hs], True, True, r=[Kls, hi_t], w=[pS])
                            P.op("dve", lambda e: e.scalar_tensor_tensor(out=Sst[h][:], in0=Sst[h][:], scalar=elast[h][:, i:i + 1], in1=pS[:, 0:128], op0=ALU.mult, op1=ALU.add), r=[Sst[h], elast[h], pS], w=[Sst[h]])
                            P.op("act", lambda e: e.copy(out=Sbf[h][:], in_=Sst[h][:]), r=[Sst[h]], w=[Sbf[h]])
                        if not is_prompt:
                            for h in range(4):
                                P.dma("pool", hgs[seq - 1, h], Sst[h][:], r=[Sst[h]], is_output=True)
                        sq = P.tmp("sq", [128, 512], F32, bufs=1); ss4 = P.tmp("ss4", [128, 4], F32)
                        P.op("pool", lambda e: e.tensor_tensor(out=sq[:n, :], in0=ob[:n, :], in1=ob[:n, :], op=ALU.mult), r=[ob], w=[sq])
                        P.op("dve", lambda e: e.tensor_reduce(out=ss4[:n, :], in_=sq[:n, :].rearrange("p (h d) -> p h d", h=4), axis=AX.X, op=ALU.add), r=[sq], w=[ss4])
                        rstd_from_ssq(ss4, n, slice(0, 4), 1.0 / 128.0)
                        P.op("dve", lambda e: e.tensor_tensor(out=sq[:n, :].rearrange("p (h d) -> p h d", h=4), in0=ob[:n, :].rearrange("p (h d) -> p h d", h=4),
                                                              in1=ss4[:n, :].unsqueeze(2).to_broadcast([n, 4, 128]), op=ALU.mult), r=[ob, ss4], w=[sq])
                        P.op("pool", lambda e: e.tensor_tensor(out=sq[:n, :], in0=sq[:n, :], in1=hgn[:n, :], op=ALU.mult), r=[sq, hgn], w=[sq])
                        obn = P.tmp("obn", [128, 512], BF16)
                        P.op("dve", lambda e: e.tensor_tensor(out=obn[:n, :], in0=sq[:n, :], in1=hg_t[:n, i, :], op=ALU.mult), r=[sq, hg_t], w=[obn])
                        pt_ = nxt_pbf()
                        for c4 in range(4):
                            tr(pt_[:, c4 * 128:c4 * 128 + n], obn[:n, c4 * 128:(c4 + 1) * 128], identb[:n, :n], r=[obn, identb], w=[pt_])
                        P.op("act", lambda e: e.copy(out=obT[:, :, cs], in_=pt_[:, 0:512].rearrange("p (c t) -> p c t", c=4)[:, :, 0:n]), r=[pt_], w=[obT])
                ckpt(4)
                with P.scope():
                    mixT = P.sb("mixT", [128, KD, N], BF16)
                    for half in range(2):
                        sga = P.tmp("sga", [128, 4, N], BF16, bufs=1); sgb = P.tmp("sgb", [128, 4, N], BF16, bufs=1)
                        sl = wget("in%d" % (7 + half))
                        for m in range(4):
                            pb = nxt_pf(); bmmh(sl, m, pb, hT, N)
                            P.op("act", lambda e: e.activation(out=sga[:, m, :], in_=pb[:, 0:N], func=AF.Sigmoid), r=[pb], w=[sga])
                        sl = wget("in%d" % (9 + half))
                        for m in range(4):
                            pb = nxt_pf(); bmmh(sl, m, pb, hT, N)
                            P.op("act", lambda e: e.activation(out=sgb[:, m, :], in_=pb[:, 0:N], func=AF.Sigmoid), r=[pb], w=[sgb])
                        sl = wget("wa%d" % half)
                        for m in range(4):
                            pb = nxt_pf()
                            for k in range(4):
                                mm(pb[:, 0:N], sl[:, k, m * 128:(m + 1) * 128], oaT[:, k, :], k == 0, k == 3, r=[sl, oaT], w=[pb])
                            P.op("dve", lambda e: e.tensor_tensor(out=sga[:, m, :], in0=sga[:, m, :], in1=pb[:, 0:N], op=ALU.mult), r=[sga, pb], w=[sga])
                        sl = wget("wb%d" % half)
                        for m in range(4):
                            pb = nxt_pf()
                            for k in range(4):
                                mm(pb[:, 0:N], sl[:, k, m * 128:(m + 1) * 128], obT[:, k, :], k == 0, k == 3, r=[sl, obT], w=[pb])
                            P.op("dve", lambda e: e.tensor_tensor(out=sgb[:, m, :], in0=sgb[:, m, :], in1=pb[:, 0:N], op=ALU.mult), r=[sgb, pb], w=[sgb])
                            P.op("pool", lambda e: e.tensor_tensor(out=mixT[:, half * 4 + m, :], in0=sga[:, m, :], in1=sgb[:, m, :], op=ALU.add), r=[sga, sgb], w=[mixT])
                    for half in range(2):
                        sl = wget("wo%d" % half)
                        for i, (seq, n, xd, outs) in enumerate(tiles):
                            build_G(seq)
                            pb = nxt_pf()
                            for k in range(KD):
                                mm(pb[:n, :], mixT[:, k, coff[i]:coff[i] + n], sl[:, k, :], k == 0, k == KD - 1, r=[sl, mixT], w=[pb])
                            xt6 = P.tmp("xt6", [128, 512], F32)
                            P.dma("sp", xt6[:n, :], xd[:, half * 512:(half + 1) * 512], w=[xt6])
                            tmp = P.tmp("tmp", [128, 512], F32)
                            P.op("dve", lambda e: e.tensor_tensor(out=tmp[:n, :], in0=pb[:n, :], in1=G1[:n, half * 512:(half + 1) * 512], op=ALU.mult), r=[pb, G1], w=[tmp])
                            P.op("pool", lambda e: e.tensor_tensor(out=x2[:n, i, half * 512:(half + 1) * 512], in0=tmp[:n, :], in1=xt6[:n, :], op=ALU.add), r=[tmp, xt6], w=[x2])
            ckpt(5)
            with P.scope():
                for i, (seq, n, xd, outs) in enumerate(tiles):
                    norm_to_hT(x2[:n, i, :], [x2], n, seq, s2, 24, hT, coff[i])
            with P.scope():
                actT = P.sb("actT", [128, KF, N], BF16)
                if not is_prompt:
                    for j, (seq, n, xd, outs) in enumerate(tiles):
                        P.dma("sp", HS[:, j], convT[seq - 1], w=[HS])
                cur = {"name": None, "sl": None}

                def upchunk(col):
                    nm = "up%d" % (col // 512)
                    if cur["name"] != nm:
                        cur["sl"] = wget(nm); cur["name"] = nm
                    return cur["sl"], (col % 512) // 128

                for f in range(KF):
                    sl, m = upchunk(f * 128)
                    pb = nxt_pf(); bmmh(sl, m, pb, hT, N)
                    ue = P.tmp("ue", [128, nseg, segn + 2], F32)
                    P.op("pool", lambda e: e.tensor_copy(out=ue[:, :, 0:2], in_=HS[:, 0:nseg, f, :]), r=[HS], w=[ue])
                    P.op("act", lambda e: e.copy(out=ue[:, :, 2:2 + segn], in_=pb[:, 0:N].rearrange("p (s t) -> p s t", s=nseg)), r=[pb], w=[ue])
                    P.op("pool", lambda e: e.tensor_copy(out=HS[:, 0:nseg, f, :], in_=ue[:, :, nrs:nrs + 2]), r=[ue], w=[HS])
                    t1 = P.tmp("t1", [128, nseg, segn], F32); t2 = P.tmp("t2", [128, nseg, segn], F32, bufs=1)
                    P.op("dve", lambda e: e.tensor_scalar(out=t1[:], in0=ue[:, :, 2:2 + segn], scalar1=cw[:, f, 2:3], scalar2=cb[:, f:f + 1], op0=ALU.mult, op1=ALU.add), r=[ue, cw, cb], w=[t1])
                    P.op("dve", lambda e: e.scalar_tensor_tensor(out=t2[:], in0=ue[:, :, 1:1 + segn], scalar=cw[:, f, 1:2], in1=t1[:], op0=ALU.mult, op1=ALU.add), r=[ue, cw, t1], w=[t2])
                    P.op("dve", lambda e: e.scalar_tensor_tensor(out=t1[:], in0=ue[:, :, 0:segn], scalar=cw[:, f, 0:1], in1=t2[:], op0=ALU.mult, op1=ALU.add), r=[ue, cw, t2], w=[t1])
                    P.op("act", lambda e: e.activation(out=actT[:, f, :], in_=t1[:].rearrange("p s t -> p (s t)"), func=AF.Gelu), r=[t1], w=[actT])
                for f in range(KF):
                    sl, m = upchunk(FF + f * 128)
                    pb = nxt_pf(); bmmh(sl, m, pb, hT, N)
                    P.op("dve", lambda e: e.tensor_tensor(out=actT[:, f, :], in0=actT[:, f, :], in1=pb[:, 0:N], op=ALU.mult), r=[actT, pb], w=[actT])
                if not is_prompt:
                    for j, (seq, n, xd, outs) in enumerate(tiles):
                        P.dma("pool", cvs[seq - 1], HS[:, j], r=[HS], is_output=True)
                ckpt(6)
                for half in range(2):
                    accs = [pf[i] for i in range(NT)]
                    for kg, (k0, nk) in enumerate(((0, 8), (8, 8), (16, 6))):
                        sl = wget("dn%d_%d" % (half, kg))
                        for i, (seq, n, xd, outs) in enumerate(tiles):
                            for k in range(nk):
                                mm(accs[i][:n, :], actT[:, k0 + k, coff[i]:coff[i] + n], sl[:, k, :], (kg == 0 and k == 0), (kg == 2 and k == nk - 1), r=[sl, actT], w=[accs[i]])
                    for i, (seq, n, xd, outs) in enumerate(tiles):
                        build_G(seq)
                        hsl = slice(half * 512, (half + 1) * 512)
                        tmp = P.tmp("tmp", [128, 512], F32)
                        P.op("dve", lambda e: e.tensor_tensor(out=tmp[:n, :], in0=accs[i][:n, :], in1=G2[:n, hsl], op=ALU.mult), r=[accs[i], G2], w=[tmp])
                        P.op("pool", lambda e: e.tensor_tensor(out=x2[:n, i, hsl], in0=tmp[:n, :], in1=x2[:n, i, hsl], op=ALU.add), r=[tmp, x2], w=[x2])
            with P.scope():
                for i, (seq, n, xd, outs) in enumerate(tiles):
                    junk = P.tmp("junk", [128, D], BF16, bufs=1); ssq = P.tmp("ssq", [128, 1], F32)
                    P.op("act", lambda e: e.activation(out=junk[:n, :], in_=x2[:n, i, :], func=AF.Square, accum_out=ssq[:n, 0:1]), r=[x2], w=[junk, ssq])
                    rstd_from_ssq(ssq, n, slice(0, 1), 1.0 / D)
                    for half in range(2):
                        hsl = slice(half * 512, (half + 1) * 512)
                        yo = P.tmp("yo", [128, 512], F32)
                        P.op("dve", lambda e: e.scalar_tensor_tensor(out=yo[:n, :], in0=x2[:n, i, hsl], scalar=ssq[:n, 0:1], in1=fg[:n, hsl], op0=ALU.mult, op1=ALU.mult), r=[x2, ssq, fg], w=[yo])
                        P.dma("pool", outs["y"][:, hsl], yo[:n, :], r=[yo], is_output=True)

    def bmmh(sl, m, pb, hT, N):
        for k in range(KD):
            mm(pb[:, 0:N], sl[:, k, m * 128:(m + 1) * 128], hT[:, k, :], k == 0, k == KD - 1, r=[sl, hT], w=[pb])

    def attn_prompt(s, ctx, qT, oaT):
        KT = ctx["KT"]; V = ctx["V"]; kmT = ctx["kmT"]
        with P.scope():
            maskT = P.sb("maskT", [128, 8, 512], BF16)
            P.op("dve", lambda e: e.memset(maskT[:], 0.0), w=[maskT])
            oatok = P.sb("oatok", [128, 4, 512], BF16)
            for qi in range(4):
                own = 2 * s + qi // 2
                cs = slice(qi * 128, (qi + 1) * 128)
                pg = nxt_pf()
                for h in range(8):
                    mm(pg[:, h * 16:(h + 1) * 16], qT[:, h, cs], kmT[:, h // 2, :], True, True, r=[qT, kmT], w=[pg])
                gsb = P.tmp("gsb", [128, 8, 16], F32)
                P.op("act", lambda e: e.copy(out=gsb[:], in_=pg[:, 0:128].rearrange("p (h b) -> p h b", h=8)), r=[pg], w=[gsb])
                if own < 16:
                    P.op("dve", lambda e: e.memset(gsb[:, :, own:16], NEG), w=[gsb])
                m8 = P.tmp("m8", [128, 8, 8], F32)
                if own >= 3:
                    for h in range(8):
                        P.op("dve", lambda e: e.max(out=m8[:, h, :], in_=gsb[:, h, :]), r=[gsb], w=[m8])
                else:
                    P.op("dve", lambda e: e.memset(m8[:], NEG * 0.1), w=[m8])
                mp = P.tmp("mp", [128, 8, 32], BF16)
                P.op("dve", lambda e: e.memset(mp[:], 0.0), w=[mp])
                P.op("dve", lambda e: e.memset(mp[:, :, 16:17], MB), w=[mp])
                selt = P.tmp("selt", [128, 8, 16], F32)
                P.op("dve", lambda e: e.tensor_tensor(out=selt[:], in0=gsb[:], in1=m8[:, :, 2:3].to_broadcast([128, 8, 16]), op=ALU.is_ge), r=[gsb, m8], w=[selt])
                P.op("dve", lambda e: e.tensor_scalar(out=mp[:, :, 0:16], in0=selt[:], scalar1=MB, scalar2=None, op0=ALU.mult), r=[selt], w=[mp])
                pt_ = nxt_pbf()
                for h in range(8):
                    tr(pt_[:32, h * 128:(h + 1) * 128], mp[:, h, :], identb[:, :], r=[mp, identb], w=[pt_])
                P.op("act", lambda e: e.copy(out=maskT[0:32, :, cs], in_=pt_[:32, :].rearrange("p (h q) -> p h q", h=8)), r=[pt_], w=[maskT])
            if cfg.get("astage", 9) <= 0:
                raise StopBuild()
            nkt = 4 * (s + 1)
            for h in range(8):
                rows = slice((h % 2) * 64, (h % 2) * 64 + 64)
                hp = h // 2
                oacc = pf[2 + h % 2]
                first_pv = True
                for kt in range(nkt):
                    blk = kt // 2
                    st_k = kt // 4
                    KTt = KT[st_k]; Vt = V[st_k]
                    kcs = slice((kt % 4) * 128, (kt % 4) * 128 + 128)
                    sc = pf[4 + kt % 2]
                    if blk == 2 * s + 1:
                        c0 = 256
                    else:
                        c0 = 0
                    ops = []
                    ops.append((sc[:, c0:512], KTt[:, hp, kcs], qT[:, h, c0:512], [KTt, qT]))
                    valid = []
                    if blk < 2 * s:
                        ops.append((sc[:, 0:512], esel[:, blk * 128:(blk + 1) * 128], maskT[:, h, 0:512], [esel, maskT]))
                        valid = [0, 1, 2, 3]
                    else:
                        if blk == 2 * s:
                            ops.append((sc[:, 256:512], esel[:, blk * 128:(blk + 1) * 128], maskT[:, h, 256:512], [esel, maskT]))
                            valid = [2, 3]
                            ki = kt - 4 * s; qbase = 0
                        else:
                            ki = kt - 4 * s - 2; qbase = 2
                        for ql in range(2):
                            qi = qbase + ql
                            qcs = slice(qi * 128, (qi + 1) * 128)
                            if ki < ql:
                                ops.append((sc[:, qcs], esel[:, 16 * 128:17 * 128], maskT[:, h, qcs], [esel, maskT]))
                                valid.append(qi)
                            elif ki == ql:
                                ops.append((sc[:, qcs], identb[:, :], causal[:, :], [identb, causal]))
                                valid.append(qi)
                    for oi, (o_, l_, r_, bufs) in enumerate(ops):
                        mm(o_, l_, r_, oi == 0, oi == len(ops) - 1, r=bufs, w=[sc])
                    if cfg.get("astage", 9) <= 1:
                        raise StopBuild()
                    PT = P.tmp("PT", [128, 512], BF16)
                    P.op("act", lambda e: e.activation(out=PT[:, c0:512], in_=sc[:, c0:512], func=AF.Exp, scale=0.125, bias=nmb[:, 0:1]), r=[sc, nmb], w=[PT])
                    if cfg.get("astage", 9) <= 2:
                        raise StopBuild()
                    for qi in sorted(valid):
                        mm(oacc[:, qi * 65:(qi + 1) * 65], PT[:, qi * 128:(qi + 1) * 128], Vt[:, kt % 4, h, 0:65], first_pv, False, r=[PT, Vt], w=[oacc])
                        first_pv = False
                if cfg.get("astage", 9) <= 3:
                    raise StopBuild()
                rs = P.tmp("rs", [128, 4, 1], F32)
                ov = oacc[:, 0:260].rearrange("p (q d) -> p q d", q=4)
                P.op("dve", lambda e: e.reciprocal(out=rs[:], in_=ov[:, :, 64:65]), r=[oacc], w=[rs])
                P.op("dve", lambda e: e.tensor_tensor(out=oatok[:, :, h * 64:(h + 1) * 64], in0=ov[:, :, 0:64], in1=rs[:].to_broadcast([128, 4, 64]), op=ALU.mult), r=[oacc, rs], w=[oatok])
            for qi in range(4):
                pt_ = nxt_pbf()
                for c4 in range(4):
                    tr(pt_[:, c4 * 128:(c4 + 1) * 128], oatok[:, qi, c4 * 128:(c4 + 1) * 128], identb[:, :], r=[oatok, identb], w=[pt_])
                P.op("act", lambda e: e.copy(out=oaT[:, :, qi * 128:(qi + 1) * 128], in_=pt_[:, 0:512].rearrange("p (c t) -> p c t", c=4)), r=[pt_], w=[oaT])

    def attn_sample(j, qT, akT, vtok, oaT, c0):
        NB = NB_S
        cs = slice(c0, c0 + 4)
        with P.scope():
            scst = P.sb("scst", [128, NPG, 32], F32)
            kps = P.sb("kps", [128, 4, NPG], F32)
            Qbd = P.sb("Qbd", [128, 4, 8], BF16)
            ptb = P.sb("ptb", [128, NPG], I32); ptf = P.sb("ptf", [128, NPG], F32); pidx = P.sb("pidx", [128, NPG], I32)
            iot = P.sb("iot", [128, 1], F32)
            P.dma("sp", ptb[:], ptab[j:j + 1, :].partition_broadcast(128), w=[ptb])
            P.op("pool", lambda e: e.iota(iot[:], pattern=[[0, 1]], base=0, channel_multiplier=1, allow_small_or_imprecise_dtypes=True), w=[iot])
            P.op("dve", lambda e: e.tensor_copy(out=ptf[:], in_=ptb[:]), r=[ptb], w=[ptf])
            P.op("dve", lambda e: e.tensor_scalar(out=ptf[:], in0=ptf[:], scalar1=128.0, scalar2=iot[:, 0:1], op0=ALU.mult, op1=ALU.add), r=[ptf, iot], w=[ptf])
            P.op("dve", lambda e: e.tensor_copy(out=pidx[:], in_=ptf[:]), r=[ptf], w=[pidx])
            qv = qT[:, :, cs].rearrange("p (a two) t -> p a two t", two=2)
            P.op("dve", lambda e: e.tensor_copy(out=Qbd[:, :, 0:4], in_=qv[:, :, 0, :]), r=[qT], w=[Qbd])
            P.op("dve", lambda e: e.tensor_copy(out=Qbd[:, :, 4:8], in_=qv[:, :, 1, :]), r=[qT], w=[Qbd])
            if cfg.get("sastage", 99) <= 0:
                raise StopBuild()
            scps = None
            for p in range(NPG):
                Kp = P.tmp("Kp", [128, 512], BF16)
                P.idma(out=Kp[:], in_=cache_k, in_offset=bass.IndirectOffsetOnAxis(ap=pidx[:, p:p + 1], axis=0), r=[pidx], w=[Kp])
                pt_ = nxt_pbf()
                for c4 in range(4):
                    tr(pt_[:, c4 * 128:(c4 + 1) * 128], Kp[:, c4 * 128:(c4 + 1) * 128], identb[:, :], r=[Kp, identb], w=[pt_])
                KTt = P.tmp("KTt", [128, 4, 128], BF16)
                P.op("act", lambda e: e.copy(out=KTt[:], in_=pt_[:, 0:512].rearrange("p (c t) -> p c t", c=4)), r=[pt_], w=[KTt])
                P.op("dve", lambda e: e.tensor_reduce(out=kps[:, :, p:p + 1], in_=KTt[:], axis=AX.X, op=ALU.add), r=[KTt], w=[kps])
                if p % 16 == 0:
                    scps = pf[4 + (p // 16) % 2]
                for hp in range(4):
                    o0 = (p % 16) * 32 + hp * 8
                    mm(scps[:, o0:o0 + 8], KTt[:, hp, :], Qbd[:, hp, :], True, True, r=[KTt, Qbd], w=[scps])
                if p % 16 == 15 or p == NPG - 1:
                    p0 = (p // 16) * 16
                    npp = p - p0 + 1
                    P.op("act", lambda e: e.copy(out=scst[:, p0:p0 + npp, :], in_=scps[:, 0:npp * 32].rearrange("p (a b) -> p a b", b=32)), r=[scps], w=[scst])
            if cfg.get("sastage", 99) <= 1:
                raise StopBuild()
            pn = nxt_pf()
            for hp in range(4):
                mm(pn[:4, hp * 8:(hp + 1) * 8], akT[:, hp, cs], Qbd[:, hp, :], True, True, r=[akT, Qbd], w=[pn])
            scn = P.sb("scn", [4, 32], F32); pnew = P.sb("pnew", [128, 32], BF16)
            P.op("dve", lambda e: e.memset(pnew[:], 0.0), w=[pnew])
            P.op("dve", lambda e: e.tensor_tensor(out=scn[:], in0=pn[:4, 0:32], in1=cm4[:], op=ALU.add), r=[pn, cm4], w=[scn])
            P.op("act", lambda e: e.activation(out=pnew[:4, :], in_=scn[:], func=AF.Exp, scale=0.125, bias=nmb[:4, 0:1]), r=[scn, nmb], w=[pnew])
            if cfg.get("sastage", 99) <= 2:
                raise StopBuild()
            kmS = P.sb("kmS", [128, 4, NB], F32); kmb = P.sb("kmb", [128, 4, NB], BF16)
            P.op("dve", lambda e: e.tensor_reduce(out=kmS[:], in_=kps[:].rearrange("p c (b t) -> p c b t", t=2), axis=AX.X, op=ALU.add), r=[kps], w=[kmS])
            P.op("act", lambda e: e.activation(out=kmb[:], in_=kmS[:], func=AF.Copy, scale=1.0 / 256.0), r=[kmS], w=[kmb])
            pg = nxt_pf()
            for h in range(8):
                mm(pg[:4, h * NB:(h + 1) * NB], qT[:, h, cs], kmb[:, h // 2, :], True, True, r=[qT, kmb], w=[pg])
            gsb = P.sb("gsbs", [4, 8, NB], F32); m8 = P.sb("m8s", [4, 8, 8], F32)
            P.op("act", lambda e: e.copy(out=gsb[:], in_=pg[:4, 0:8 * NB].rearrange("p (h b) -> p h b", h=8)), r=[pg], w=[gsb])
            for h in range(8):
                P.op("dve", lambda e: e.max(out=m8[:, h, :], in_=gsb[:, h, :]), r=[gsb], w=[m8])
            selt = P.sb("selts", [4, 8, NB], F32)
            P.op("dve", lambda e: e.tensor_tensor(out=selt[:], in0=gsb[:], in1=m8[:, :, 2:3].to_broadcast([4, 8, NB]), op=ALU.is_ge), r=[gsb, m8], w=[selt])
            mpT = P.sb("mpT", [4, NB, 8], BF16)
            P.op("dve", lambda e: e.tensor_scalar(out=mpT[:], in0=selt[:].rearrange("p h b -> p b h"), scalar1=MB, scalar2=None, op0=ALU.mult), r=[selt], w=[mpT])
            X = P.sb("X", [128, NB * 8, 4], BF16)
            P.op("dve", lambda e: e.memset(X[:], 0.0), w=[X])
            P.op("dve", lambda e: e.tensor_tensor(out=X[:4], in0=mpT[:].rearrange("p b h -> p (b h)").unsqueeze(2).to_broadcast([4, NB * 8, 4]),
                                                  in1=eye4[:].unsqueeze(1).to_broadcast([4, NB * 8, 4]), op=ALU.mult), r=[mpT, eye4], w=[X])
            if cfg.get("sastage", 99) <= 3:
                raise StopBuild()
            PTs = P.sb("PTs", [128, NPG, 32], BF16)
            Xf = X[:].rearrange("p a b -> p (a b)")
            nbank = (NB * 32 + 511) // 512
            for bk in range(nbank):
                w0 = bk * 512; w1 = min(NB * 32, w0 + 512); wd = w1 - w0
                pm = nxt_pf()
                mm(pm[:, 0:wd], onesb[:, :], Xf[:, w0:w1], True, True, r=[onesb, X], w=[pm])
                nblk = wd // 32; b0 = w0 // 32
                sm = P.tmp("sm", [128, 16, 2, 32], F32)
                P.op("dve", lambda e: e.tensor_tensor(out=sm[:, 0:nblk], in0=scst[:, 2 * b0:2 * (b0 + nblk), :].rearrange("p (b t) c -> p b t c", t=2),
                                                      in1=pm[:, 0:wd].rearrange("p (b c) -> p b c", c=32).unsqueeze(2).to_broadcast([128, nblk, 2, 32]), op=ALU.add), r=[scst, pm], w=[sm])
                P.op("act", lambda e: e.activation(out=PTs[:, 2 * b0:2 * (b0 + nblk), :].rearrange("p (b t) c -> p b t c", t=2), in_=sm[:, 0:nblk], func=AF.Exp, scale=0.125, bias=nmb[:, 0:1]), r=[sm, nmb], w=[PTs])
            if cfg.get("sastage", 99) <= 4:
                raise StopBuild()
            oS = pf[2]; rS = pf[3]
            for p in range(NPG):
                Vp = P.tmp("Vp", [128, 512], BF16)
                P.idma(out=Vp[:], in_=cache_v, in_offset=bass.IndirectOffsetOnAxis(ap=pidx[:, p:p + 1], axis=0), r=[pidx], w=[Vp])
                mm(oS[:32, :], PTs[:, p, :], Vp[:], p == 0, False, r=[PTs, Vp], w=[oS])
                mm(rS[:32, 0:1], PTs[:, p, :], onesb[:, 0:1], p == 0, False, r=[PTs, onesb], w=[rS])
            jt = c0 // 128
            mm(oS[:32, :], pnew[:], vtok[:, jt, :], False, True, r=[pnew, vtok], w=[oS])
            mm(rS[:32, 0:1], pnew[:], onesb[:, 0:1], False, True, r=[pnew, onesb], w=[rS])
            if cfg.get("sastage", 99) <= 5:
                raise StopBuild()
            rr_ = P.sb("rrs", [32, 1], F32); msk = P.sb("msk", [128, 512], BF16)
            P.op("dve", lambda e: e.memset(msk[:], 0.0), w=[msk])
            P.op("dve", lambda e: e.reciprocal(out=rr_[:], in_=rS[:32, 0:1]), r=[rS], w=[rr_])
            P.op("dve", lambda e: e.scalar_tensor_tensor(out=msk[:32, :], in0=oS[:32, :], scalar=rr_[:, 0:1], in1=hmask[:], op0=ALU.mult, op1=ALU.mult), r=[oS, rr_, hmask], w=[msk])
            po = nxt_pf()
            mm(po[:4, :], selq[:], msk[:], True, True, r=[selq, msk], w=[po])
            oat = P.sb("oat", [128, 512], BF16)
            P.op("dve", lambda e: e.memset(oat[:], 0.0), w=[oat])
            P.op("act", lambda e: e.copy(out=oat[:4, :], in_=po[:4, :]), r=[po], w=[oat])
            pt_ = nxt_pbf()
            for c4 in range(4):
                tr(pt_[:, c4 * 128:(c4 + 1) * 128], oat[:, c4 * 128:(c4 + 1) * 128], identb[:, :], r=[oat, identb], w=[pt_])
            P.op("act", lambda e: e.copy(out=oaT[:, :, cs], in_=pt_[:, 0:512].rearrange("p (c t) -> p c t", c=4)[:, :, 0:4]), r=[pt_], w=[oaT])

    def main_program():
        if cfg.get("do_prompt", 1):
            P.op("dve", lambda e: e.memset(HS[:], 0.0), w=[HS])
            for h in range(4):
                P.op("dve", lambda e: e.memset(Sst[h][:], 0.0), w=[Sst[h]])
                P.op("dve", lambda e: e.memset(Sbf[h][:], 0.0), w=[Sbf[h]])
            with P.scope():
                ctx = {"KT": [P.sb("KT%d" % s, [128, 4, 512], BF16) for s in range(NST)],
                       "V": [P.sb("V%d" % s, [128, 4, 8, 66], BF16) for s in range(NST)],
                       "kmT": P.sb("kmT", [128, 4, 16], BF16)}
                P.op("dve", lambda e: e.memset(ctx["kmT"][:], 0.0), w=[ctx["kmT"]])
                for s in range(NST):
                    P.op("dve", lambda e: e.memset(ctx["V"][s][:], 1.0), w=[ctx["V"][s]])
                for s in range(NST):
                    tiles = []
                    for i in range(4):
                        r0 = s * 512 + i * 128
                        tiles.append((0, 128, xp[r0:r0 + 128, :], {"k": kp[r0:r0 + 128, :], "v": vp[r0:r0 + 128, :], "y": yp[r0:r0 + 128, :]}))
                    supertile(tiles, True, s, ctx)
                for h in range(4):
                    P.dma("pool", hgp[h], Sst[h][:], r=[Sst[h]], is_output=True)
                P.dma("pool", cvp, HS[:, 0], r=[HS], is_output=True)
        if cfg.get("do_sample", 1):
            tiles = []
            for j in range(NS):
                tiles.append(((1 + j) if not cfg.get("dbg_seq0") else 0, 128, xs[j * 128:(j + 1) * 128, :], {"k": ksn[j * 128:(j + 1) * 128, :], "v": vsn[j * 128:(j + 1) * 128, :], "y": ys[j * 128:(j + 1) * 128, :]}))
            supertile(tiles, False, 0, None)


    try:
        main_program()
    except StopBuild:
        pass
    P.finish()
    return P


def host_consts():
    bf = ml_dtypes.bfloat16
    c = {}
    c["c_identb"] = np.eye(128, dtype=np.float32).astype(bf)
    c["c_identf"] = np.eye(128, dtype=np.float32)
    s = np.arange(128)
    c["c_tri"] = (s[:, None] <= s[None, :]).astype(np.float32)
    c["c_causal"] = ((s[:, None] <= s[None, :]).astype(np.float32) * MB).astype(bf)
    es = np.zeros((128, 17, 128), np.float32)
    for j in range(17):
        es[j, j, :] = 1.0
    c["c_esel"] = es.reshape(128, 17 * 128).astype(bf)
    cm = np.zeros((4, 8, 4), np.float32)
    for k in range(4):
        for q in range(4):
            cm[k, :, q] = MB if k <= q else 0.0
    c["c_cm4"] = cm.reshape(4, 32)
    c["c_eye4"] = np.eye(4, dtype=np.float32).astype(bf)
    hm = np.zeros((8, 4, 8, 64), np.float32)
    for h in range(8):
        hm[h, :, h, :] = 1.0
    c["c_hmask"] = hm.reshape(32, 512)
    sq = np.zeros((8, 4, 4), np.float32)
    for q in range(4):
        sq[:, q, q] = 1.0
    sqp = np.zeros((128, 4), np.float32)
    sqp[:32] = sq.reshape(32, 4)
    c["c_selq"] = sqp.astype(bf)
    return c


def make_in_maps(cfg, ncores, x_prompt, x_sample, cache_k, cache_v, state_hgrn, state_conv, page_table, c_prompt, c_sample,
                 norm1_g, norm2_g, w_ada, b_ada, w_in, hgrn_lb_logits, hg_norm_g, w_a_out, w_b_out, w_o,
                 w_up, conv_w, conv_b, w_down, final_g):
    T = cfg["T"]; NS = cfg["NS"]; NPG = cfg["NPG"]; NPHYS = cfg["NPHYS"]
    f = np.float32
    A = lambda a: np.ascontiguousarray(np.asarray(a))
    consts = host_consts()
    shared = dict(consts)
    shared["cache_k"] = A(np.asarray(cache_k)[0, :NPHYS]).reshape(NPHYS * 128, 512)
    shared["cache_v"] = A(np.asarray(cache_v)[0, :NPHYS]).reshape(NPHYS * 128, 512)
    shared["w_ada"] = A(w_ada)[0]; shared["w_in"] = A(w_in)[0]; shared["w_a_out"] = A(w_a_out)[0]
    shared["w_b_out"] = A(w_b_out)[0]; shared["w_o"] = A(w_o)[0]; shared["w_up"] = A(w_up)[0]; shared["w_down"] = A(w_down)[0]
    shared["n1gT"] = A(A(norm1_g)[0].reshape(KD, 128).T); shared["n2gT"] = A(A(norm2_g)[0].reshape(KD, 128).T)
    shared["badaT"] = A(A(b_ada)[0].reshape(48, 128).T)
    shared["lbT"] = A(A(hgrn_lb_logits).reshape(2, 4, 128).transpose(2, 1, 0))
    shared["convwT"] = A(A(conv_w)[0].reshape(3, KF, 128).transpose(2, 1, 0))
    shared["convbT"] = A(A(conv_b)[0].reshape(KF, 128).T)
    shared["hgn_bc"] = A(np.broadcast_to(A(hg_norm_g)[0][None, :], (128, 512)))
    shared["fg_bc"] = A(np.broadcast_to(A(final_g)[None, :], (128, D)))
    maps = []
    for c in range(ncores):
        m = dict(shared)
        m["xp"] = A(x_prompt)[c, :T] if T > 0 else np.zeros((0, D), f)
        xsp = np.zeros((NS, 128, D), f)
        xsp[:, :4, :] = A(x_sample)[c * NS:(c + 1) * NS]
        m["xs"] = xsp.reshape(NS * 128, D)
        cc = np.concatenate([A(c_prompt)[c:c + 1], A(c_sample)[c * NS:(c + 1) * NS]], axis=0)
        m["cT"] = A(cc.reshape(1 + NS, KD, 128).transpose(2, 1, 0))
        m["ptab"] = A(page_table)[c * NS:(c + 1) * NS, :NPG].astype(np.int32)
        m["shg"] = A(state_hgrn)[0, c * NS:(c + 1) * NS]
        m["convT"] = A(A(state_conv)[0, c * NS:(c + 1) * NS].reshape(NS, 2, KF, 128).transpose(0, 3, 2, 1))
        maps.append(m)
    return maps


def gather_outputs(cfg, ncores, res):
    T = cfg["T"]; NS = cfg["NS"]
    R = res.results
    y_prompt = np.stack([R[c]["yp"] for c in range(ncores)])
    y_sample = np.concatenate([R[c]["ys"].reshape(NS, 128, D)[:, :4] for c in range(ncores)])
    k_prompt = np.stack([R[c]["kp"].reshape(T, 8, 64) for c in range(ncores)])[None]
    v_prompt = np.stack([R[c]["vp"].reshape(T, 8, 64) for c in range(ncores)])[None]
    k_sample = np.concatenate([R[c]["ksn"].reshape(NS, 128, 8, 64)[:, :4] for c in range(ncores)])[None]
    v_sample = np.concatenate([R[c]["vsn"].reshape(NS, 128, 8, 64)[:, :4] for c in range(ncores)])[None]
    hg_p = np.stack([R[c]["hgp"] for c in range(ncores)])[None]
    hg_s = np.concatenate([R[c]["hgs"] for c in range(ncores)])[None]
    cv_p = np.stack([R[c]["cvp"].transpose(2, 1, 0).reshape(2, FF) for c in range(ncores)])[None]
    cv_s = np.concatenate([R[c]["cvs"].transpose(0, 3, 2, 1).reshape(NS, 2, FF) for c in range(ncores)])[None]
    outs = (y_prompt, y_sample, k_prompt, v_prompt, k_sample, v_sample, hg_p, hg_s, cv_p, cv_s)
    return tuple(np.ascontiguousarray(o.astype(np.float32)) for o in outs)


def kernel(**inputs):
    cfg = {"T": 4096, "NS": 4, "NPG": 128, "NPHYS": 5120, "do_prompt": 1, "do_sample": 1}
    nc = bass.Bass("TRN2", target_bir_lowering=False)
    build(nc, cfg)
    maps = make_in_maps(cfg, 8, **inputs)
    res = run_bass_kernel_spmd(nc, maps, core_ids=list(range(8)))
    return gather_outputs(cfg, 8, res)
```

```python
import contextlib
import numpy as np
import concourse.bass as bass
import concourse.mybir as mybir

F32 = mybir.dt.float32
BF16 = mybir.dt.bfloat16
I32 = mybir.dt.int32
U32 = mybir.dt.uint32
AF = mybir.ActivationFunctionType
ALU = mybir.AluOpType
AX = mybir.AxisListType

NDMA = 6
NDMA_Q = {"sp": 6, "pool": 2, "act": 2}


class Buf:
    def __init__(self, prog, name, ap, root=None, lo=0, hi=None):
        self.prog = prog
        self.name = name
        self.ap = ap
        self.root = root if root is not None else self
        self.lo = lo
        self.hi = hi if hi is not None else 1 << 60
        if root is None:
            self.writes = []
            self.reads = []
            self.init_ev = dict(prog.freed_events)

    def sub(self, lo, hi, ap=None):
        return Buf(self.prog, self.name, ap if ap is not None else self.ap, root=self.root, lo=lo, hi=hi)

    def __getitem__(self, key):
        return self.ap[key]


class Prog:
    def __init__(self, nc):
        self.nc = nc
        self.eng = {"pe": nc.tensor, "act": nc.scalar, "dve": nc.vector, "pool": nc.gpsimd, "sp": nc.sync}
        self.stack = contextlib.ExitStack()
        self.sem = {}
        for k in ("pe", "act", "dve", "pool"):
            self.sem[k] = self.stack.enter_context(nc.semaphore("s_" + k))
        for q in ("sp", "pool", "act"):
            for i in range(NDMA):
                self.sem[("d", q, i)] = self.stack.enter_context(nc.semaphore("d_%s%d" % (q, i)))
        self.count = {k: 0 for k in ("pe", "act", "dve", "pool")}
        self.dcount = {"sp": 0, "pool": 0, "act": 0}
        self.waited = {e: {} for e in ("pe", "act", "dve", "pool", "sp")}
        self.freed_events = {}
        self.scopes = []
        self.out_events = {}
        self.nwaits = 0
        self.rings = {}
        self.uid = 0

    @contextlib.contextmanager
    def scope(self):
        st = contextlib.ExitStack()
        bufs = []
        self.scopes.append((st, bufs, {}))
        try:
            yield
        finally:
            self.scopes.pop()
            for b in bufs:
                for (_, _, k, v) in b.writes + b.reads:
                    if self.freed_events.get(k, 0) < v:
                        self.freed_events[k] = v
                for k, v in b.init_ev.items():
                    if self.freed_events.get(k, 0) < v:
                        self.freed_events[k] = v
            st.close()

    def sb(self, name, shape, dtype):
        st, bufs = self.scopes[-1][:2] if self.scopes else (self.stack, [])
        self.uid += 1
        t = st.enter_context(self.nc.sbuf_tensor("%s_%d" % (name, self.uid), list(shape), dtype))
        b = Buf(self, name, t)
        if self.scopes:
            bufs.append(b)
        return b

    def tmp(self, name, shape, dtype, bufs=2):
        st, bufs_l, rings = self.scopes[-1]
        ent = rings.setdefault(name, [[], 0])
        if len(ent[0]) < bufs:
            self.uid += 1
            t = st.enter_context(self.nc.sbuf_tensor("%s_%d" % (name, self.uid), list(shape), dtype))
            b = Buf(self, name, t)
            bufs_l.append(b)
            ent[0].append(b)
            return b
        b = ent[0][ent[1] % bufs]
        ent[1] += 1
        return b

    def ps(self, name, shape, dtype):
        st, bufs = self.scopes[-1][:2] if self.scopes else (self.stack, [])
        t = st.enter_context(self.nc.psum_tensor(name, list(shape), dtype))
        b = Buf(self, name, t)
        b.excl = True
        if self.scopes:
            bufs.append(b)
        return b

    def dram(self, name, shape, dtype, kind="Internal"):
        t = self.nc.dram_tensor(name, list(shape), dtype, kind=kind)
        b = Buf(self, name, t.ap())
        b.init_ev = {}
        return b

    @staticmethod
    def _ov(a, b):
        return a[0] < b.hi and b.lo < a[1]

    def _collect(self, engine, reads, writes):
        raw = {}
        oth = {}

        def add(d, k, v):
            if d.get(k, 0) < v:
                d[k] = v

        for b in reads:
            r = b.root
            for k, v in r.init_ev.items():
                add(raw, k, v)
            for w in r.writes:
                if self._ov(w, b):
                    add(raw, w[2], w[3])
        for b in writes:
            r = b.root
            for k, v in r.init_ev.items():
                add(raw, k, v)
            for w in r.writes:
                if self._ov(w, b):
                    add(oth, w[2], w[3])
            for rd in r.reads:
                if self._ov(rd, b):
                    add(oth, rd[2], rd[3])
        return raw, oth

    def _emit_waits(self, engine, deps, is_dma, extra=None):
        raw, oth = deps
        e = self.eng[engine]
        wd = self.waited[engine]
        need = dict(raw)
        for k, v in oth.items():
            if k == engine and not is_dma:
                continue
            if need.get(k, 0) < v:
                need[k] = v
        if extra:
            for k, v in extra.items():
                if need.get(k, 0) < v:
                    need[k] = v
        for k, v in need.items():
            if wd.get(k, 0) >= v:
                continue
            e.wait_ge(self.sem[k], v)
            self.nwaits += 1
            wd[k] = v

    def _record(self, reads, writes, k, v):
        for b in reads:
            r = b.root
            r.reads = [x for x in r.reads if not (x[2] == k and x[0] == b.lo and x[1] == b.hi)]
            r.reads.append((b.lo, b.hi, k, v))
        for b in writes:
            r = b.root
            r.writes = [x for x in r.writes if not (b.lo <= x[0] and x[1] <= b.hi)]
            r.reads = [x for x in r.reads if not (b.lo <= x[0] and x[1] <= b.hi)]
            r.writes.append((b.lo, b.hi, k, v))
            if b.lo <= 0 and b.hi >= (1 << 60):
                r.init_ev = {}

    def op(self, engine, fn, r=(), w=()):
        ex = [b for b in r if getattr(b.root, "excl", False) and b not in w]
        if ex:
            w = list(w) + ex
        deps = self._collect(engine, r, w)
        self._emit_waits(engine, deps, False)
        ins = fn(self.eng[engine])
        self.count[engine] += 1
        v = self.count[engine]
        ins.then_inc(self.sem[engine], 1)
        self._record(r, w, engine, v)
        return ins

    def dma(self, queue, out, in_, r=(), w=(), is_output=False, **kw):
        n = self.dcount[queue]
        nq = NDMA_Q[queue]
        slot = n % nq
        k = ("d", queue, slot)
        deps = self._collect(queue, r, w)
        prev = 16 * (n // nq)
        self._emit_waits(queue, deps, True, extra={k: prev} if prev > 0 else None)
        ins = self.eng[queue].dma_start(out=out, in_=in_, **kw)
        v = prev + 16
        ins.then_inc(self.sem[k], 16)
        self.dcount[queue] += 1
        self._record(r, w, k, v)
        if is_output:
            self.out_events[k] = max(self.out_events.get(k, 0), v)
        return ins

    def idma(self, out, in_, in_offset, r=(), w=()):
        queue = "pool"
        n = self.dcount[queue]
        nq = NDMA_Q[queue]
        slot = n % nq
        k = ("d", queue, slot)
        deps = self._collect(queue, r, w)
        prev = 16 * (n // nq)
        self._emit_waits(queue, deps, True, extra={k: prev} if prev > 0 else None)
        ins = self.nc.gpsimd.indirect_dma_start(out=out, out_offset=None, in_=in_, in_offset=in_offset)
        v = prev + 16
        ins.then_inc(self.sem[k], 16)
        self.dcount[queue] += 1
        self._record(r, w, k, v)
        return ins

    def finish(self):
        e = self.eng["sp"]
        for q in ("sp", "pool", "act"):
            n = self.dcount[q]
            nq = NDMA_Q[q]
            for slot in range(nq):
                cnt = (n - slot + nq - 1) // nq if n > slot else 0
                if cnt > 0:
                    e.wait_ge(self.sem[("d", q, slot)], 16 * cnt)
        for k in ("pe", "act", "dve", "pool"):
            if self.count[k] > 0:
                e.wait_ge(self.sem[k], self.count[k])
        self.stack.close()


from concourse.bass_utils import run_bass_kernel_spmd
import ml_dtypes

D = 1024
KD = 8
NIN = 5632
FF = 2816
KF = 22
EPS = 1e-6
MB = 8192.0
NEG = -1.0e30


def wchunks():
    L = []
    for c in range(7):
        L.append(("in%d" % c, "w_in", 0, 8, c * 512))
    for half in range(2):
        L.append(("in%d" % (7 + half), "w_in", 0, 8, (7 + half) * 512))
        L.append(("in%d" % (9 + half), "w_in", 0, 8, (9 + half) * 512))
        L.append(("wa%d" % half, "w_a_out", 0, 4, half * 512))
        L.append(("wb%d" % half, "w_b_out", 0, 4, half * 512))
    for half in range(2):
        L.append(("wo%d" % half, "w_o", 0, 8, half * 512))
    for c in range(11):
        L.append(("up%d" % c, "w_up", 0, 8, c * 512))
    for half in range(2):
        for kg, (k0, nk) in enumerate(((0, 8), (8, 8), (16, 6))):
            L.append(("dn%d_%d" % (half, kg), "w_down", k0, nk, half * 512))
    return L


class StopBuild(Exception):
    pass


def build(nc, cfg):
    stage = cfg.get("stage", 99)

    def ckpt(n):
        if stage <= n:
            raise StopBuild()
    T = cfg["T"]; NS = cfg["NS"]; NPG = cfg["NPG"]; NPHYS = cfg["NPHYS"]
    NSEQ = 1 + NS
    NST = T // 512
    NB_S = NPG // 2
    P = Prog(nc)

    def din(name, shape, dt=F32):
        return nc.dram_tensor(name, list(shape), dt, kind="ExternalInput").ap()

    def dout(name, shape, dt=F32):
        return nc.dram_tensor(name, list(shape), dt, kind="ExternalOutput").ap()

    xp = din("xp", [T, D]); xs = din("xs", [NS * 128, D])
    cT = din("cT", [128, KD, NSEQ])
    cache_k = din("cache_k", [NPHYS * 128, 512]); cache_v = din("cache_v", [NPHYS * 128, 512])
    ptab = din("ptab", [NS, NPG], I32)
    shg = din("shg", [NS, 4, 128, 128])
    convT = din("convT", [NS, 128, KF, 2])
    W = {"w_ada": din("w_ada", [D, 6 * D]), "w_in": din("w_in", [D, NIN]), "w_a_out": din("w_a_out", [512, D]),
         "w_b_out": din("w_b_out", [512, D]), "w_o": din("w_o", [D, D]), "w_up": din("w_up", [D, 2 * FF]),
         "w_down": din("w_down", [FF, D])}
    n1gT = din("n1gT", [128, KD]); n2gT = din("n2gT", [128, KD]); badaT = din("badaT", [128, 48])
    lbT = din("lbT", [128, 4, 2]); convwT = din("convwT", [128, KF, 3]); convbT = din("convbT", [128, KF])
    hgn_bc = din("hgn_bc", [128, 512]); fg_bc = din("fg_bc", [128, D])
    c_identb = din("c_identb", [128, 128], BF16); c_identf = din("c_identf", [128, 128])
    c_tri = din("c_tri", [128, 128]); c_causal = din("c_causal", [128, 128], BF16)
    c_esel = din("c_esel", [128, 17 * 128], BF16)
    c_cm4 = din("c_cm4", [4, 32]); c_eye4 = din("c_eye4", [4, 4], BF16)
    c_hmask = din("c_hmask", [32, 512]); c_selq = din("c_selq", [128, 4], BF16)

    yp = dout("yp", [T, D]); ys = dout("ys", [NS * 128, D])
    kp = dout("kp", [T, 512]); vp = dout("vp", [T, 512])
    ksn = dout("ksn", [NS * 128, 512]); vsn = dout("vsn", [NS * 128, 512])
    hgp = dout("hgp", [4, 128, 128]); hgs = dout("hgs", [NS, 4, 128, 128])
    cvp = dout("cvp", [128, KF, 2]); cvs = dout("cvs", [NS, 128, KF, 2])

    WCH = wchunks()
    NCH = len(WCH)
    wscr = [P.dram("wscr%d" % i, [128, 8 * 512], BF16) for i in range(NCH)]

    NSLOT = 4
    wslot = [P.sb("wslot%d" % i, [128, 8, 512], BF16) for i in range(NSLOT)]
    identb = P.sb("identb", [128, 128], BF16); identf = P.sb("identf", [128, 128], F32)
    tri = P.sb("tri", [128, 128], F32); causal = P.sb("causal", [128, 128], BF16)
    esel = P.sb("esel", [128, 17 * 128], BF16)
    cm4 = P.sb("cm4", [4, 32], F32); eye4 = P.sb("eye4", [4, 4], BF16)
    hmask = P.sb("hmask", [32, 512], F32); selq = P.sb("selq", [128, 4], BF16)
    onesf = P.sb("onesf", [128, 128], F32); onesb = P.sb("onesb", [128, 128], BF16)
    hgn = P.sb("hgn", [128, 512], F32); fg = P.sb("fg", [128, D], F32)
    modT = P.sb("modT", [128, 48, NSEQ], F32)
    s1 = P.sb("s1", [128, KD, NSEQ], F32); s2 = P.sb("s2", [128, KD, NSEQ], F32)
    lb = P.sb("lb", [128, 4], F32); oml = P.sb("oml", [128, 4], F32); noml = P.sb("noml", [128, 4], F32)
    cw = P.sb("cw", [128, KF, 3], F32); cb = P.sb("cb", [128, KF], F32)
    G1 = P.sb("G1", [128, D], F32); G2 = P.sb("G2", [128, D], F32)
    gstate = {"seq": None}
    pf = [P.ps("pf%d" % i, [128, 512], F32) for i in range(6)]
    pbf = [P.ps("pbf%d" % i, [128, 1024], BF16) for i in range(2)]
    rr = {"pf": 0, "pbf": 0}

    def nxt_pf(group=(0, 1, 2, 3)):
        i = group[rr["pf"] % len(group)]
        rr["pf"] += 1
        return pf[i]

    def nxt_pbf():
        i = rr["pbf"] % 2
        rr["pbf"] += 1
        return pbf[i]

    def mm(out, lhsT, rhs, start, stop, r, w):
        return P.op("pe", lambda e: e.matmul(out, lhsT, rhs, start=start, stop=stop, skip_group_check=True), r=r, w=w)

    def tr(out, in_, ident, r, w):
        return P.op("pe", lambda e: e.transpose(out, in_, ident), r=r, w=w)

    for sbt, dr in ((identb, c_identb), (identf, c_identf), (tri, c_tri), (causal, c_causal), (esel, c_esel),
                    (cm4, c_cm4), (eye4, c_eye4), (hmask, c_hmask), (selq, c_selq), (hgn, hgn_bc), (fg, fg_bc),
                    (cw, convwT), (cb, convbT)):
        P.dma("sp", sbt[:], dr, w=[sbt])
    P.op("dve", lambda e: e.memset(onesf[:], 1.0), w=[onesf])
    P.op("dve", lambda e: e.memset(onesb[:], 1.0), w=[onesb])

    def setup_phase():
      with P.scope():
          cTs = P.sb("cTs", [128, KD, NSEQ], F32); scb = P.sb("scb", [128, KD, 8], BF16)
          n1g = P.sb("n1g", [128, KD], F32); n2g = P.sb("n2g", [128, KD], F32); bada = P.sb("bada", [128, 48], F32)
          lbl = P.sb("lbl", [128, 4, 2], F32); lbd = P.sb("lbd", [128, 4], F32)
          P.dma("sp", cTs[:], cT, w=[cTs]); P.dma("sp", n1g[:], n1gT, w=[n1g]); P.dma("sp", n2g[:], n2gT, w=[n2g])
          P.dma("sp", bada[:], badaT, w=[bada]); P.dma("sp", lbl[:], lbT, w=[lbl])
          P.op("act", lambda e: e.activation(out=scb[:, :, 0:NSEQ], in_=cTs[:], func=AF.Silu), r=[cTs], w=[scb])
          P.op("dve", lambda e: e.tensor_tensor(out=lbd[:], in0=lbl[:, :, 0], in1=lbl[:, :, 1], op=ALU.subtract), r=[lbl], w=[lbd])
          P.op("act", lambda e: e.activation(out=lb[:], in_=lbd[:], func=AF.Sigmoid), r=[lbd], w=[lb])
          P.op("dve", lambda e: e.tensor_scalar(out=oml[:], in0=lb[:], scalar1=-1.0, scalar2=1.0, op0=ALU.mult, op1=ALU.add), r=[lb], w=[oml])
          P.op("dve", lambda e: e.tensor_scalar(out=noml[:], in0=oml[:], scalar1=-1.0, scalar2=None, op0=ALU.mult), r=[oml], w=[noml])
          if stage <= -3:
              raise StopBuild()
          mps = pf[0]
          for c12 in range(12):
              sl = wslot[c12 % NSLOT]
              P.dma("pool", sl[:], W["w_ada"][:, c12 * 512:(c12 + 1) * 512].rearrange("(k p) j -> p k j", p=128), w=[sl])
              for m in range(4):
                  ch = c12 * 4 + m
                  for k in range(KD):
                      mm(mps[:, ch * NSEQ:(ch + 1) * NSEQ], sl[:, k, m * 128:(m + 1) * 128], scb[:, k, 0:NSEQ],
                         start=(k == 0), stop=(k == KD - 1), r=[sl, scb], w=[mps])
          P.op("dve", lambda e: e.tensor_tensor(out=modT[:], in0=mps[:, 0:48 * NSEQ].rearrange("p (c s) -> p c s", s=NSEQ),
                                                in1=bada[:].unsqueeze(2).to_broadcast([128, 48, NSEQ]), op=ALU.add), r=[mps, bada], w=[modT])
          for (sd, ng, off) in ((s1, n1g, 8), (s2, n2g, 32)):
              P.op("dve", lambda e: e.tensor_scalar(out=sd[:], in0=modT[:, off:off + 8, :], scalar1=1.0, scalar2=None, op0=ALU.add), r=[modT], w=[sd])
              P.op("dve", lambda e: e.tensor_tensor(out=sd[:], in0=sd[:], in1=ng[:].unsqueeze(2).to_broadcast([128, KD, NSEQ]), op=ALU.mult), r=[sd, ng], w=[sd])
          if stage <= -2:
              raise StopBuild()
          for i, (nm, wn, k0, nk, c0) in enumerate(WCH[:(cfg.get('ncast', 999) if cfg.get('wmode', 'scratch') == 'scratch' else 0)]):
              sl = wslot[i % NSLOT]
              src = W[wn][k0 * 128:(k0 + nk) * 128, c0:c0 + 512].rearrange("(k p) j -> p k j", p=128)
              P.dma("pool", sl[:, 0:nk, :], src, w=[sl])
              P.dma("sp", wscr[i][:, 0:nk * 512], sl[:, 0:nk, :].rearrange("p k j -> p (k j)"), r=[sl], w=[wscr[i]])


    try:
        setup_phase()
    except StopBuild:
        P.finish()
        return P
    if stage <= -1:
        P.finish()
        return P

    ckpt_setup = True
    ws = {"pos": 0, "issued": 0}

    def w_issue_upto(n):
        while ws["issued"] < n:
            i = ws["issued"]
            ci = i % NCH
            nk = WCH[ci][3]
            sl = wslot[i % NSLOT]
            if cfg.get("wmode", "scratch") == "scratch":
                P.dma("sp", sl[:, 0:nk, :].rearrange("p k j -> p (k j)"), wscr[ci][:, 0:nk * 512], r=[wscr[ci]], w=[sl])
            else:
                (nm_, wn_, k0_, nk_, c0_) = WCH[ci]
                src_ = W[wn_][k0_ * 128:(k0_ + nk_) * 128, c0_:c0_ + 512].rearrange("(k p) j -> p k j", p=128)
                P.dma("pool", sl[:, 0:nk_, :], src_, w=[sl])
            ws["issued"] += 1

    def wget(name):
        i = ws["pos"]
        ci = i % NCH
        assert WCH[ci][0] == name, (WCH[ci][0], name)
        w_issue_upto(i + 3)
        ws["pos"] += 1
        return wslot[i % NSLOT]

    def build_G(seq):
        if gstate["seq"] == seq:
            return
        gstate["seq"] = seq
        for (G, off) in ((G1, 16), (G2, 40)):
            for k in range(KD):
                dg = P.tmp("dg", [128, 128], F32)
                P.op("dve", lambda e: e.tensor_scalar(out=dg[:], in0=identf[:], scalar1=modT[:, off + k, seq:seq + 1], scalar2=None, op0=ALU.mult), r=[identf, modT], w=[dg])
                pb = nxt_pf(group=(4, 5))
                mm(pb[:, 0:128], onesf[:, 0:128], dg[:], True, True, r=[onesf, dg], w=[pb])
                P.op("act", lambda e: e.copy(out=G[:, k * 128:(k + 1) * 128], in_=pb[:, 0:128]), r=[pb], w=[G])

    def rstd_from_ssq(ssq, n, cols, scale):
        P.op("act", lambda e: e.activation(out=ssq[:n, cols], in_=ssq[:n, cols], func=AF.Sqrt, scale=scale, bias=epsb[:n, 0:1]), r=[ssq, epsb], w=[ssq])
        P.op("dve", lambda e: e.reciprocal(out=ssq[:n, cols], in_=ssq[:n, cols]), r=[ssq], w=[ssq])

    epsb = P.sb("epsb", [128, 1], F32)
    P.op("dve", lambda e: e.memset(epsb[:], EPS), w=[epsb])
    nmb = P.sb("nmb", [128, 1], F32)
    P.op("dve", lambda e: e.memset(nmb[:], -MB / 8.0), w=[nmb])

    nrm_cnt = {"n": 0}

    def norm_to_hT(src_ap, src_bufs, n, seq, sc, shoff, hT, col0):
        nrm_cnt["n"] += 1
        xn = P.tmp("xn", [128, D], BF16)
        ssq = P.tmp("ssq", [128, 1], F32)
        P.op("act", lambda e: e.activation(out=xn[:n, :], in_=src_ap, func=AF.Square, accum_out=ssq[:n, 0:1]), r=src_bufs, w=[xn, ssq])
        rstd_from_ssq(ssq, n, slice(0, 1), 1.0 / D)
        P.op("dve", lambda e: e.tensor_scalar(out=xn[:n, :], in0=src_ap, scalar1=ssq[:n, 0:1], scalar2=None, op0=ALU.mult), r=src_bufs + [ssq], w=[xn])
        pb = nxt_pbf()
        for k in range(KD):
            tr(pb[:, k * 128:k * 128 + n], xn[:n, k * 128:(k + 1) * 128], identb[:n, :n], r=[xn, identb], w=[pb])
        for k in range(KD):
            if nrm_cnt["n"] % 2 == 0:
                P.op("dve", lambda e: e.tensor_scalar(out=hT[:, k, col0:col0 + n], in0=pb[:, k * 128:k * 128 + n], scalar1=sc[:, k, seq:seq + 1],
                                                      scalar2=modT[:, shoff + k, seq:seq + 1], op0=ALU.mult, op1=ALU.add), r=[pb, sc, modT], w=[hT])
            else:
                P.op("act", lambda e: e.activation(out=hT[:, k, col0:col0 + n], in_=pb[:, k * 128:k * 128 + n], func=AF.Identity,
                                                   scale=sc[:, k, seq:seq + 1], bias=modT[:, shoff + k, seq:seq + 1]), r=[pb, sc, modT], w=[hT])

    NSEG = max(1, NS)
    HS = P.sb("HS", [128, NSEG, KF, 2], F32)
    Sst = [P.sb("S%d" % h, [128, 128], F32) for h in range(4)]
    Sbf = [P.sb("Sbf%d" % h, [128, 128], BF16) for h in range(4)]

    def supertile(tiles, is_prompt, st_idx, attn_ctx):
        coff = []
        c = 0
        for t in tiles:
            coff.append(c); c += t[1]
        N = c
        NT = len(tiles)
        nseg = 1 if is_prompt else NT
        segn = N // nseg
        nr = 128 if is_prompt else 4
        nrs = segn if is_prompt else 4
        with P.scope():
            hT = P.sb("hT", [128, KD, N], BF16)
            x2 = P.sb("x2", [128, NT, D], F32)
            with P.scope():
                oaT = P.sb("oaT", [128, 4, N], BF16)
                obT = P.sb("obT", [128, 4, N], BF16)
                ckpt(0)
                with P.scope():
                    for i, (seq, n, xd, outs) in enumerate(tiles):
                        xt = P.tmp("xt", [128, D], F32)
                        P.dma("sp", xt[:n, :], xd, w=[xt])
                        norm_to_hT(xt[:n, :], [xt], n, seq, s1, 0, hT, coff[i])

                ckpt(1)

                def amm(sl, i, pb):
                    n = tiles[i][1]
                    for k in range(KD):
                        mm(pb[:n, :], hT[:, k, coff[i]:coff[i] + n], sl[:, k, :], k == 0, k == KD - 1, r=[sl, hT], w=[pb])

                with P.scope():
                    qT = P.sb("qT", [128, 8, N], BF16)
                    P.op("dve", lambda e: e.memset(qT[:], 0.0), w=[qT])
                    akT = P.sb("akT", [128, 4, N], BF16) if not is_prompt else None
                    vtok = P.sb("vtok", [128, NT, 512], BF16) if not is_prompt else None
                    sl = wget("in0")
                    for m in range(4):
                        pb = nxt_pf(); bmmh(sl, m, pb, hT, N)
                        P.op("act", lambda e: e.copy(out=qT[0:64, 2 * m, :], in_=pb[0:64, 0:N]), r=[pb], w=[qT])
                        P.op("act", lambda e: e.copy(out=qT[64:128, 2 * m + 1, :], in_=pb[64:128, 0:N]), r=[pb], w=[qT])
                    sl = wget("in1")
                    KTs = attn_ctx["KT"][st_idx] if is_prompt else akT
                    for m in range(4):
                        pb = nxt_pf(); bmmh(sl, m, pb, hT, N)
                        P.op("dve", lambda e: e.tensor_copy(out=KTs[:, m, :], in_=pb[:, 0:N]), r=[pb], w=[KTs])
                    for i, (seq, n, xd, outs) in enumerate(tiles):
                        pb = nxt_pf(); amm(sl, i, pb)
                        kst = P.tmp("kvst", [128, 512], F32)
                        P.op("act", lambda e: e.copy(out=kst[:n, :], in_=pb[:n, :]), r=[pb], w=[kst])
                        if not cfg.get("dbg_nokv"):
                            P.dma("pool", outs["k"], kst[:n, :], r=[kst], is_output=True)
                    if is_prompt:
                        kms = P.tmp("kms", [128, 4, 2], F32)
                        P.op("dve", lambda e: e.tensor_reduce(out=kms[:], in_=KTs[:].rearrange("p c (b t) -> p c b t", b=2), axis=AX.X, op=ALU.add), r=[KTs], w=[kms])
                        kmT = attn_ctx["kmT"]
                        P.op("act", lambda e: e.activation(out=kmT[:, :, 2 * st_idx:2 * st_idx + 2], in_=kms[:], func=AF.Copy, scale=1.0 / 256.0), r=[kms], w=[kmT])
                    sl = wget("in2")
                    for i, (seq, n, xd, outs) in enumerate(tiles):
                        pb = nxt_pf(); amm(sl, i, pb)
                        vst = P.tmp("kvst", [128, 512], F32)
                        P.op("act", lambda e: e.copy(out=vst[:n, :], in_=pb[:n, :]), r=[pb], w=[vst])
                        if not cfg.get("dbg_nokv"):
                            P.dma("pool", outs["v"], vst[:n, :], r=[vst], is_output=True)
                        if is_prompt:
                            Vs = attn_ctx["V"][st_idx]
                            P.op("dve", lambda e: e.tensor_copy(out=Vs[:, i, :, 0:64], in_=pb[:, :].rearrange("p (h d) -> p h d", h=8)), r=[pb], w=[Vs])
                        elif not cfg.get("dbg_novtok"):
                            P.op("dve", lambda e: e.tensor_copy(out=vtok[:n, i, :], in_=pb[:n, :]), r=[pb], w=[vtok])
                    ckpt(2)
                    if is_prompt:
                        attn_prompt(st_idx, attn_ctx, qT, oaT)
                    else:
                        for j, (seq, n, xd, outs) in enumerate(tiles):
                            attn_sample(j, qT, akT, vtok, oaT, coff[j])
                ckpt(3)
                with P.scope():
                    hi_t = P.sb("hi_t", [128, NT, 512], BF16)
                    hg_t = P.sb("hg_t", [128, NT, 512], BF16)
                    sl3 = wget("in3")
                    qts = [P.sb("qt%d" % h, [128, N], F32) for h in range(4)]
                    for h in range(4):
                        pb = nxt_pf(); bmmh(sl3, h, pb, hT, N)
                        P.op("act", lambda e: e.activation(out=qts[h][:], in_=pb[:, 0:N], func=AF.Silu), r=[pb], w=[qts[h]])
                    sl4 = wget("in4")
                    Qh = [P.sb("Qh%d" % h, [128, N], BF16) for h in range(4)]
                    Qm = [P.sb("Qm%d" % h, [128, N], BF16) for h in range(4)]
                    Km = [P.sb("Km%d" % h, [128, N], BF16) for h in range(4)]
                    Kl = [P.sb("Kl%d" % h, [128, N], BF16) for h in range(4)]
                    elast = [P.sb("el%d" % h, [128, NT], F32) for h in range(4)]
                    for h in range(4):
                        pb = nxt_pf(); bmmh(sl4, h, pb, hT, N)
                        sig = P.tmp("sig", [128, N], F32, bufs=1); kk = P.tmp("kk", [128, N], F32, bufs=1)
                        bb = P.tmp("bb", [128, N], F32, bufs=1); eb = P.tmp("eb", [128, N], F32, bufs=1)
                        P.op("act", lambda e: e.activation(out=sig[:], in_=pb[:, 0:N], func=AF.Sigmoid), r=[pb], w=[sig])
                        P.op("dve", lambda e: e.tensor_scalar(out=kk[:], in0=sig[:], scalar1=noml[:, h:h + 1], scalar2=oml[:, h:h + 1], op0=ALU.mult, op1=ALU.add), r=[sig, noml, oml], w=[kk])
                        P.op("act", lambda e: e.activation(out=sig[:], in_=sig[:], func=AF.Ln, scale=oml[:, h:h + 1], bias=lb[:, h:h + 1]), r=[sig, oml, lb], w=[sig])
                        if nr < 128:
                            P.op("dve", lambda e: e.memset(bb[:], 0.0), w=[bb])
                        for i, (seq, n, xd, outs) in enumerate(tiles):
                            cs = slice(coff[i], coff[i] + nr)
                            P.op("dve", lambda e: e.tensor_tensor_scan(out=bb[:, cs], data0=onesf[:, 0:nr], data1=sig[:, cs], initial=0.0, op0=ALU.mult, op1=ALU.add), r=[onesf, sig], w=[bb])
                        P.op("act", lambda e: e.activation(out=eb[:], in_=bb[:], func=AF.Exp), r=[bb], w=[eb])
                        P.op("dve", lambda e: e.tensor_tensor(out=qts[h][:], in0=qts[h][:], in1=eb[:], op=ALU.mult), r=[qts[h], eb], w=[qts[h]])
                        P.op("act", lambda e: e.copy(out=Qh[h][:], in_=qts[h][:]), r=[qts[h]], w=[Qh[h]])
                        P.op("dve", lambda e: e.reciprocal(out=bb[:], in_=eb[:]), r=[eb], w=[bb])
                        P.op("dve", lambda e: e.tensor_tensor(out=kk[:], in0=kk[:], in1=bb[:], op=ALU.mult), r=[kk, bb], w=[kk])
                        for i, (seq, n, xd, outs) in enumerate(tiles):
                            cs = slice(coff[i], coff[i] + n)
                            mid = coff[i] + nr // 2; last = coff[i] + nr - 1
                            P.op("pool", lambda e: e.tensor_copy(out=elast[h][:, i:i + 1], in_=eb[:, last:last + 1]), r=[eb], w=[elast[h]])
                            P.op("dve", lambda e: e.tensor_scalar(out=Qm[h][:, cs], in0=qts[h][:, cs], scalar1=bb[:, mid:mid + 1], scalar2=None, op0=ALU.mult), r=[qts[h], bb], w=[Qm[h]])
                            P.op("dve", lambda e: e.tensor_scalar(out=Km[h][:, cs], in0=kk[:, cs], scalar1=eb[:, mid:mid + 1], scalar2=None, op0=ALU.mult), r=[kk, eb], w=[Km[h]])
                            P.op("dve", lambda e: e.tensor_scalar(out=Kl[h][:, cs], in0=kk[:, cs], scalar1=eb[:, last:last + 1], scalar2=None, op0=ALU.mult), r=[kk, eb], w=[Kl[h]])
                            if nr < 128:
                                P.op("dve", lambda e: e.memset(Kl[h][:, coff[i] + nr:coff[i] + n], 0.0), w=[Kl[h]])
                    sl5 = wget("in5")
                    for i, (seq, n, xd, outs) in enumerate(tiles):
                        pb = nxt_pf(); amm(sl5, i, pb)
                        P.op("act", lambda e: e.copy(out=hi_t[:n, i, :], in_=pb[:n, :]), r=[pb], w=[hi_t])
                    sl6 = wget("in6")
                    for i, (seq, n, xd, outs) in enumerate(tiles):
                        pb = nxt_pf(); amm(sl6, i, pb)
                        P.op("act", lambda e: e.activation(out=hg_t[:n, i, :], in_=pb[:n, :], func=AF.Silu), r=[pb], w=[hg_t])
                    for i, (seq, n, xd, outs) in enumerate(tiles):
                        cs = slice(coff[i], coff[i] + n)
                        if not is_prompt:
                            for h in range(4):
                                P.dma("sp", Sst[h][:], shg[seq - 1, h], w=[Sst[h]])
                                P.op("act", lambda e: e.copy(out=Sbf[h][:], in_=Sst[h][:]), r=[Sst[h]], w=[Sbf[h]])
                        ob = P.tmp("ob", [128, 512], F32)
                        for h in range(4):
                            hs = slice(h * 128, (h + 1) * 128)
                            pa = nxt_pf()
                            mm(pa[:n, 0:n], Km[h][:, cs], Qm[h][:, cs], True, True, r=[Km[h], Qm[h]], w=[pa])
                            AT = P.tmp("AT", [128, 128], BF16)
                            P.op("dve", lambda e: e.tensor_tensor(out=AT[:n, :n], in0=pa[:n, 0:n], in1=tri[:n, :n], op=ALU.mult), r=[pa, tri], w=[AT])
                            pt_ = nxt_pbf()
                            tr(pt_[:n, 0:128], Kl[h][:, cs], identb[:, :], r=[Kl[h], identb], w=[pt_])
                            Kls = P.tmp("Kls", [128, 128], BF16)
                            P.op("act", lambda e: e.copy(out=Kls[:n, :], in_=pt_[:n, 0:128]), r=[pt_], w=[Kls])
                            po = nxt_pf()
                            mm(po[:n, 0:128], AT[:n, :n], hi_t[:n, i, hs], True, False, r=[AT, hi_t], w=[po])
                            mm(po[:n, 0:128], Qh[h][:, cs], Sbf[h][:], False, True, r=[Qh[h], Sbf[h]], w=[po])
                            P.op("act", lambda e: e.copy(out=ob[:n, hs], in_=po[:n, 0:128]), r=[po], w=[ob])
                            pS = nxt_pf()
                            mm(pS[:, 0:128], Kls[:n, :], hi_t[:n, i, hs], True, True, r=[Kls, hi_t], w=[pS])
                            P.op("dve", lambda e: e.scalar_tensor_tensor(out=Sst[h][:], in0=Sst[h][:], scalar=elast[h][:, i:i + 1], in1=pS[:, 0:128], op0=ALU.mult, op1=ALU.add), r=[Sst[h], elast[h], pS], w=[Sst[h]])
                            P.op("act", lambda e: e.copy(out=Sbf[h][:], in_=Sst[h][:]), r=[Sst[h]], w=[Sbf[h]])
                        if not is_prompt:
                            for h in range(4):
                                P.dma("pool", hgs[seq - 1, h], Sst[h][:], r=[Sst[h]], is_output=True)
                        sq = P.tmp("sq", [128, 512], F32, bufs=1); ss4 = P.tmp("ss4", [128, 4], F32)
                        P.op("pool", lambda e: e.tensor_tensor(out=sq[:n, :], in0=ob[:n, :], in1=ob[:n, :], op=ALU.mult), r=[ob], w=[sq])
                        P.op("dve", lambda e: e.tensor_reduce(out=ss4[:n, :], in_=sq[:n, :].rearrange("p (h d) -> p h d", h=4), axis=AX.X, op=ALU.add), r=[sq], w=[ss4])
                        rstd_from_ssq(ss4, n, slice(0, 4), 1.0 / 128.0)
                        P.op("dve", lambda e: e.tensor_tensor(out=sq[:n, :].rearrange("p (h d) -> p h d", h=4), in0=ob[:n, :].rearrange("p (h d) -> p h d", h=4),
                                                              in1=ss4[:n, :].unsqueeze(2).to_broadcast([n, 4, 128]), op=ALU.mult), r=[ob, ss4], w=[sq])
                        P.op("pool", lambda e: e.tensor_tensor(out=sq[:n, :], in0=sq[:n, :], in1=hgn[:n, :], op=ALU.mult), r=[sq, hgn], w=[sq])
                        obn = P.tmp("obn", [128, 512], BF16)
                        P.op("dve", lambda e: e.tensor_tensor(out=obn[:n, :], in0=sq[:n, :], in1=hg_t[:n, i, :], op=ALU.mult), r=[sq, hg_t], w=[obn])
                        pt_ = nxt_pbf()
                        for c4 in range(4):
                            tr(pt_[:, c4 * 128:c4 * 128 + n], obn[:n, c4 * 128:(c4 + 1) * 128], identb[:n, :n], r=[obn, identb], w=[pt_])
                        P.op("act", lambda e: e.copy(out=obT[:, :, cs], in_=pt_[:, 0:512].rearrange("p (c t) -> p c t", c=4)[:, :, 0:n]), r=[pt_], w=[obT])
                ckpt(4)
                with P.scope():
                    mixT = P.sb("mixT", [128, KD, N], BF16)
                    for half in range(2):
                        sga = P.tmp("sga", [128, 4, N], BF16, bufs=1); sgb = P.tmp("sgb", [128, 4, N], BF16, bufs=1)
                        sl = wget("in%d" % (7 + half))
                        for m in range(4):
                            pb = nxt_pf(); bmmh(sl, m, pb, hT, N)
                            P.op("act", lambda e: e.activation(out=sga[:, m, :], in_=pb[:, 0:N], func=AF.Sigmoid), r=[pb], w=[sga])
                        sl = wget("in%d" % (9 + half))
                        for m in range(4):
                            pb = nxt_pf(); bmmh(sl, m, pb, hT, N)
                            P.op("act", lambda e: e.activation(out=sgb[:, m, :], in_=pb[:, 0:N], func=AF.Sigmoid), r=[pb], w=[sgb])
                        sl = wget("wa%d" % half)
                        for m in range(4):
                            pb = nxt_pf()
                            for k in range(4):
                                mm(pb[:, 0:N], sl[:, k, m * 128:(m + 1) * 128], oaT[:, k, :], k == 0, k == 3, r=[sl, oaT], w=[pb])
                            P.op("dve", lambda e: e.tensor_tensor(out=sga[:, m, :], in0=sga[:, m, :], in1=pb[:, 0:N], op=ALU.mult), r=[sga, pb], w=[sga])
                        sl = wget("wb%d" % half)
                        for m in range(4):
                            pb = nxt_pf()
                            for k in range(4):
                                mm(pb[:, 0:N], sl[:, k, m * 128:(m + 1) * 128], obT[:, k, :], k == 0, k == 3, r=[sl, obT], w=[pb])
                            P.op("dve", lambda e: e.tensor_tensor(out=sgb[:, m, :], in0=sgb[:, m, :], in1=pb[:, 0:N], op=ALU.mult), r=[sgb, pb], w=[sgb])
                            P.op("pool", lambda e: e.tensor_tensor(out=mixT[:, half * 4 + m, :], in0=sga[:, m, :], in1=sgb[:, m, :], op=ALU.add), r=[sga, sgb], w=[mixT])
                    for half in range(2):
                        sl = wget("wo%d" % half)
                        for i, (seq, n, xd, outs) in enumerate(tiles):
                            build_G(seq)
                            pb = nxt_pf()
                            for k in range(KD):
                                mm(pb[:n, :], mixT[:, k, coff[i]:coff[i] + n], sl[:, k, :], k == 0, k == KD - 1, r=[sl, mixT], w=[pb])
                            xt6 = P.tmp("xt6", [128, 512], F32)
                            P.dma("sp", xt6[:n, :], xd[:, half * 512:(half + 1) * 512], w=[xt6])
                            tmp = P.tmp("tmp", [128, 512], F32)
                            P.op("dve", lambda e: e.tensor_tensor(out=tmp[:n, :], in0=pb[:n, :], in1=G1[:n, half * 512:(half + 1) * 512], op=ALU.mult), r=[pb, G1], w=[tmp])
                            P.op("pool", lambda e: e.tensor_tensor(out=x2[:n, i, half * 512:(half + 1) * 512], in0=tmp[:n, :], in1=xt6[:n, :], op=ALU.add), r=[tmp, xt6], w=[x2])
            ckpt(5)
            with P.scope():
                for i, (seq, n, xd, outs) in enumerate(tiles):
                    norm_to_hT(x2[:n, i, :], [x2], n, seq, s2, 24, hT, coff[i])
            with P.scope():
                actT = P.sb("actT", [128, KF, N], BF16)
                if not is_prompt:
                    for j, (seq, n, xd, outs) in enumerate(tiles):
                        P.dma("sp", HS[:, j], convT[seq - 1], w=[HS])
                cur = {"name": None, "sl": None}

                def upchunk(col):
                    nm = "up%d" % (col // 512)
                    if cur["name"] != nm:
                        cur["sl"] = wget(nm); cur["name"] = nm
                    return cur["sl"], (col % 512) // 128

                for f in range(KF):
                    sl, m = upchunk(f * 128)
                    pb = nxt_pf(); bmmh(sl, m, pb, hT, N)
                    ue = P.tmp("ue", [128, nseg, segn + 2], F32)
                    P.op("pool", lambda e: e.tensor_copy(out=ue[:, :, 0:2], in_=HS[:, 0:nseg, f, :]), r=[HS], w=[ue])
                    P.op("act", lambda e: e.copy(out=ue[:, :, 2:2 + segn], in_=pb[:, 0:N].rearrange("p (s t) -> p s t", s=nseg)), r=[pb], w=[ue])
                    P.op("pool", lambda e: e.tensor_copy(out=HS[:, 0:nseg, f, :], in_=ue[:, :, nrs:nrs + 2]), r=[ue], w=[HS])
                    t1 = P.tmp("t1", [128, nseg, segn], F32); t2 = P.tmp("t2", [128, nseg, segn], F32, bufs=1)
                    P.op("dve", lambda e: e.tensor_scalar(out=t1[:], in0=ue[:, :, 2:2 + segn], scalar1=cw[:, f, 2:3], scalar2=cb[:, f:f + 1], op0=ALU.mult, op1=ALU.add), r=[ue, cw, cb], w=[t1])
                    P.op("dve", lambda e: e.scalar_tensor_tensor(out=t2[:], in0=ue[:, :, 1:1 + segn], scalar=cw[:, f, 1:2], in1=t1[:], op0=ALU.mult, op1=ALU.add), r=[ue, cw, t1], w=[t2])
                    P.op("dve", lambda e: e.scalar_tensor_tensor(out=t1[:], in0=ue[:, :, 0:segn], scalar=cw[:, f, 0:1], in1=t2[:], op0=ALU.mult, op1=ALU.add), r=[ue, cw, t2], w=[t1])
                    P.op("act", lambda e: e.activation(out=actT[:, f, :], in_=t1[:].rearrange("p s t -> p (s t)"), func=AF.Gelu), r=[t1], w=[actT])
                for f in range(KF):
                    sl, m = upchunk(FF + f * 128)
                    pb = nxt_pf(); bmmh(sl, m, pb, hT, N)
                    P.op("dve", lambda e: e.tensor_tensor(out=actT[:, f, :], in0=actT[:, f, :], in1=pb[:, 0:N], op=ALU.mult), r=[actT, pb], w=[actT])
                if not is_prompt:
                    for j, (seq, n, xd, outs) in enumerate(tiles):
                        P.dma("pool", cvs[seq - 1], HS[:, j], r=[HS], is_output=True)
                ckpt(6)
                for half in range(2):
                    accs = [pf[i] for i in range(NT)]
                    for kg, (k0, nk) in enumerate(((0, 8), (8, 8), (16, 6))):
                        sl = wget("dn%d_%d" % (half, kg))
                        for i, (seq, n, xd, outs) in enumerate(tiles):
                            for k in range(nk):
                                mm(accs[i][:n, :], actT[:, k0 + k, coff[i]:coff[i] + n], sl[:, k, :], (kg == 0 and k == 0), (kg == 2 and k == nk - 1), r=[sl, actT], w=[accs[i]])
                    for i, (seq, n, xd, outs) in enumerate(tiles):
                        build_G(seq)
                        hsl = slice(half * 512, (half + 1) * 512)
                        tmp = P.tmp("tmp", [128, 512], F32)
                        P.op("dve", lambda e: e.tensor_tensor(out=tmp[:n, :], in0=accs[i][:n, :], in1=G2[:n, hsl], op=ALU.mult), r=[accs[i], G2], w=[tmp])
                        P.op("pool", lambda e: e.tensor_tensor(out=x2[:n, i, hsl], in0=tmp[:n, :], in1=x2[:n, i, hsl], op=ALU.add), r=[tmp, x2], w=[x2])
            with P.scope():
                for i, (seq, n, xd, outs) in enumerate(tiles):
                    junk = P.tmp("junk", [128, D], BF16, bufs=1); ssq = P.tmp("ssq", [128, 1], F32)
                    P.op("act", lambda e: e.activation(out=junk[:n, :], in_=x2[:n, i, :], func=AF.Square, accum_out=ssq[:n, 0:1]), r=[x2], w=[junk, ssq])
                    rstd_from_ssq(ssq, n, slice(0, 1), 1.0 / D)
                    for half in range(2):
                        hsl = slice(half * 512, (half + 1) * 512)
                        yo = P.tmp("yo", [128, 512], F32)
                        P.op("dve", lambda e: e.scalar_tensor_tensor(out=yo[:n, :], in0=x2[:n, i, hsl], scalar=ssq[:n, 0:1], in1=fg[:n, hsl], op0=ALU.mult, op1=ALU.mult), r=[x2, ssq, fg], w=[yo])
                        P.dma("pool", outs["y"][:, hsl], yo[:n, :], r=[yo], is_output=True)

    def bmmh(sl, m, pb, hT, N):
        for k in range(KD):
            mm(pb[:, 0:N], sl[:, k, m * 128:(m + 1) * 128], hT[:, k, :], k == 0, k == KD - 1, r=[sl, hT], w=[pb])

    def attn_prompt(s, ctx, qT, oaT):
        KT = ctx["KT"]; V = ctx["V"]; kmT = ctx["kmT"]
        with P.scope():
            maskT = P.sb("maskT", [128, 8, 512], BF16)
            P.op("dve", lambda e: e.memset(maskT[:], 0.0), w=[maskT])
            oatok = P.sb("oatok", [128, 4, 512], BF16)
            for qi in range(4):
                own = 2 * s + qi // 2
                cs = slice(qi * 128, (qi + 1) * 128)
                pg = nxt_pf()
                for h in range(8):
                    mm(pg[:, h * 16:(h + 1) * 16], qT[:, h, cs], kmT[:, h // 2, :], True, True, r=[qT, kmT], w=[pg])
                gsb = P.tmp("gsb", [128, 8, 16], F32)
                P.op("act", lambda e: e.copy(out=gsb[:], in_=pg[:, 0:128].rearrange("p (h b) -> p h b", h=8)), r=[pg], w=[gsb])
                if own < 16:
                    P.op("dve", lambda e: e.memset(gsb[:, :, own:16], NEG), w=[gsb])
                m8 = P.tmp("m8", [128, 8, 8], F32)
                if own >= 3:
                    for h in range(8):
                        P.op("dve", lambda e: e.max(out=m8[:, h, :], in_=gsb[:, h, :]), r=[gsb], w=[m8])
                else:
                    P.op("dve", lambda e: e.memset(m8[:], NEG * 0.1), w=[m8])
                mp = P.tmp("mp", [128, 8, 32], BF16)
                P.op("dve", lambda e: e.memset(mp[:], 0.0), w=[mp])
                P.op("dve", lambda e: e.memset(mp[:, :, 16:17], MB), w=[mp])
                selt = P.tmp("selt", [128, 8, 16], F32)
                P.op("dve", lambda e: e.tensor_tensor(out=selt[:], in0=gsb[:], in1=m8[:, :, 2:3].to_broadcast([128, 8, 16]), op=ALU.is_ge), r=[gsb, m8], w=[selt])
                P.op("dve", lambda e: e.tensor_scalar(out=mp[:, :, 0:16], in0=selt[:], scalar1=MB, scalar2=None, op0=ALU.mult), r=[selt], w=[mp])
                pt_ = nxt_pbf()
                for h in range(8):
                    tr(pt_[:32, h * 128:(h + 1) * 128], mp[:, h, :], identb[:, :], r=[mp, identb], w=[pt_])
                P.op("act", lambda e: e.copy(out=maskT[0:32, :, cs], in_=pt_[:32, :].rearrange("p (h q) -> p h q", h=8)), r=[pt_], w=[maskT])
            if cfg.get("astage", 9) <= 0:
                raise StopBuild()
            nkt = 4 * (s + 1)
            for h in range(8):
                rows = slice((h % 2) * 64, (h % 2) * 64 + 64)
                hp = h // 2
                oacc = pf[2 + h % 2]
                first_pv = True
                for kt in range(nkt):
                    blk = kt // 2
                    st_k = kt // 4
                    KTt = KT[st_k]; Vt = V[st_k]
                    kcs = slice((kt % 4) * 128, (kt % 4) * 128 + 128)
                    sc = pf[4 + kt % 2]
                    if blk == 2 * s + 1:
                        c0 = 256
                    else:
                        c0 = 0
                    ops = []
                    ops.append((sc[:, c0:512], KTt[:, hp, kcs], qT[:, h, c0:512], [KTt, qT]))
                    valid = []
                    if blk < 2 * s:
                        ops.append((sc[:, 0:512], esel[:, blk * 128:(blk + 1) * 128], maskT[:, h, 0:512], [esel, maskT]))
                        valid = [0, 1, 2, 3]
                    else:
                        if blk == 2 * s:
                            ops.append((sc[:, 256:512], esel[:, blk * 128:(blk + 1) * 128], maskT[:, h, 256:512], [esel, maskT]))
                            valid = [2, 3]
                            ki = kt - 4 * s; qbase = 0
                        else:
                            ki = kt - 4 * s - 2; qbase = 2
                        for ql in range(2):
                            qi = qbase + ql
                            qcs = slice(qi * 128, (qi + 1) * 128)
                            if ki < ql:
                                ops.append((sc[:, qcs], esel[:, 16 * 128:17 * 128], maskT[:, h, qcs], [esel, maskT]))
                                valid.append(qi)
                            elif ki == ql:
                                ops.append((sc[:, qcs], identb[:, :], causal[:, :], [identb, causal]))
                                valid.append(qi)
                    for oi, (o_, l_, r_, bufs) in enumerate(ops):
                        mm(o_, l_, r_, oi == 0, oi == len(ops) - 1, r=bufs, w=[sc])
                    if cfg.get("astage", 9) <= 1:
                        raise StopBuild()
                    PT = P.tmp("PT", [128, 512], BF16)
                    P.op("act", lambda e: e.activation(out=PT[:, c0:512], in_=sc[:, c0:512], func=AF.Exp, scale=0.125, bias=nmb[:, 0:1]), r=[sc, nmb], w=[PT])
                    if cfg.get("astage", 9) <= 2:
                        raise StopBuild()
                    for qi in sorted(valid):
                        mm(oacc[:, qi * 65:(qi + 1) * 65], PT[:, qi * 128:(qi + 1) * 128], Vt[:, kt % 4, h, 0:65], first_pv, False, r=[PT, Vt], w=[oacc])
                        first_pv = False
                if cfg.get("astage", 9) <= 3:
                    raise StopBuild()
                rs = P.tmp("rs", [128, 4, 1], F32)
                ov = oacc[:, 0:260].rearrange("p (q d) -> p q d", q=4)
                P.op("dve", lambda e: e.reciprocal(out=rs[:], in_=ov[:, :, 64:65]), r=[oacc], w=[rs])
                P.op("dve", lambda e: e.tensor_tensor(out=oatok[:, :, h * 64:(h + 1) * 64], in0=ov[:, :, 0:64], in1=rs[:].to_broadcast([128, 4, 64]), op=ALU.mult), r=[oacc, rs], w=[oatok])
            for qi in range(4):
                pt_ = nxt_pbf()
                for c4 in range(4):
                    tr(pt_[:, c4 * 128:(c4 + 1) * 128], oatok[:, qi, c4 * 128:(c4 + 1) * 128], identb[:, :], r=[oatok, identb], w=[pt_])
                P.op("act", lambda e: e.copy(out=oaT[:, :, qi * 128:(qi + 1) * 128], in_=pt_[:, 0:512].rearrange("p (c t) -> p c t", c=4)), r=[pt_], w=[oaT])

    def attn_sample(j, qT, akT, vtok, oaT, c0):
        NB = NB_S
        cs = slice(c0, c0 + 4)
        with P.scope():
            scst = P.sb("scst", [128, NPG, 32], F32)
            kps = P.sb("kps", [128, 4, NPG], F32)
            Qbd = P.sb("Qbd", [128, 4, 8], BF16)
            ptb = P.sb("ptb", [128, NPG], I32); ptf = P.sb("ptf", [128, NPG], F32); pidx = P.sb("pidx", [128, NPG], I32)
            iot = P.sb("iot", [128, 1], F32)
            P.dma("sp", ptb[:], ptab[j:j + 1, :].partition_broadcast(128), w=[ptb])
            P.op("pool", lambda e: e.iota(iot[:], pattern=[[0, 1]], base=0, channel_multiplier=1, allow_small_or_imprecise_dtypes=True), w=[iot])
            P.op("dve", lambda e: e.tensor_copy(out=ptf[:], in_=ptb[:]), r=[ptb], w=[ptf])
            P.op("dve", lambda e: e.tensor_scalar(out=ptf[:], in0=ptf[:], scalar1=128.0, scalar2=iot[:, 0:1], op0=ALU.mult, op1=ALU.add), r=[ptf, iot], w=[ptf])
            P.op("dve", lambda e: e.tensor_copy(out=pidx[:], in_=ptf[:]), r=[ptf], w=[pidx])
            qv = qT[:, :, cs].rearrange("p (a two) t -> p a two t", two=2)
            P.op("dve", lambda e: e.tensor_copy(out=Qbd[:, :, 0:4], in_=qv[:, :, 0, :]), r=[qT], w=[Qbd])
            P.op("dve", lambda e: e.tensor_copy(out=Qbd[:, :, 4:8], in_=qv[:, :, 1, :]), r=[qT], w=[Qbd])
            if cfg.get("sastage", 99) <= 0:
                raise StopBuild()
            scps = None
            for p in range(NPG):
                Kp = P.tmp("Kp", [128, 512], BF16, bufs=6)
                P.idma(out=Kp[:], in_=cache_k, in_offset=bass.IndirectOffsetOnAxis(ap=pidx[:, p:p + 1], axis=0), r=[pidx], w=[Kp])
                pt_ = nxt_pbf()
                for c4 in range(4):
                    tr(pt_[:, c4 * 128:(c4 + 1) * 128], Kp[:, c4 * 128:(c4 + 1) * 128], identb[:, :], r=[Kp, identb], w=[pt_])
                KTt = P.tmp("KTt", [128, 4, 128], BF16, bufs=3)
                P.op("act", lambda e: e.copy(out=KTt[:], in_=pt_[:, 0:512].rearrange("p (c t) -> p c t", c=4)), r=[pt_], w=[KTt])
                P.op("dve", lambda e: e.tensor_reduce(out=kps[:, :, p:p + 1], in_=KTt[:], axis=AX.X, op=ALU.add), r=[KTt], w=[kps])
                if p % 16 == 0:
                    scps = pf[4 + (p // 16) % 2]
                for hp in range(4):
                    o0 = (p % 16) * 32 + hp * 8
                    mm(scps[:, o0:o0 + 8], KTt[:, hp, :], Qbd[:, hp, :], True, True, r=[KTt, Qbd], w=[scps])
                if p % 16 == 15 or p == NPG - 1:
                    p0 = (p // 16) * 16
                    npp = p - p0 + 1
                    P.op("act", lambda e: e.copy(out=scst[:, p0:p0 + npp, :], in_=scps[:, 0:npp * 32].rearrange("p (a b) -> p a b", b=32)), r=[scps], w=[scst])
            if cfg.get("sastage", 99) <= 1:
                raise StopBuild()
            pn = nxt_pf()
            for hp in range(4):
                mm(pn[:4, hp * 8:(hp + 1) * 8], akT[:, hp, cs], Qbd[:, hp, :], True, True, r=[akT, Qbd], w=[pn])
            scn = P.sb("scn", [4, 32], F32); pnew = P.sb("pnew", [128, 32], BF16)
            P.op("dve", lambda e: e.memset(pnew[:], 0.0), w=[pnew])
            P.op("dve", lambda e: e.tensor_tensor(out=scn[:], in0=pn[:4, 0:32], in1=cm4[:], op=ALU.add), r=[pn, cm4], w=[scn])
            P.op("act", lambda e: e.activation(out=pnew[:4, :], in_=scn[:], func=AF.Exp, scale=0.125, bias=nmb[:4, 0:1]), r=[scn, nmb], w=[pnew])
            if cfg.get("sastage", 99) <= 2:
                raise StopBuild()
            kmS = P.sb("kmS", [128, 4, NB], F32); kmb = P.sb("kmb", [128, 4, NB], BF16)
            P.op("dve", lambda e: e.tensor_reduce(out=kmS[:], in_=kps[:].rearrange("p c (b t) -> p c b t", t=2), axis=AX.X, op=ALU.add), r=[kps], w=[kmS])
            P.op("act", lambda e: e.activation(out=kmb[:], in_=kmS[:], func=AF.Copy, scale=1.0 / 256.0), r=[kmS], w=[kmb])
            pg = nxt_pf()
            for h in range(8):
                mm(pg[:4, h * NB:(h + 1) * NB], qT[:, h, cs], kmb[:, h // 2, :], True, True, r=[qT, kmb], w=[pg])
            gsb = P.sb("gsbs", [4, 8, NB], F32); m8 = P.sb("m8s", [4, 8, 8], F32)
            P.op("act", lambda e: e.copy(out=gsb[:], in_=pg[:4, 0:8 * NB].rearrange("p (h b) -> p h b", h=8)), r=[pg], w=[gsb])
            for h in range(8):
                P.op("dve", lambda e: e.max(out=m8[:, h, :], in_=gsb[:, h, :]), r=[gsb], w=[m8])
            selt = P.sb("selts", [4, 8, NB], F32)
            P.op("dve", lambda e: e.tensor_tensor(out=selt[:], in0=gsb[:], in1=m8[:, :, 2:3].to_broadcast([4, 8, NB]), op=ALU.is_ge), r=[gsb, m8], w=[selt])
            mpT = P.sb("mpT", [4, NB, 8], BF16)
            P.op("dve", lambda e: e.tensor_scalar(out=mpT[:], in0=selt[:].rearrange("p h b -> p b h"), scalar1=MB, scalar2=None, op0=ALU.mult), r=[selt], w=[mpT])
            X = P.sb("X", [128, NB * 8, 4], BF16)
            P.op("dve", lambda e: e.memset(X[:], 0.0), w=[X])
            P.op("dve", lambda e: e.tensor_tensor(out=X[:4], in0=mpT[:].rearrange("p b h -> p (b h)").unsqueeze(2).to_broadcast([4, NB * 8, 4]),
                                                  in1=eye4[:].unsqueeze(1).to_broadcast([4, NB * 8, 4]), op=ALU.mult), r=[mpT, eye4], w=[X])
            if cfg.get("sastage", 99) <= 3:
                raise StopBuild()
            PTs = P.sb("PTs", [128, NPG, 32], BF16)
            Xf = X[:].rearrange("p a b -> p (a b)")
            nbank = (NB * 32 + 511) // 512
            for bk in range(nbank):
                w0 = bk * 512; w1 = min(NB * 32, w0 + 512); wd = w1 - w0
                pm = nxt_pf()
                mm(pm[:, 0:wd], onesb[:, :], Xf[:, w0:w1], True, True, r=[onesb, X], w=[pm])
                nblk = wd // 32; b0 = w0 // 32
                sm = P.tmp("sm", [128, 16, 2, 32], F32)
                P.op("dve", lambda e: e.tensor_tensor(out=sm[:, 0:nblk], in0=scst[:, 2 * b0:2 * (b0 + nblk), :].rearrange("p (b t) c -> p b t c", t=2),
                                                      in1=pm[:, 0:wd].rearrange("p (b c) -> p b c", c=32).unsqueeze(2).to_broadcast([128, nblk, 2, 32]), op=ALU.add), r=[scst, pm], w=[sm])
                P.op("act", lambda e: e.activation(out=PTs[:, 2 * b0:2 * (b0 + nblk), :].rearrange("p (b t) c -> p b t c", t=2), in_=sm[:, 0:nblk], func=AF.Exp, scale=0.125, bias=nmb[:, 0:1]), r=[sm, nmb], w=[PTs])
            if cfg.get("sastage", 99) <= 4:
                raise StopBuild()
            oS = pf[2]; rS = pf[3]
            for p in range(NPG):
                Vp = P.tmp("Vp", [128, 512], BF16, bufs=6)
                P.idma(out=Vp[:], in_=cache_v, in_offset=bass.IndirectOffsetOnAxis(ap=pidx[:, p:p + 1], axis=0), r=[pidx], w=[Vp])
                mm(oS[:32, :], PTs[:, p, :], Vp[:], p == 0, False, r=[PTs, Vp], w=[oS])
                mm(rS[:32, 0:1], PTs[:, p, :], onesb[:, 0:1], p == 0, False, r=[PTs, onesb], w=[rS])
            jt = c0 // 128
            mm(oS[:32, :], pnew[:], vtok[:, jt, :], False, True, r=[pnew, vtok], w=[oS])
            mm(rS[:32, 0:1], pnew[:], onesb[:, 0:1], False, True, r=[pnew, onesb], w=[rS])
            if cfg.get("sastage", 99) <= 5:
                raise StopBuild()
            rr_ = P.sb("rrs", [32, 1], F32); msk = P.sb("msk", [128, 512], BF16)
            P.op("dve", lambda e: e.memset(msk[:], 0.0), w=[msk])
            P.op("dve", lambda e: e.reciprocal(out=rr_[:], in_=rS[:32, 0:1]), r=[rS], w=[rr_])
            P.op("dve", lambda e: e.scalar_tensor_tensor(out=msk[:32, :], in0=oS[:32, :], scalar=rr_[:, 0:1], in1=hmask[:], op0=ALU.mult, op1=ALU.mult), r=[oS, rr_, hmask], w=[msk])
            po = nxt_pf()
            mm(po[:4, :], selq[:], msk[:], True, True, r=[selq, msk], w=[po])
            oat = P.sb("oat", [128, 512], BF16)
            P.op("dve", lambda e: e.memset(oat[:], 0.0), w=[oat])
            P.op("act", lambda e: e.copy(out=oat[:4, :], in_=po[:4, :]), r=[po], w=[oat])
            pt_ = nxt_pbf()
            for c4 in range(4):
                tr(pt_[:, c4 * 128:(c4 + 1) * 128], oat[:, c4 * 128:(c4 + 1) * 128], identb[:, :], r=[oat, identb], w=[pt_])
            P.op("act", lambda e: e.copy(out=oaT[:, :, cs], in_=pt_[:, 0:512].rearrange("p (c t) -> p c t", c=4)[:, :, 0:4]), r=[pt_], w=[oaT])

    def main_program():
        if cfg.get("do_prompt", 1):
            P.op("dve", lambda e: e.memset(HS[:], 0.0), w=[HS])
            for h in range(4):
                P.op("dve", lambda e: e.memset(Sst[h][:], 0.0), w=[Sst[h]])
                P.op("dve", lambda e: e.memset(Sbf[h][:], 0.0), w=[Sbf[h]])
            with P.scope():
                ctx = {"KT": [P.sb("KT%d" % s, [128, 4, 512], BF16) for s in range(NST)],
                       "V": [P.sb("V%d" % s, [128, 4, 8, 66], BF16) for s in range(NST)],
                       "kmT": P.sb("kmT", [128, 4, 16], BF16)}
                P.op("dve", lambda e: e.memset(ctx["kmT"][:], 0.0), w=[ctx["kmT"]])
                for s in range(NST):
                    P.op("dve", lambda e: e.memset(ctx["V"][s][:], 1.0), w=[ctx["V"][s]])
                for s in range(NST):
                    tiles = []
                    for i in range(4):
                        r0 = s * 512 + i * 128
                        tiles.append((0, 128, xp[r0:r0 + 128, :], {"k": kp[r0:r0 + 128, :], "v": vp[r0:r0 + 128, :], "y": yp[r0:r0 + 128, :]}))
                    supertile(tiles, True, s, ctx)
                for h in range(4):
                    P.dma("pool", hgp[h], Sst[h][:], r=[Sst[h]], is_output=True)
                P.dma("pool", cvp, HS[:, 0], r=[HS], is_output=True)
        if cfg.get("do_sample", 1):
            tiles = []
            for j in range(NS):
                tiles.append(((1 + j) if not cfg.get("dbg_seq0") else 0, 128, xs[j * 128:(j + 1) * 128, :], {"k": ksn[j * 128:(j + 1) * 128, :], "v": vsn[j * 128:(j + 1) * 128, :], "y": ys[j * 128:(j + 1) * 128, :]}))
            supertile(tiles, False, 0, None)


    try:
        main_program()
    except StopBuild:
        pass
    P.finish()
    return P


def host_consts():
    bf = ml_dtypes.bfloat16
    c = {}
    c["c_identb"] = np.eye(128, dtype=np.float32).astype(bf)
    c["c_identf"] = np.eye(128, dtype=np.float32)
    s = np.arange(128)
    c["c_tri"] = (s[:, None] <= s[None, :]).astype(np.float32)
    c["c_causal"] = ((s[:, None] <= s[None, :]).astype(np.float32) * MB).astype(bf)
    es = np.zeros((128, 17, 128), np.float32)
    for j in range(17):
        es[j, j, :] = 1.0
    c["c_esel"] = es.reshape(128, 17 * 128).astype(bf)
    cm = np.zeros((4, 8, 4), np.float32)
    for k in range(4):
        for q in range(4):
            cm[k, :, q] = MB if k <= q else 0.0
    c["c_cm4"] = cm.reshape(4, 32)
    c["c_eye4"] = np.eye(4, dtype=np.float32).astype(bf)
    hm = np.zeros((8, 4, 8, 64), np.float32)
    for h in range(8):
        hm[h, :, h, :] = 1.0
    c["c_hmask"] = hm.reshape(32, 512)
    sq = np.zeros((8, 4, 4), np.float32)
    for q in range(4):
        sq[:, q, q] = 1.0
    sqp = np.zeros((128, 4), np.float32)
    sqp[:32] = sq.reshape(32, 4)
    c["c_selq"] = sqp.astype(bf)
    return c


def make_in_maps(cfg, ncores, x_prompt, x_sample, cache_k, cache_v, state_hgrn, state_conv, page_table, c_prompt, c_sample,
                 norm1_g, norm2_g, w_ada, b_ada, w_in, hgrn_lb_logits, hg_norm_g, w_a_out, w_b_out, w_o,
                 w_up, conv_w, conv_b, w_down, final_g):
    T = cfg["T"]; NS = cfg["NS"]; NPG = cfg["NPG"]; NPHYS = cfg["NPHYS"]
    f = np.float32
    A = lambda a: np.ascontiguousarray(np.asarray(a))
    consts = host_consts()
    shared = dict(consts)
    shared["cache_k"] = A(np.asarray(cache_k)[0, :NPHYS]).reshape(NPHYS * 128, 512)
    shared["cache_v"] = A(np.asarray(cache_v)[0, :NPHYS]).reshape(NPHYS * 128, 512)
    shared["w_ada"] = A(w_ada)[0]; shared["w_in"] = A(w_in)[0]; shared["w_a_out"] = A(w_a_out)[0]
    shared["w_b_out"] = A(w_b_out)[0]; shared["w_o"] = A(w_o)[0]; shared["w_up"] = A(w_up)[0]; shared["w_down"] = A(w_down)[0]
    shared["n1gT"] = A(A(norm1_g)[0].reshape(KD, 128).T); shared["n2gT"] = A(A(norm2_g)[0].reshape(KD, 128).T)
    shared["badaT"] = A(A(b_ada)[0].reshape(48, 128).T)
    shared["lbT"] = A(A(hgrn_lb_logits).reshape(2, 4, 128).transpose(2, 1, 0))
    shared["convwT"] = A(A(conv_w)[0].reshape(3, KF, 128).transpose(2, 1, 0))
    shared["convbT"] = A(A(conv_b)[0].reshape(KF, 128).T)
    shared["hgn_bc"] = A(np.broadcast_to(A(hg_norm_g)[0][None, :], (128, 512)))
    shared["fg_bc"] = A(np.broadcast_to(A(final_g)[None, :], (128, D)))
    maps = []
    for c in range(ncores):
        m = dict(shared)
        m["xp"] = A(x_prompt)[c, :T] if T > 0 else np.zeros((0, D), f)
        xsp = np.zeros((NS, 128, D), f)
        xsp[:, :4, :] = A(x_sample)[c * NS:(c + 1) * NS]
        m["xs"] = xsp.reshape(NS * 128, D)
        cc = np.concatenate([A(c_prompt)[c:c + 1], A(c_sample)[c * NS:(c + 1) * NS]], axis=0)
        m["cT"] = A(cc.reshape(1 + NS, KD, 128).transpose(2, 1, 0))
        m["ptab"] = A(page_table)[c * NS:(c + 1) * NS, :NPG].astype(np.int32)
        m["shg"] = A(state_hgrn)[0, c * NS:(c + 1) * NS]
        m["convT"] = A(A(state_conv)[0, c * NS:(c + 1) * NS].reshape(NS, 2, KF, 128).transpose(0, 3, 2, 1))
        maps.append(m)
    return maps


def gather_outputs(cfg, ncores, res):
    T = cfg["T"]; NS = cfg["NS"]
    R = res.results
    y_prompt = np.stack([R[c]["yp"] for c in range(ncores)])
    y_sample = np.concatenate([R[c]["ys"].reshape(NS, 128, D)[:, :4] for c in range(ncores)])
    k_prompt = np.stack([R[c]["kp"].reshape(T, 8, 64) for c in range(ncores)])[None]
    v_prompt = np.stack([R[c]["vp"].reshape(T, 8, 64) for c in range(ncores)])[None]
    k_sample = np.concatenate([R[c]["ksn"].reshape(NS, 128, 8, 64)[:, :4] for c in range(ncores)])[None]
    v_sample = np.concatenate([R[c]["vsn"].reshape(NS, 128, 8, 64)[:, :4] for c in range(ncores)])[None]
    hg_p = np.stack([R[c]["hgp"] for c in range(ncores)])[None]
    hg_s = np.concatenate([R[c]["hgs"] for c in range(ncores)])[None]
    cv_p = np.stack([R[c]["cvp"].transpose(2, 1, 0).reshape(2, FF) for c in range(ncores)])[None]
    cv_s = np.concatenate([R[c]["cvs"].transpose(0, 3, 2, 1).reshape(NS, 2, FF) for c in range(ncores)])[None]
    outs = (y_prompt, y_sample, k_prompt, v_prompt, k_sample, v_sample, hg_p, hg_s, cv_p, cv_s)
    return tuple(np.ascontiguousarray(o.astype(np.float32)) for o in outs)


def kernel(**inputs):
    cfg = {"T": 4096, "NS": 4, "NPG": 128, "NPHYS": 5120, "do_prompt": 1, "do_sample": 1}
    nc = bass.Bass("TRN2", target_bir_lowering=False)
    build(nc, cfg)
    maps = make_in_maps(cfg, 8, **inputs)
    res = run_bass_kernel_spmd(nc, maps, core_ids=list(range(8)))
    return gather_outputs(cfg, 8, res)
```

```python
import contextlib
import numpy as np
import concourse.bass as bass
import concourse.mybir as mybir

F32 = mybir.dt.float32
BF16 = mybir.dt.bfloat16
I32 = mybir.dt.int32
U32 = mybir.dt.uint32
AF = mybir.ActivationFunctionType
ALU = mybir.AluOpType
AX = mybir.AxisListType

NDMA = 6
NDMA_Q = {"sp": 6, "pool": 2, "act": 2}


class Buf:
    def __init__(self, prog, name, ap, root=None, lo=0, hi=None):
        self.prog = prog
        self.name = name
        self.ap = ap
        self.root = root if root is not None else self
        self.lo = lo
        self.hi = hi if hi is not None else 1 << 60
        if root is None:
            self.writes = []
            self.reads = []
            self.init_ev = dict(prog.freed_events)

    def sub(self, lo, hi, ap=None):
        return Buf(self.prog, self.name, ap if ap is not None else self.ap, root=self.root, lo=lo, hi=hi)

    def __getitem__(self, key):
        return self.ap[key]


class Prog:
    def __init__(self, nc):
        self.nc = nc
        self.eng = {"pe": nc.tensor, "act": nc.scalar, "dve": nc.vector, "pool": nc.gpsimd, "sp": nc.sync}
        self.stack = contextlib.ExitStack()
        self.sem = {}
        for k in ("pe", "act", "dve", "pool"):
            self.sem[k] = self.stack.enter_context(nc.semaphore("s_" + k))
        for q in ("sp", "pool", "act"):
            for i in range(NDMA):
                self.sem[("d", q, i)] = self.stack.enter_context(nc.semaphore("d_%s%d" % (q, i)))
        self.count = {k: 0 for k in ("pe", "act", "dve", "pool")}
        self.dcount = {"sp": 0, "pool": 0, "act": 0}
        self.waited = {e: {} for e in ("pe", "act", "dve", "pool", "sp")}
        self.freed_events = {}
        self.scopes = []
        self.out_events = {}
        self.nwaits = 0
        self.rings = {}
        self.uid = 0

    @contextlib.contextmanager
    def scope(self):
        st = contextlib.ExitStack()
        bufs = []
        self.scopes.append((st, bufs, {}))
        try:
            yield
        finally:
            self.scopes.pop()
            for b in bufs:
                for (_, _, k, v) in b.writes + b.reads:
                    if self.freed_events.get(k, 0) < v:
                        self.freed_events[k] = v
                for k, v in b.init_ev.items():
                    if self.freed_events.get(k, 0) < v:
                        self.freed_events[k] = v
            st.close()

    def sb(self, name, shape, dtype):
        st, bufs = self.scopes[-1][:2] if self.scopes else (self.stack, [])
        self.uid += 1
        t = st.enter_context(self.nc.sbuf_tensor("%s_%d" % (name, self.uid), list(shape), dtype))
        b = Buf(self, name, t)
        if self.scopes:
            bufs.append(b)
        return b

    def tmp(self, name, shape, dtype, bufs=2):
        st, bufs_l, rings = self.scopes[-1]
        ent = rings.setdefault(name, [[], 0])
        if len(ent[0]) < bufs:
            self.uid += 1
            t = st.enter_context(self.nc.sbuf_tensor("%s_%d" % (name, self.uid), list(shape), dtype))
            b = Buf(self, name, t)
            bufs_l.append(b)
            ent[0].append(b)
            return b
        b = ent[0][ent[1] % bufs]
        ent[1] += 1
        return b

    def ps(self, name, shape, dtype):
        st, bufs = self.scopes[-1][:2] if self.scopes else (self.stack, [])
        t = st.enter_context(self.nc.psum_tensor(name, list(shape), dtype))
        b = Buf(self, name, t)
        b.excl = True
        if self.scopes:
            bufs.append(b)
        return b

    def dram(self, name, shape, dtype, kind="Internal"):
        t = self.nc.dram_tensor(name, list(shape), dtype, kind=kind)
        b = Buf(self, name, t.ap())
        b.init_ev = {}
        return b

    @staticmethod
    def _ov(a, b):
        return a[0] < b.hi and b.lo < a[1]

    def _collect(self, engine, reads, writes):
        raw = {}
        oth = {}

        def add(d, k, v):
            if d.get(k, 0) < v:
                d[k] = v

        for b in reads:
            r = b.root
            for k, v in r.init_ev.items():
                add(raw, k, v)
            for w in r.writes:
                if self._ov(w, b):
                    add(raw, w[2], w[3])
        for b in writes:
            r = b.root
            for k, v in r.init_ev.items():
                add(raw, k, v)
            for w in r.writes:
                if self._ov(w, b):
                    add(oth, w[2], w[3])
            for rd in r.reads:
                if self._ov(rd, b):
                    add(oth, rd[2], rd[3])
        return raw, oth

    def _emit_waits(self, engine, deps, is_dma, extra=None):
        raw, oth = deps
        e = self.eng[engine]
        wd = self.waited[engine]
        need = dict(raw)
        for k, v in oth.items():
            if k == engine and not is_dma:
                continue
            if need.get(k, 0) < v:
                need[k] = v
        if extra:
            for k, v in extra.items():
                if need.get(k, 0) < v:
                    need[k] = v
        for k, v in need.items():
            if wd.get(k, 0) >= v:
                continue
            e.wait_ge(self.sem[k], v)
            self.nwaits += 1
            wd[k] = v

    def _record(self, reads, writes, k, v):
        for b in reads:
            r = b.root
            r.reads = [x for x in r.reads if not (x[2] == k and x[0] == b.lo and x[1] == b.hi)]
            r.reads.append((b.lo, b.hi, k, v))
        for b in writes:
            r = b.root
            r.writes = [x for x in r.writes if not (b.lo <= x[0] and x[1] <= b.hi)]
            r.reads = [x for x in r.reads if not (b.lo <= x[0] and x[1] <= b.hi)]
            r.writes.append((b.lo, b.hi, k, v))
            if b.lo <= 0 and b.hi >= (1 << 60):
                r.init_ev = {}

    def op(self, engine, fn, r=(), w=()):
        ex = [b for b in r if getattr(b.root, "excl", False) and b not in w]
        if ex:
            w = list(w) + ex
        deps = self._collect(engine, r, w)
        self._emit_waits(engine, deps, False)
        ins = fn(self.eng[engine])
        self.count[engine] += 1
        v = self.count[engine]
        ins.then_inc(self.sem[engine], 1)
        self._record(r, w, engine, v)
        return ins

    def dma(self, queue, out, in_, r=(), w=(), is_output=False, **kw):
        n = self.dcount[queue]
        nq = NDMA_Q[queue]
        slot = n % nq
        k = ("d", queue, slot)
        deps = self._collect(queue, r, w)
        prev = 16 * (n // nq)
        self._emit_waits(queue, deps, True, extra={k: prev} if prev > 0 else None)
        ins = self.eng[queue].dma_start(out=out, in_=in_, **kw)
        v = prev + 16
        ins.then_inc(self.sem[k], 16)
        self.dcount[queue] += 1
        self._record(r, w, k, v)
        if is_output:
            self.out_events[k] = max(self.out_events.get(k, 0), v)
        return ins

    def idma(self, out, in_, in_offset, r=(), w=()):
        queue = "pool"
        n = self.dcount[queue]
        nq = NDMA_Q[queue]
        slot = n % nq
        k = ("d", queue, slot)
        deps = self._collect(queue, r, w)
        prev = 16 * (n // nq)
        self._emit_waits(queue, deps, True, extra={k: prev} if prev > 0 else None)
        ins = self.nc.gpsimd.indirect_dma_start(out=out, out_offset=None, in_=in_, in_offset=in_offset)
        v = prev + 16
        ins.then_inc(self.sem[k], 16)
        self.dcount[queue] += 1
        self._record(r, w, k, v)
        return ins

    def finish(self):
        e = self.eng["sp"]
        for q in ("sp", "pool", "act"):
            n = self.dcount[q]
            nq = NDMA_Q[q]
            for slot in range(nq):
                cnt = (n - slot + nq - 1) // nq if n > slot else 0
                if cnt > 0:
                    e.wait_ge(self.sem[("d", q, slot)], 16 * cnt)
        for k in ("pe", "act", "dve", "pool"):
            if self.count[k] > 0:
                e.wait_ge(self.sem[k], self.count[k])
        self.stack.close()


from concourse.bass_utils import run_bass_kernel_spmd
import ml_dtypes

D = 1024
KD = 8
NIN = 5632
FF = 2816
KF = 22
EPS = 1e-6
MB = 8192.0
NEG = -1.0e30


def wchunks():
    L = []
    for c in range(7):
        L.append(("in%d" % c, "w_in", 0, 8, c * 512))
    for half in range(2):
        L.append(("in%d" % (7 + half), "w_in", 0, 8, (7 + half) * 512))
        L.append(("in%d" % (9 + half), "w_in", 0, 8, (9 + half) * 512))
        L.append(("wa%d" % half, "w_a_out", 0, 4, half * 512))
        L.append(("wb%d" % half, "w_b_out", 0, 4, half * 512))
    for half in range(2):
        L.append(("wo%d" % half, "w_o", 0, 8, half * 512))
    for c in range(11):
        L.append(("up%d" % c, "w_up", 0, 8, c * 512))
    for half in range(2):
        for kg, (k0, nk) in enumerate(((0, 8), (8, 8), (16, 6))):
            L.append(("dn%d_%d" % (half, kg), "w_down", k0, nk, half * 512))
    return L


class StopBuild(Exception):
    pass


def build(nc, cfg):
    stage = cfg.get("stage", 99)

    def ckpt(n):
        if stage <= n:
            raise StopBuild()
    T = cfg["T"]; NS = cfg["NS"]; NPG = cfg["NPG"]; NPHYS = cfg["NPHYS"]
    NSEQ = 1 + NS
    NST = T // 512
    NB_S = NPG // 2
    P = Prog(nc)

    def din(name, shape, dt=F32):
        return nc.dram_tensor(name, list(shape), dt, kind="ExternalInput").ap()

    def dout(name, shape, dt=F32):
        return nc.dram_tensor(name, list(shape), dt, kind="ExternalOutput").ap()

    xp = din("xp", [T, D]); xs = din("xs", [NS * 128, D])
    cT = din("cT", [128, KD, NSEQ])
    cache_k = din("cache_k", [NPHYS * 128, 512]); cache_v = din("cache_v", [NPHYS * 128, 512])
    ptab = din("ptab", [NS, NPG], I32)
    shg = din("shg", [NS, 4, 128, 128])
    convT = din("convT", [NS, 128, KF, 2])
    W = {"w_ada": din("w_ada", [D, 6 * D]), "w_in": din("w_in", [D, NIN]), "w_a_out": din("w_a_out", [512, D]),
         "w_b_out": din("w_b_out", [512, D]), "w_o": din("w_o", [D, D]), "w_up": din("w_up", [D, 2 * FF]),
         "w_down": din("w_down", [FF, D])}
    n1gT = din("n1gT", [128, KD]); n2gT = din("n2gT", [128, KD]); badaT = din("badaT", [128, 48])
    lbT = din("lbT", [128, 4, 2]); convwT = din("convwT", [128, KF, 3]); convbT = din("convbT", [128, KF])
    hgn_bc = din("hgn_bc", [128, 512]); fg_bc = din("fg_bc", [128, D])
    c_identb = din("c_identb", [128, 128], BF16); c_identf = din("c_identf", [128, 128])
    c_tri = din("c_tri", [128, 128]); c_causal = din("c_causal", [128, 128], BF16)
    c_esel = din("c_esel", [128, 17 * 128], BF16)
    c_cm4 = din("c_cm4", [4, 32]); c_eye4 = din("c_eye4", [4, 4], BF16)
    c_hmask = din("c_hmask", [32, 512]); c_selq = din("c_selq", [128, 4], BF16)

    yp = dout("yp", [T, D]); ys = dout("ys", [NS * 128, D])
    kp = dout("kp", [T, 512]); vp = dout("vp", [T, 512])
    ksn = dout("ksn", [NS * 128, 512]); vsn = dout("vsn", [NS * 128, 512])
    hgp = dout("hgp", [4, 128, 128]); hgs = dout("hgs", [NS, 4, 128, 128])
    cvp = dout("cvp", [128, KF, 2]); cvs = dout("cvs", [NS, 128, KF, 2])

    WCH = wchunks()
    NCH = len(WCH)
    wscr = [P.dram("wscr%d" % i, [128, 8 * 512], BF16) for i in range(NCH)]

    NSLOT = 4
    wslot = [P.sb("wslot%d" % i, [128, 8, 512], BF16) for i in range(NSLOT)]
    identb = P.sb("identb", [128, 128], BF16); identf = P.sb("identf", [128, 128], F32)
    tri = P.sb("tri", [128, 128], F32); causal = P.sb("causal", [128, 128], BF16)
    esel = P.sb("esel", [128, 17 * 128], BF16)
    cm4 = P.sb("cm4", [4, 32], F32); eye4 = P.sb("eye4", [4, 4], BF16)
    hmask = P.sb("hmask", [32, 512], F32); selq = P.sb("selq", [128, 4], BF16)
    onesf = P.sb("onesf", [128, 128], F32); onesb = P.sb("onesb", [128, 128], BF16)
    hgn = P.sb("hgn", [128, 512], F32); fg = P.sb("fg", [128, D], F32)
    modT = P.sb("modT", [128, 48, NSEQ], F32)
    s1 = P.sb("s1", [128, KD, NSEQ], F32); s2 = P.sb("s2", [128, KD, NSEQ], F32)
    lb = P.sb("lb", [128, 4], F32); oml = P.sb("oml", [128, 4], F32); noml = P.sb("noml", [128, 4], F32)
    cw = P.sb("cw", [128, KF, 3], F32); cb = P.sb("cb", [128, KF], F32)
    G1 = P.sb("G1", [128, D], F32); G2 = P.sb("G2", [128, D], F32)
    gstate = {"seq": None}
    pf = [P.ps("pf%d" % i, [128, 512], F32) for i in range(6)]
    pbf = [P.ps("pbf%d" % i, [128, 1024], BF16) for i in range(2)]
    rr = {"pf": 0, "pbf": 0}

    def nxt_pf(group=(0, 1, 2, 3)):
        i = group[rr["pf"] % len(group)]
        rr["pf"] += 1
        return pf[i]

    def nxt_pbf():
        i = rr["pbf"] % 2
        rr["pbf"] += 1
        return pbf[i]

    def mm(out, lhsT, rhs, start, stop, r, w):
        return P.op("pe", lambda e: e.matmul(out, lhsT, rhs, start=start, stop=stop, skip_group_check=True), r=r, w=w)

    def tr(out, in_, ident, r, w):
        return P.op("pe", lambda e: e.transpose(out, in_, ident), r=r, w=w)

    for sbt, dr in ((identb, c_identb), (identf, c_identf), (tri, c_tri), (causal, c_causal), (esel, c_esel),
                    (cm4, c_cm4), (eye4, c_eye4), (hmask, c_hmask), (selq, c_selq), (hgn, hgn_bc), (fg, fg_bc),
                    (cw, convwT), (cb, convbT)):
        P.dma("sp", sbt[:], dr, w=[sbt])
    P.op("dve", lambda e: e.memset(onesf[:], 1.0), w=[onesf])
    P.op("dve", lambda e: e.memset(onesb[:], 1.0), w=[onesb])

    def setup_phase():
      with P.scope():
          cTs = P.sb("cTs", [128, KD, NSEQ], F32); scb = P.sb("scb", [128, KD, 8], BF16)
          n1g = P.sb("n1g", [128, KD], F32); n2g = P.sb("n2g", [128, KD], F32); bada = P.sb("bada", [128, 48], F32)
          lbl = P.sb("lbl", [128, 4, 2], F32); lbd = P.sb("lbd", [128, 4], F32)
          P.dma("sp", cTs[:], cT, w=[cTs]); P.dma("sp", n1g[:], n1gT, w=[n1g]); P.dma("sp", n2g[:], n2gT, w=[n2g])
          P.dma("sp", bada[:], badaT, w=[bada]); P.dma("sp", lbl[:], lbT, w=[lbl])
          P.op("act", lambda e: e.activation(out=scb[:, :, 0:NSEQ], in_=cTs[:], func=AF.Silu), r=[cTs], w=[scb])
          P.op("dve", lambda e: e.tensor_tensor(out=lbd[:], in0=lbl[:, :, 0], in1=lbl[:, :, 1], op=ALU.subtract), r=[lbl], w=[lbd])
          P.op("act", lambda e: e.activation(out=lb[:], in_=lbd[:], func=AF.Sigmoid), r=[lbd], w=[lb])
          P.op("dve", lambda e: e.tensor_scalar(out=oml[:], in0=lb[:], scalar1=-1.0, scalar2=1.0, op0=ALU.mult, op1=ALU.add), r=[lb], w=[oml])
          P.op("dve", lambda e: e.tensor_scalar(out=noml[:], in0=oml[:], scalar1=-1.0, scalar2=None, op0=ALU.mult), r=[oml], w=[noml])
          if stage <= -3:
              raise StopBuild()
          mps = pf[0]
          for c12 in range(12):
              sl = wslot[c12 % NSLOT]
              P.dma("pool", sl[:], W["w_ada"][:, c12 * 512:(c12 + 1) * 512].rearrange("(k p) j -> p k j", p=128), w=[sl])
              for m in range(4):
                  ch = c12 * 4 + m
                  for k in range(KD):
                      mm(mps[:, ch * NSEQ:(ch + 1) * NSEQ], sl[:, k, m * 128:(m + 1) * 128], scb[:, k, 0:NSEQ],
                         start=(k == 0), stop=(k == KD - 1), r=[sl, scb], w=[mps])
          P.op("dve", lambda e: e.tensor_tensor(out=modT[:], in0=mps[:, 0:48 * NSEQ].rearrange("p (c s) -> p c s", s=NSEQ),
                                                in1=bada[:].unsqueeze(2).to_broadcast([128, 48, NSEQ]), op=ALU.add), r=[mps, bada], w=[modT])
          for (sd, ng, off) in ((s1, n1g, 8), (s2, n2g, 32)):
              P.op("dve", lambda e: e.tensor_scalar(out=sd[:], in0=modT[:, off:off + 8, :], scalar1=1.0, scalar2=None, op0=ALU.add), r=[modT], w=[sd])
              P.op("dve", lambda e: e.tensor_tensor(out=sd[:], in0=sd[:], in1=ng[:].unsqueeze(2).to_broadcast([128, KD, NSEQ]), op=ALU.mult), r=[sd, ng], w=[sd])
          if stage <= -2:
              raise StopBuild()
          for i, (nm, wn, k0, nk, c0) in enumerate(WCH[:(cfg.get('ncast', 999) if cfg.get('wmode', 'scratch') == 'scratch' else 0)]):
              sl = wslot[i % NSLOT]
              src = W[wn][k0 * 128:(k0 + nk) * 128, c0:c0 + 512].rearrange("(k p) j -> p k j", p=128)
              P.dma("pool", sl[:, 0:nk, :], src, w=[sl])
              P.dma("sp", wscr[i][:, 0:nk * 512], sl[:, 0:nk, :].rearrange("p k j -> p (k j)"), r=[sl], w=[wscr[i]])


    try:
        setup_phase()
    except StopBuild:
        P.finish()
        return P
    if stage <= -1:
        P.finish()
        return P

    ckpt_setup = True
    ws = {"pos": 0, "issued": 0}

    def w_issue_upto(n):
        while ws["issued"] < n:
            i = ws["issued"]
            ci = i % NCH
            nk = WCH[ci][3]
            sl = wslot[i % NSLOT]
            if cfg.get("wmode", "scratch") == "scratch":
                P.dma("sp", sl[:, 0:nk, :].rearrange("p k j -> p (k j)"), wscr[ci][:, 0:nk * 512], r=[wscr[ci]], w=[sl])
            else:
                (nm_, wn_, k0_, nk_, c0_) = WCH[ci]
                src_ = W[wn_][k0_ * 128:(k0_ + nk_) * 128, c0_:c0_ + 512].rearrange("(k p) j -> p k j", p=128)
                P.dma("pool", sl[:, 0:nk_, :], src_, w=[sl])
            ws["issued"] += 1

    def wget(name):
        i = ws["pos"]
        ci = i % NCH
        assert WCH[ci][0] == name, (WCH[ci][0], name)
        w_issue_upto(i + 3)
        ws["pos"] += 1
        return wslot[i % NSLOT]

    def build_G(seq):
        if gstate["seq"] == seq:
            return
        gstate["seq"] = seq
        for (G, off) in ((G1, 16), (G2, 40)):
            for k in range(KD):
                dg = P.tmp("dg", [128, 128], F32)
                P.op("dve", lambda e: e.tensor_scalar(out=dg[:], in0=identf[:], scalar1=modT[:, off + k, seq:seq + 1], scalar2=None, op0=ALU.mult), r=[identf, modT], w=[dg])
                pb = nxt_pf(group=(4, 5))
                mm(pb[:, 0:128], onesf[:, 0:128], dg[:], True, True, r=[onesf, dg], w=[pb])
                P.op("act", lambda e: e.copy(out=G[:, k * 128:(k + 1) * 128], in_=pb[:, 0:128]), r=[pb], w=[G])

    def rstd_from_ssq(ssq, n, cols, scale):
        P.op("act", lambda e: e.activation(out=ssq[:n, cols], in_=ssq[:n, cols], func=AF.Sqrt, scale=scale, bias=epsb[:n, 0:1]), r=[ssq, epsb], w=[ssq])
        P.op("dve", lambda e: e.reciprocal(out=ssq[:n, cols], in_=ssq[:n, cols]), r=[ssq], w=[ssq])

    epsb = P.sb("epsb", [128, 1], F32)
    P.op("dve", lambda e: e.memset(epsb[:], EPS), w=[epsb])
    nmb = P.sb("nmb", [128, 1], F32)
    P.op("dve", lambda e: e.memset(nmb[:], -MB / 8.0), w=[nmb])

    nrm_cnt = {"n": 0}

    def norm_to_hT(src_ap, src_bufs, n, seq, sc, shoff, hT, col0):
        nrm_cnt["n"] += 1
        xn = P.tmp("xn", [128, D], BF16)
        ssq = P.tmp("ssq", [128, 1], F32)
        P.op("act", lambda e: e.activation(out=xn[:n, :], in_=src_ap, func=AF.Square, accum_out=ssq[:n, 0:1]), r=src_bufs, w=[xn, ssq])
        rstd_from_ssq(ssq, n, slice(0, 1), 1.0 / D)
        P.op("dve", lambda e: e.tensor_scalar(out=xn[:n, :], in0=src_ap, scalar1=ssq[:n, 0:1], scalar2=None, op0=ALU.mult), r=src_bufs + [ssq], w=[xn])
        pb = nxt_pbf()
        for k in range(KD):
            tr(pb[:, k * 128:k * 128 + n], xn[:n, k * 128:(k + 1) * 128], identb[:n, :n], r=[xn, identb], w=[pb])
        for k in range(KD):
            if nrm_cnt["n"] % 2 == 0:
                P.op("dve", lambda e: e.tensor_scalar(out=hT[:, k, col0:col0 + n], in0=pb[:, k * 128:k * 128 + n], scalar1=sc[:, k, seq:seq + 1],
                                                      scalar2=modT[:, shoff + k, seq:seq + 1], op0=ALU.mult, op1=ALU.add), r=[pb, sc, modT], w=[hT])
            else:
                P.op("act", lambda e: e.activation(out=hT[:, k, col0:col0 + n], in_=pb[:, k * 128:k * 128 + n], func=AF.Identity,
                                                   scale=sc[:, k, seq:seq + 1], bias=modT[:, shoff + k, seq:seq + 1]), r=[pb, sc, modT], w=[hT])

    NSEG = max(1, NS)
    HS = P.sb("HS", [128, NSEG, KF, 2], F32)
    Sst = [P.sb("S%d" % h, [128, 128], F32) for h in range(4)]
    Sbf = [P.sb("Sbf%d" % h, [128, 128], BF16) for h in range(4)]

    def supertile(tiles, is_prompt, st_idx, attn_ctx):
        coff = []
        c = 0
        for t in tiles:
            coff.append(c); c += t[1]
        N = c
        NT = len(tiles)
        nseg = 1 if is_prompt else NT
        segn = N // nseg
        nr = 128 if is_prompt else 4
        nrs = segn if is_prompt else 4
        with P.scope():
            hT = P.sb("hT", [128, KD, N], BF16)
            x2 = P.sb("x2", [128, NT, D], F32)
            with P.scope():
                oaT = P.sb("oaT", [128, 4, N], BF16)
                obT = P.sb("obT", [128, 4, N], BF16)
                ckpt(0)
                with P.scope():
                    for i, (seq, n, xd, outs) in enumerate(tiles):
                        xt = P.tmp("xt", [128, D], F32)
                        P.dma("sp", xt[:n, :], xd, w=[xt])
                        norm_to_hT(xt[:n, :], [xt], n, seq, s1, 0, hT, coff[i])

                ckpt(1)

                def amm(sl, i, pb):
                    n = tiles[i][1]
                    for k in range(KD):
                        mm(pb[:n, :], hT[:, k, coff[i]:coff[i] + n], sl[:, k, :], k == 0, k == KD - 1, r=[sl, hT], w=[pb])

                with P.scope():
                    qT = P.sb("qT", [128, 8, N], BF16)
                    P.op("dve", lambda e: e.memset(qT[:], 0.0), w=[qT])
                    akT = P.sb("akT", [128, 4, N], BF16) if not is_prompt else None
                    vtok = P.sb("vtok", [128, NT, 512], BF16) if not is_prompt else None
                    sl = wget("in0")
                    for m in range(4):
                        pb = nxt_pf(); bmmh(sl, m, pb, hT, N)
                        P.op("act", lambda e: e.copy(out=qT[0:64, 2 * m, :], in_=pb[0:64, 0:N]), r=[pb], w=[qT])
                        P.op("act", lambda e: e.copy(out=qT[64:128, 2 * m + 1, :], in_=pb[64:128, 0:N]), r=[pb], w=[qT])
                    sl = wget("in1")
                    KTs = attn_ctx["KT"][st_idx] if is_prompt else akT
                    for m in range(4):
                        pb = nxt_pf(); bmmh(sl, m, pb, hT, N)
                        P.op("dve", lambda e: e.tensor_copy(out=KTs[:, m, :], in_=pb[:, 0:N]), r=[pb], w=[KTs])
                    for i, (seq, n, xd, outs) in enumerate(tiles):
                        pb = nxt_pf(); amm(sl, i, pb)
                        kst = P.tmp("kvst", [128, 512], F32)
                        P.op("act", lambda e: e.copy(out=kst[:n, :], in_=pb[:n, :]), r=[pb], w=[kst])
                        if not cfg.get("dbg_nokv"):
                            P.dma("pool", outs["k"], kst[:n, :], r=[kst], is_output=True)
                    if is_prompt:
                        kms = P.tmp("kms", [128, 4, 2], F32)
                        P.op("dve", lambda e: e.tensor_reduce(out=kms[:], in_=KTs[:].rearrange("p c (b t) -> p c b t", b=2), axis=AX.X, op=ALU.add), r=[KTs], w=[kms])
                        kmT = attn_ctx["kmT"]
                        P.op("act", lambda e: e.activation(out=kmT[:, :, 2 * st_idx:2 * st_idx + 2], in_=kms[:], func=AF.Copy, scale=1.0 / 256.0), r=[kms], w=[kmT])
                    sl = wget("in2")
                    for i, (seq, n, xd, outs) in enumerate(tiles):
                        pb = nxt_pf(); amm(sl, i, pb)
                        vst = P.tmp("kvst", [128, 512], F32)
                        P.op("act", lambda e: e.copy(out=vst[:n, :], in_=pb[:n, :]), r=[pb], w=[vst])
                        if not cfg.get("dbg_nokv"):
                            P.dma("pool", outs["v"], vst[:n, :], r=[vst], is_output=True)
                        if is_prompt:
                            Vs = attn_ctx["V"][st_idx]
                            P.op("dve", lambda e: e.tensor_copy(out=Vs[:, i, :, 0:64], in_=pb[:, :].rearrange("p (h d) -> p h d", h=8)), r=[pb], w=[Vs])
                        elif not cfg.get("dbg_novtok"):
                            P.op("dve", lambda e: e.tensor_copy(out=vtok[:n, i, :], in_=pb[:n, :]), r=[pb], w=[vtok])
                    ckpt(2)
                    if is_prompt:
                        attn_prompt(st_idx, attn_ctx, qT, oaT)
                    else:
                        for j, (seq, n, xd, outs) in enumerate(tiles):
                            attn_sample(j, qT, akT, vtok, oaT, coff[j])
                ckpt(3)
                with P.scope():
                    hi_t = P.sb("hi_t", [128, NT, 512], BF16)
                    hg_t = P.sb("hg_t", [128, NT, 512], BF16)
                    sl3 = wget("in3")
                    qts = [P.sb("qt%d" % h, [128, N], F32) for h in range(4)]
                    for h in range(4):
                        pb = nxt_pf(); bmmh(sl3, h, pb, hT, N)
                        P.op("act", lambda e: e.activation(out=qts[h][:], in_=pb[:, 0:N], func=AF.Silu), r=[pb], w=[qts[h]])
                    sl4 = wget("in4")
                    Qh = [P.sb("Qh%d" % h, [128, N], BF16) for h in range(4)]
                    Qm = [P.sb("Qm%d" % h, [128, N], BF16) for h in range(4)]
                    Km = [P.sb("Km%d" % h, [128, N], BF16) for h in range(4)]
                    Kl = [P.sb("Kl%d" % h, [128, N], BF16) for h in range(4)]
                    elast = [P.sb("el%d" % h, [128, NT], F32) for h in range(4)]
                    for h in range(4):
                        pb = nxt_pf(); bmmh(sl4, h, pb, hT, N)
                        sig = P.tmp("sig", [128, N], F32, bufs=1); kk = P.tmp("kk", [128, N], F32, bufs=1)
                        bb = P.tmp("bb", [128, N], F32, bufs=1); eb = P.tmp("eb", [128, N], F32, bufs=1)
                        P.op("act", lambda e: e.activation(out=sig[:], in_=pb[:, 0:N], func=AF.Sigmoid), r=[pb], w=[sig])
                        P.op("dve", lambda e: e.tensor_scalar(out=kk[:], in0=sig[:], scalar1=noml[:, h:h + 1], scalar2=oml[:, h:h + 1], op0=ALU.mult, op1=ALU.add), r=[sig, noml, oml], w=[kk])
                        P.op("act", lambda e: e.activation(out=sig[:], in_=sig[:], func=AF.Ln, scale=oml[:, h:h + 1], bias=lb[:, h:h + 1]), r=[sig, oml, lb], w=[sig])
                        if nr < 128:
                            P.op("dve", lambda e: e.memset(bb[:], 0.0), w=[bb])
                        for i, (seq, n, xd, outs) in enumerate(tiles):
                            cs = slice(coff[i], coff[i] + nr)
                            P.op("dve", lambda e: e.tensor_tensor_scan(out=bb[:, cs], data0=onesf[:, 0:nr], data1=sig[:, cs], initial=0.0, op0=ALU.mult, op1=ALU.add), r=[onesf, sig], w=[bb])
                        P.op("act", lambda e: e.activation(out=eb[:], in_=bb[:], func=AF.Exp), r=[bb], w=[eb])
                        P.op("dve", lambda e: e.tensor_tensor(out=qts[h][:], in0=qts[h][:], in1=eb[:], op=ALU.mult), r=[qts[h], eb], w=[qts[h]])
                        P.op("act", lambda e: e.copy(out=Qh[h][:], in_=qts[h][:]), r=[qts[h]], w=[Qh[h]])
                        P.op("dve", lambda e: e.reciprocal(out=bb[:], in_=eb[:]), r=[eb], w=[bb])
                        P.op("dve", lambda e: e.tensor_tensor(out=kk[:], in0=kk[:], in1=bb[:], op=ALU.mult), r=[kk, bb], w=[kk])
                        for i, (seq, n, xd, outs) in enumerate(tiles):
                            cs = slice(coff[i], coff[i] + n)
                            mid = coff[i] + nr // 2; last = coff[i] + nr - 1
                            P.op("pool", lambda e: e.tensor_copy(out=elast[h][:, i:i + 1], in_=eb[:, last:last + 1]), r=[eb], w=[elast[h]])
                            P.op("dve", lambda e: e.tensor_scalar(out=Qm[h][:, cs], in0=qts[h][:, cs], scalar1=bb[:, mid:mid + 1], scalar2=None, op0=ALU.mult), r=[qts[h], bb], w=[Qm[h]])
                            P.op("dve", lambda e: e.tensor_scalar(out=Km[h][:, cs], in0=kk[:, cs], scalar1=eb[:, mid:mid + 1], scalar2=None, op0=ALU.mult), r=[kk, eb], w=[Km[h]])
                            P.op("dve", lambda e: e.tensor_scalar(out=Kl[h][:, cs], in0=kk[:, cs], scalar1=eb[:, last:last + 1], scalar2=None, op0=ALU.mult), r=[kk, eb], w=[Kl[h]])
                            if nr < 128:
                                P.op("dve", lambda e: e.memset(Kl[h][:, coff[i] + nr:coff[i] + n], 0.0), w=[Kl[h]])
                    sl5 = wget("in5")
                    for i, (seq, n, xd, outs) in enumerate(tiles):
                        pb = nxt_pf(); amm(sl5, i, pb)
                        P.op("act", lambda e: e.copy(out=hi_t[:n, i, :], in_=pb[:n, :]), r=[pb], w=[hi_t])
                    sl6 = wget("in6")
                    for i, (seq, n, xd, outs) in enumerate(tiles):
                        pb = nxt_pf(); amm(sl6, i, pb)
                        P.op("act", lambda e: e.activation(out=hg_t[:n, i, :], in_=pb[:n, :], func=AF.Silu), r=[pb], w=[hg_t])
                    for i, (seq, n, xd, outs) in enumerate(tiles):
                        cs = slice(coff[i], coff[i] + n)
                        if not is_prompt:
                            for h in range(4):
                                P.dma("sp", Sst[h][:], shg[seq - 1, h], w=[Sst[h]])
                                P.op("act", lambda e: e.copy(out=Sbf[h][:], in_=Sst[h][:]), r=[Sst[h]], w=[Sbf[h]])
                        ob = P.tmp("ob", [128, 512], F32)
                        for h in range(4):
                            hs = slice(h * 128, (h + 1) * 128)
                            pa = nxt_pf()
                            mm(pa[:n, 0:n], Km[h][:, cs], Qm[h][:, cs], True, True, r=[Km[h], Qm[h]], w=[pa])
                            AT = P.tmp("AT", [128, 128], BF16)
                            P.op("dve", lambda e: e.tensor_tensor(out=AT[:n, :n], in0=pa[:n, 0:n], in1=tri[:n, :n], op=ALU.mult), r=[pa, tri], w=[AT])
                            pt_ = nxt_pbf()
                            tr(pt_[:n, 0:128], Kl[h][:, cs], identb[:, :], r=[Kl[h], identb], w=[pt_])
                            Kls = P.tmp("Kls", [128, 128], BF16)
                            P.op("act", lambda e: e.copy(out=Kls[:n, :], in_=pt_[:n, 0:128]), r=[pt_], w=[Kls])
                            po = nxt_pf()
                            mm(po[:n, 0:128], AT[:n, :n], hi_t[:n, i, hs], True, False, r=[AT, hi_t], w=[po])
                            mm(po[:n, 0:128], Qh[h][:, cs], Sbf[h][:], False, True, r=[Qh[h], Sbf[h]], w=[po])
                            P.op("act", lambda e: e.copy(out=ob[:n, hs], in_=po[:n, 0:128]), r=[po], w=[ob])
                            pS = nxt_pf()
                            mm(pS[:, 0:128], Kls[:n, :], hi_t[:n, i, hs], True, True, r=[Kls, hi_t], w=[pS])
                            P.op("dve", lambda e: e.scalar_tensor_tensor(out=Sst[h][:], in0=Sst[h][:], scalar=elast[h][:, i:i + 1], in1=pS[:, 0:128], op0=ALU.mult, op1=ALU.add), r=[Sst[h], elast[h], pS], w=[Sst[h]])
                            P.op("act", lambda e: e.copy(out=Sbf[h][:], in_=Sst[h][:]), r=[Sst[h]], w=[Sbf[h]])
                        if not is_prompt:
                            for h in range(4):
                                P.dma("pool", hgs[seq - 1, h], Sst[h][:], r=[Sst[h]], is_output=True)
                        sq = P.tmp("sq", [128, 512], F32, bufs=1); ss4 = P.tmp("ss4", [128, 4], F32)
                        P.op("pool", lambda e: e.tensor_tensor(out=sq[:n, :], in0=ob[:n, :], in1=ob[:n, :], op=ALU.mult), r=[ob], w=[sq])
                        P.op("dve", lambda e: e.tensor_reduce(out=ss4[:n, :], in_=sq[:n, :].rearrange("p (h d) -> p h d", h=4), axis=AX.X, op=ALU.add), r=[sq], w=[ss4])
                        rstd_from_ssq(ss4, n, slice(0, 4), 1.0 / 128.0)
                        P.op("dve", lambda e: e.tensor_tensor(out=sq[:n, :].rearrange("p (h d) -> p h d", h=4), in0=ob[:n, :].rearrange("p (h d) -> p h d", h=4),
                                                              in1=ss4[:n, :].unsqueeze(2).to_broadcast([n, 4, 128]), op=ALU.mult), r=[ob, ss4], w=[sq])
                        P.op("pool", lambda e: e.tensor_tensor(out=sq[:n, :], in0=sq[:n, :], in1=hgn[:n, :], op=ALU.mult), r=[sq, hgn], w=[sq])
                        obn = P.tmp("obn", [128, 512], BF16)
                        P.op("dve", lambda e: e.tensor_tensor(out=obn[:n, :], in0=sq[:n, :], in1=hg_t[:n, i, :], op=ALU.mult), r=[sq, hg_t], w=[obn])
                        pt_ = nxt_pbf()
                        for c4 in range(4):
                            tr(pt_[:, c4 * 128:c4 * 128 + n], obn[:n, c4 * 128:(c4 + 1) * 128], identb[:n, :n], r=[obn, identb], w=[pt_])
                        P.op("act", lambda e: e.copy(out=obT[:, :, cs], in_=pt_[:, 0:512].rearrange("p (c t) -> p c t", c=4)[:, :, 0:n]), r=[pt_], w=[obT])
                ckpt(4)
                with P.scope():
                    mixT = P.sb("mixT", [128, KD, N], BF16)
                    for half in range(2):
                        sga = P.tmp("sga", [128, 4, N], BF16, bufs=1); sgb = P.tmp("sgb", [128, 4, N], BF16, bufs=1)
                        sl = wget("in%d" % (7 + half))
                        for m in range(4):
                            pb = nxt_pf(); bmmh(sl, m, pb, hT, N)
                            P.op("act", lambda e: e.activation(out=sga[:, m, :], in_=pb[:, 0:N], func=AF.Sigmoid), r=[pb], w=[sga])
                        sl = wget("in%d" % (9 + half))
                        for m in range(4):
                            pb = nxt_pf(); bmmh(sl, m, pb, hT, N)
                            P.op("act", lambda e: e.activation(out=sgb[:, m, :], in_=pb[:, 0:N], func=AF.Sigmoid), r=[pb], w=[sgb])
                        sl = wget("wa%d" % half)
                        for m in range(4):
                            pb = nxt_pf()
                            for k in range(4):
                                mm(pb[:, 0:N], sl[:, k, m * 128:(m + 1) * 128], oaT[:, k, :], k == 0, k == 3, r=[sl, oaT], w=[pb])
                            P.op("dve", lambda e: e.tensor_tensor(out=sga[:, m, :], in0=sga[:, m, :], in1=pb[:, 0:N], op=ALU.mult), r=[sga, pb], w=[sga])
                        sl = wget("wb%d" % half)
                        for m in range(4):
                            pb = nxt_pf()
                            for k in range(4):
                                mm(pb[:, 0:N], sl[:, k, m * 128:(m + 1) * 128], obT[:, k, :], k == 0, k == 3, r=[sl, obT], w=[pb])
                            P.op("dve", lambda e: e.tensor_tensor(out=sgb[:, m, :], in0=sgb[:, m, :], in1=pb[:, 0:N], op=ALU.mult), r=[sgb, pb], w=[sgb])
                            P.op("pool", lambda e: e.tensor_tensor(out=mixT[:, half * 4 + m, :], in0=sga[:, m, :], in1=sgb[:, m, :], op=ALU.add), r=[sga, sgb], w=[mixT])
                    for half in range(2):
                        sl = wget("wo%d" % half)
                        for i, (seq, n, xd, outs) in enumerate(tiles):
                            build_G(seq)
                            pb = nxt_pf()
                            for k in range(KD):
                                mm(pb[:n, :], mixT[:, k, coff[i]:coff[i] + n], sl[:, k, :], k == 0, k == KD - 1, r=[sl, mixT], w=[pb])
                            xt6 = P.tmp("xt6", [128, 512], F32)
                            P.dma("sp", xt6[:n, :], xd[:, half * 512:(half + 1) * 512], w=[xt6])
                            tmp = P.tmp("tmp", [128, 512], F32)
                            P.op("dve", lambda e: e.tensor_tensor(out=tmp[:n, :], in0=pb[:n, :], in1=G1[:n, half * 512:(half + 1) * 512], op=ALU.mult), r=[pb, G1], w=[tmp])
                            P.op("pool", lambda e: e.tensor_tensor(out=x2[:n, i, half * 512:(half + 1) * 512], in0=tmp[:n, :], in1=xt6[:n, :], op=ALU.add), r=[tmp, xt6], w=[x2])
            ckpt(5)
            with P.scope():
                for i, (seq, n, xd, outs) in enumerate(tiles):
                    norm_to_hT(x2[:n, i, :], [x2], n, seq, s2, 24, hT, coff[i])
            with P.scope():
                actT = P.sb("actT", [128, KF, N], BF16)
                if not is_prompt:
                    for j, (seq, n, xd, outs) in enumerate(tiles):
                        P.dma("sp", HS[:, j], convT[seq - 1], w=[HS])
                cur = {"name": None, "sl": None}

                def upchunk(col):
                    nm = "up%d" % (col // 512)
                    if cur["name"] != nm:
                        cur["sl"] = wget(nm); cur["name"] = nm
                    return cur["sl"], (col % 512) // 128

                for f in range(KF):
                    sl, m = upchunk(f * 128)
                    pb = nxt_pf(); bmmh(sl, m, pb, hT, N)
                    ue = P.tmp("ue", [128, nseg, segn + 2], F32)
                    P.op("pool", lambda e: e.tensor_copy(out=ue[:, :, 0:2], in_=HS[:, 0:nseg, f, :]), r=[HS], w=[ue])
                    P.op("act", lambda e: e.copy(out=ue[:, :, 2:2 + segn], in_=pb[:, 0:N].rearrange("p (s t) -> p s t", s=nseg)), r=[pb], w=[ue])
                    P.op("pool", lambda e: e.tensor_copy(out=HS[:, 0:nseg, f, :], in_=ue[:, :, nrs:nrs + 2]), r=[ue], w=[HS])
                    t1 = P.tmp("t1", [128, nseg, segn], F32); t2 = P.tmp("t2", [128, nseg, segn], F32, bufs=1)
                    P.op("dve", lambda e: e.tensor_scalar(out=t1[:], in0=ue[:, :, 2:2 + segn], scalar1=cw[:, f, 2:3], scalar2=cb[:, f:f + 1], op0=ALU.mult, op1=ALU.add), r=[ue, cw, cb], w=[t1])
                    P.op("dve", lambda e: e.scalar_tensor_tensor(out=t2[:], in0=ue[:, :, 1:1 + segn], scalar=cw[:, f, 1:2], in1=t1[:], op0=ALU.mult, op1=ALU.add), r=[ue, cw, t1], w=[t2])
                    P.op("dve", lambda e: e.scalar_tensor_tensor(out=t1[:], in0=ue[:, :, 0:segn], scalar=cw[:, f, 0:1], in1=t2[:], op0=ALU.mult, op1=ALU.add), r=[ue, cw, t2], w=[t1])
                    P.op("act", lambda e: e.activation(out=actT[:, f, :], in_=t1[:].rearrange("p s t -> p (s t)"), func=AF.Gelu), r=[t1], w=[actT])
                for f in range(KF):
                    sl, m = upchunk(FF + f * 128)
                    pb = nxt_pf(); bmmh(sl, m, pb, hT, N)
                    P.op("dve", lambda e: e.tensor_tensor(out=actT[:, f, :], in0=actT[:, f, :], in1=pb[:, 0:N], op=ALU.mult), r=[actT, pb], w=[actT])
                if not is_prompt:
                    for j, (seq, n, xd, outs) in enumerate(tiles):
                        P.dma("pool", cvs[seq - 1], HS[:, j], r=[HS], is_output=True)
                ckpt(6)
                for half in range(2):
                    accs = [pf[i] for i in range(NT)]
                    for kg, (k0, nk) in enumerate(((0, 8), (8, 8), (16, 6))):
                        sl = wget("dn%d_%d" % (half, kg))
                        for i, (seq, n, xd, outs) in enumerate(tiles):
                            for k in range(nk):
                                mm(accs[i][:n, :], actT[:, k0 + k, coff[i]:coff[i] + n], sl[:, k, :], (kg == 0 and k == 0), (kg == 2 and k == nk - 1), r=[sl, actT], w=[accs[i]])
                    for i, (seq, n, xd, outs) in enumerate(tiles):
                        build_G(seq)
                        hsl = slice(half * 512, (half + 1) * 512)
                        tmp = P.tmp("tmp", [128, 512], F32)
                        P.op("dve", lambda e: e.tensor_tensor(out=tmp[:n, :], in0=accs[i][:n, :], in1=G2[:n, hsl], op=ALU.mult), r=[accs[i], G2], w=[tmp])
                        P.op("pool", lambda e: e.tensor_tensor(out=x2[:n, i, hsl], in0=tmp[:n, :], in1=x2[:n, i, hsl], op=ALU.add), r=[tmp, x2], w=[x2])
            with P.scope():
                for i, (seq, n, xd, outs) in enumerate(tiles):
                    junk = P.tmp("junk", [128, D], BF16, bufs=1); ssq = P.tmp("ssq", [128, 1], F32)
                    P.op("act", lambda e: e.activation(out=junk[:n, :], in_=x2[:n, i, :], func=AF.Square, accum_out=ssq[:n, 0:1]), r=[x2], w=[junk, ssq])
                    rstd_from_ssq(ssq, n, slice(0, 1), 1.0 / D)
                    for half in range(2):
                        hsl = slice(half * 512, (half + 1) * 512)
                        yo = P.tmp("yo", [128, 512], F32)
                        P.op("dve", lambda e: e.scalar_tensor_tensor(out=yo[:n, :], in0=x2[:n, i, hsl], scalar=ssq[:n, 0:1], in1=fg[:n, hsl], op0=ALU.mult, op1=ALU.mult), r=[x2, ssq, fg], w=[yo])
                        P.dma("pool", outs["y"][:, hsl], yo[:n, :], r=[yo], is_output=True)

    def bmmh(sl, m, pb, hT, N):
        for k in range(KD):
            mm(pb[:, 0:N], sl[:, k, m * 128:(m + 1) * 128], hT[:, k, :], k == 0, k == KD - 1, r=[sl, hT], w=[pb])

    def attn_prompt(s, ctx, qT, oaT):
        KT = ctx["KT"]; V = ctx["V"]; kmT = ctx["kmT"]
        with P.scope():
            maskT = P.sb("maskT", [128, 8, 512], BF16)
            P.op("dve", lambda e: e.memset(maskT[:], 0.0), w=[maskT])
            oatok = P.sb("oatok", [128, 4, 512], BF16)
            for qi in range(4):
                own = 2 * s + qi // 2
                cs = slice(qi * 128, (qi + 1) * 128)
                pg = nxt_pf()
                for h in range(8):
                    mm(pg[:, h * 16:(h + 1) * 16], qT[:, h, cs], kmT[:, h // 2, :], True, True, r=[qT, kmT], w=[pg])
                gsb = P.tmp("gsb", [128, 8, 16], F32)
                P.op("act", lambda e: e.copy(out=gsb[:], in_=pg[:, 0:128].rearrange("p (h b) -> p h b", h=8)), r=[pg], w=[gsb])
                if own < 16:
                    P.op("dve", lambda e: e.memset(gsb[:, :, own:16], NEG), w=[gsb])
                m8 = P.tmp("m8", [128, 8, 8], F32)
                if own >= 3:
                    for h in range(8):
                        P.op("dve", lambda e: e.max(out=m8[:, h, :], in_=gsb[:, h, :]), r=[gsb], w=[m8])
                else:
                    P.op("dve", lambda e: e.memset(m8[:], NEG * 0.1), w=[m8])
                mp = P.tmp("mp", [128, 8, 32], BF16)
                P.op("dve", lambda e: e.memset(mp[:], 0.0), w=[mp])
                P.op("dve", lambda e: e.memset(mp[:, :, 16:17], MB), w=[mp])
                selt = P.tmp("selt", [128, 8, 16], F32)
                P.op("dve", lambda e: e.tensor_tensor(out=selt[:], in0=gsb[:], in1=m8[:, :, 2:3].to_broadcast([128, 8, 16]), op=ALU.is_ge), r=[gsb, m8], w=[selt])
                P.op("dve", lambda e: e.tensor_scalar(out=mp[:, :, 0:16], in0=selt[:], scalar1=MB, scalar2=None, op0=ALU.mult), r=[selt], w=[mp])
                pt_ = nxt_pbf()
                for h in range(8):
                    tr(pt_[:32, h * 128:(h + 1) * 128], mp[:, h, :], identb[:, :], r=[mp, identb], w=[pt_])
                P.op("act", lambda e: e.copy(out=maskT[0:32, :, cs], in_=pt_[:32, :].rearrange("p (h q) -> p h q", h=8)), r=[pt_], w=[maskT])
            if cfg.get("astage", 9) <= 0:
                raise StopBuild()
            nkt = 4 * (s + 1)
            for h in range(8):
                rows = slice((h % 2) * 64, (h % 2) * 64 + 64)
                hp = h // 2
                oacc = pf[2 + h % 2]
                first_pv = True
                for kt in range(nkt):
                    blk = kt // 2
                    st_k = kt // 4
                    KTt = KT[st_k]; Vt = V[st_k]
                    kcs = slice((kt % 4) * 128, (kt % 4) * 128 + 128)
                    sc = pf[4 + kt % 2]
                    if blk == 2 * s + 1:
                        c0 = 256
                    else:
                        c0 = 0
                    ops = []
                    ops.append((sc[:, c0:512], KTt[:, hp, kcs], qT[:, h, c0:512], [KTt, qT]))
                    valid = []
                    if blk < 2 * s:
                        ops.append((sc[:, 0:512], esel[:, blk * 128:(blk + 1) * 128], maskT[:, h, 0:512], [esel, maskT]))
                        valid = [0, 1, 2, 3]
                    else:
                        if blk == 2 * s:
                            ops.append((sc[:, 256:512], esel[:, blk * 128:(blk + 1) * 128], maskT[:, h, 256:512], [esel, maskT]))
                            valid = [2, 3]
                            ki = kt - 4 * s; qbase = 0
                        else:
                            ki = kt - 4 * s - 2; qbase = 2
                        for ql in range(2):
                            qi = qbase + ql
                            qcs = slice(qi * 128, (qi + 1) * 128)
                            if ki < ql:
                                ops.append((sc[:, qcs], esel[:, 16 * 128:17 * 128], maskT[:, h, qcs], [esel, maskT]))
                                valid.append(qi)
                            elif ki == ql:
                                ops.append((sc[:, qcs], identb[:, :], causal[:, :], [identb, causal]))
                                valid.append(qi)
                    for oi, (o_, l_, r_, bufs) in enumerate(ops):
                        mm(o_, l_, r_, oi == 0, oi == len(ops) - 1, r=bufs, w=[sc])
                    if cfg.get("astage", 9) <= 1:
                        raise StopBuild()
                    PT = P.tmp("PT", [128, 512], BF16)
                    P.op("act", lambda e: e.activation(out=PT[:, c0:512], in_=sc[:, c0:512], func=AF.Exp, scale=0.125, bias=nmb[:, 0:1]), r=[sc, nmb], w=[PT])
                    if cfg.get("astage", 9) <= 2:
                        raise StopBuild()
                    for qi in sorted(valid):
                        mm(oacc[:, qi * 65:(qi + 1) * 65], PT[:, qi * 128:(qi + 1) * 128], Vt[:, kt % 4, h, 0:65], first_pv, False, r=[PT, Vt], w=[oacc])
                        first_pv = False
                if cfg.get("astage", 9) <= 3:
                    raise StopBuild()
                rs = P.tmp("rs", [128, 4, 1], F32)
                ov = oacc[:, 0:260].rearrange("p (q d) -> p q d", q=4)
                P.op("dve", lambda e: e.reciprocal(out=rs[:], in_=ov[:, :, 64:65]), r=[oacc], w=[rs])
                P.op("dve", lambda e: e.tensor_tensor(out=oatok[:, :, h * 64:(h + 1) * 64], in0=ov[:, :, 0:64], in1=rs[:].to_broadcast([128, 4, 64]), op=ALU.mult), r=[oacc, rs], w=[oatok])
            for qi in range(4):
                pt_ = nxt_pbf()
                for c4 in range(4):
                    tr(pt_[:, c4 * 128:(c4 + 1) * 128], oatok[:, qi, c4 * 128:(c4 + 1) * 128], identb[:, :], r=[oatok, identb], w=[pt_])
                P.op("act", lambda e: e.copy(out=oaT[:, :, qi * 128:(qi + 1) * 128], in_=pt_[:, 0:512].rearrange("p (c t) -> p c t", c=4)), r=[pt_], w=[oaT])

    def attn_sample(j, qT, akT, vtok, oaT, c0):
        NB = NB_S
        cs = slice(c0, c0 + 4)
        with P.scope():
            scst = P.sb("scst", [128, NPG, 32], F32)
            kps = P.sb("kps", [128, 4, NPG], F32)
            Qbd = P.sb("Qbd", [128, 4, 8], BF16)
            ptb = P.sb("ptb", [128, NPG], I32); ptf = P.sb("ptf", [128, NPG], F32); pidx = P.sb("pidx", [128, NPG], I32)
            iot = P.sb("iot", [128, 1], F32)
            P.dma("sp", ptb[:], ptab[j:j + 1, :].partition_broadcast(128), w=[ptb])
            P.op("pool", lambda e: e.iota(iot[:], pattern=[[0, 1]], base=0, channel_multiplier=1, allow_small_or_imprecise_dtypes=True), w=[iot])
            P.op("dve", lambda e: e.tensor_copy(out=ptf[:], in_=ptb[:]), r=[ptb], w=[ptf])
            P.op("dve", lambda e: e.tensor_scalar(out=iot[64:128, :], in0=iot[64:128, :], scalar1=-64.0, scalar2=None, op0=ALU.add), r=[iot], w=[iot])
            idf = P.sb("idf", [128, NB], F32)
            ptv = ptf[:].rearrange("p (b t) -> p b t", t=2)
            P.op("dve", lambda e: e.tensor_scalar(out=idf[0:64, :], in0=ptv[0:64, :, 0], scalar1=64.0, scalar2=iot[0:64, 0:1], op0=ALU.mult, op1=ALU.add), r=[ptf, iot], w=[idf])
            P.op("dve", lambda e: e.tensor_scalar(out=idf[64:128, :], in0=ptv[64:128, :, 1], scalar1=64.0, scalar2=iot[64:128, 0:1], op0=ALU.mult, op1=ALU.add), r=[ptf, iot], w=[idf])
            P.op("dve", lambda e: e.tensor_copy(out=pidx[:, 0:NB], in_=idf[:]), r=[idf], w=[pidx])
            qv = qT[:, :, cs].rearrange("p (a two) t -> p a two t", two=2)
            P.op("dve", lambda e: e.tensor_copy(out=Qbd[:, :, 0:4], in_=qv[:, :, 0, :]), r=[qT], w=[Qbd])
            P.op("dve", lambda e: e.tensor_copy(out=Qbd[:, :, 4:8], in_=qv[:, :, 1, :]), r=[qT], w=[Qbd])
            if cfg.get("sastage", 99) <= 0:
                raise StopBuild()
            scps = None
            cache_k2 = cache_k.rearrange("(r two) c -> r (two c)", two=2)
            cache_v2 = cache_v.rearrange("(r two) c -> r (two c)", two=2)
            Kp = None
            for p in range(NPG):
                if p % 2 == 0:
                    Kp = P.tmp("Kp", [128, 1024], BF16, bufs=4)
                    P.idma(out=Kp[:], in_=cache_k2, in_offset=bass.IndirectOffsetOnAxis(ap=pidx[:, p // 2:p // 2 + 1], axis=0), r=[pidx], w=[Kp])
                so = (p % 2) * 512
                pt_ = nxt_pbf()
                for c4 in range(4):
                    tr(pt_[:, c4 * 128:(c4 + 1) * 128], Kp[:, so + c4 * 128:so + (c4 + 1) * 128], identb[:, :], r=[Kp, identb], w=[pt_])
                KTt = P.tmp("KTt", [128, 4, 128], BF16, bufs=3)
                P.op("act", lambda e: e.copy(out=KTt[:], in_=pt_[:, 0:512].rearrange("p (c t) -> p c t", c=4)), r=[pt_], w=[KTt])
                P.op("dve", lambda e: e.tensor_reduce(out=kps[:, :, p:p + 1], in_=KTt[:], axis=AX.X, op=ALU.add), r=[KTt], w=[kps])
                if p % 16 == 0:
                    scps = pf[4 + (p // 16) % 2]
                for hp in range(4):
                    o0 = (p % 16) * 32 + hp * 8
                    mm(scps[:, o0:o0 + 8], KTt[:, hp, :], Qbd[:, hp, :], True, True, r=[KTt, Qbd], w=[scps])
                if p % 16 == 15 or p == NPG - 1:
                    p0 = (p // 16) * 16
                    npp = p - p0 + 1
                    P.op("act", lambda e: e.copy(out=scst[:, p0:p0 + npp, :], in_=scps[:, 0:npp * 32].rearrange("p (a b) -> p a b", b=32)), r=[scps], w=[scst])
            if cfg.get("sastage", 99) <= 1:
                raise StopBuild()
            pn = nxt_pf()
            for hp in range(4):
                mm(pn[:4, hp * 8:(hp + 1) * 8], akT[:, hp, cs], Qbd[:, hp, :], True, True, r=[akT, Qbd], w=[pn])
            scn = P.sb("scn", [4, 32], F32); pnew = P.sb("pnew", [128, 32], BF16)
            P.op("dve", lambda e: e.memset(pnew[:], 0.0), w=[pnew])
            P.op("dve", lambda e: e.tensor_tensor(out=scn[:], in0=pn[:4, 0:32], in1=cm4[:], op=ALU.add), r=[pn, cm4], w=[scn])
            P.op("act", lambda e: e.activation(out=pnew[:4, :], in_=scn[:], func=AF.Exp, scale=0.125, bias=nmb[:4, 0:1]), r=[scn, nmb], w=[pnew])
            if cfg.get("sastage", 99) <= 2:
                raise StopBuild()
            kmS = P.sb("kmS", [128, 4, NB], F32); kmb = P.sb("kmb", [128, 4, NB], BF16)
            P.op("dve", lambda e: e.tensor_reduce(out=kmS[:], in_=kps[:].rearrange("p c (b t) -> p c b t", t=2), axis=AX.X, op=ALU.add), r=[kps], w=[kmS])
            P.op("act", lambda e: e.activation(out=kmb[:], in_=kmS[:], func=AF.Copy, scale=1.0 / 256.0), r=[kmS], w=[kmb])
            pg = nxt_pf()
            for h in range(8):
                mm(pg[:4, h * NB:(h + 1) * NB], qT[:, h, cs], kmb[:, h // 2, :], True, True, r=[qT, kmb], w=[pg])
            gsb = P.sb("gsbs", [4, 8, NB], F32); m8 = P.sb("m8s", [4, 8, 8], F32)
            P.op("act", lambda e: e.copy(out=gsb[:], in_=pg[:4, 0:8 * NB].rearrange("p (h b) -> p h b", h=8)), r=[pg], w=[gsb])
            for h in range(8):
                P.op("dve", lambda e: e.max(out=m8[:, h, :], in_=gsb[:, h, :]), r=[gsb], w=[m8])
            selt = P.sb("selts", [4, 8, NB], F32)
            P.op("dve", lambda e: e.tensor_tensor(out=selt[:], in0=gsb[:], in1=m8[:, :, 2:3].to_broadcast([4, 8, NB]), op=ALU.is_ge), r=[gsb, m8], w=[selt])
            mpT = P.sb("mpT", [4, NB, 8], BF16)
            P.op("dve", lambda e: e.tensor_scalar(out=mpT[:], in0=selt[:].rearrange("p h b -> p b h"), scalar1=MB, scalar2=None, op0=ALU.mult), r=[selt], w=[mpT])
            X = P.sb("X", [128, NB * 8, 4], BF16)
            P.op("dve", lambda e: e.memset(X[:], 0.0), w=[X])
            P.op("dve", lambda e: e.tensor_tensor(out=X[:4], in0=mpT[:].rearrange("p b h -> p (b h)").unsqueeze(2).to_broadcast([4, NB * 8, 4]),
                                                  in1=eye4[:].unsqueeze(1).to_broadcast([4, NB * 8, 4]), op=ALU.mult), r=[mpT, eye4], w=[X])
            if cfg.get("sastage", 99) <= 3:
                raise StopBuild()
            PTs = P.sb("PTs", [128, NPG, 32], BF16)
            Xf = X[:].rearrange("p a b -> p (a b)")
            nbank = (NB * 32 + 511) // 512
            for bk in range(nbank):
                w0 = bk * 512; w1 = min(NB * 32, w0 + 512); wd = w1 - w0
                pm = nxt_pf()
                mm(pm[:, 0:wd], onesb[:, :], Xf[:, w0:w1], True, True, r=[onesb, X], w=[pm])
                nblk = wd // 32; b0 = w0 // 32
                sm = P.tmp("sm", [128, 16, 2, 32], F32)
                P.op("dve", lambda e: e.tensor_tensor(out=sm[:, 0:nblk], in0=scst[:, 2 * b0:2 * (b0 + nblk), :].rearrange("p (b t) c -> p b t c", t=2),
                                                      in1=pm[:, 0:wd].rearrange("p (b c) -> p b c", c=32).unsqueeze(2).to_broadcast([128, nblk, 2, 32]), op=ALU.add), r=[scst, pm], w=[sm])
                P.op("act", lambda e: e.activation(out=PTs[:, 2 * b0:2 * (b0 + nblk), :].rearrange("p (b t) c -> p b t c", t=2), in_=sm[:, 0:nblk], func=AF.Exp, scale=0.125, bias=nmb[:, 0:1]), r=[sm, nmb], w=[PTs])
            if cfg.get("sastage", 99) <= 4:
                raise StopBuild()
            oS = pf[2]; rS = pf[3]
            Vp = None
            for p in range(NPG):
                if p % 2 == 0:
                    Vp = P.tmp("Vp", [128, 1024], BF16, bufs=4)
                    P.idma(out=Vp[:], in_=cache_v2, in_offset=bass.IndirectOffsetOnAxis(ap=pidx[:, p // 2:p // 2 + 1], axis=0), r=[pidx], w=[Vp])
                so = (p % 2) * 512
                mm(oS[:32, :], PTs[:, p, :], Vp[:, so:so + 512], p == 0, False, r=[PTs, Vp], w=[oS])
                mm(rS[:32, 0:1], PTs[:, p, :], onesb[:, 0:1], p == 0, False, r=[PTs, onesb], w=[rS])
            jt = c0 // 128
            mm(oS[:32, :], pnew[:], vtok[:, jt, :], False, True, r=[pnew, vtok], w=[oS])
            mm(rS[:32, 0:1], pnew[:], onesb[:, 0:1], False, True, r=[pnew, onesb], w=[rS])
            if cfg.get("sastage", 99) <= 5:
                raise StopBuild()
            rr_ = P.sb("rrs", [32, 1], F32); msk = P.sb("msk", [128, 512], BF16)
            P.op("dve", lambda e: e.memset(msk[:], 0.0), w=[msk])
            P.op("dve", lambda e: e.reciprocal(out=rr_[:], in_=rS[:32, 0:1]), r=[rS], w=[rr_])
            P.op("dve", lambda e: e.scalar_tensor_tensor(out=msk[:32, :], in0=oS[:32, :], scalar=rr_[:, 0:1], in1=hmask[:], op0=ALU.mult, op1=ALU.mult), r=[oS, rr_, hmask], w=[msk])
            po = nxt_pf()
            mm(po[:4, :], selq[:], msk[:], True, True, r=[selq, msk], w=[po])
            oat = P.sb("oat", [128, 512], BF16)
            P.op("dve", lambda e: e.memset(oat[:], 0.0), w=[oat])
            P.op("act", lambda e: e.copy(out=oat[:4, :], in_=po[:4, :]), r=[po], w=[oat])
            pt_ = nxt_pbf()
            for c4 in range(4):
                tr(pt_[:, c4 * 128:(c4 + 1) * 128], oat[:, c4 * 128:(c4 + 1) * 128], identb[:, :], r=[oat, identb], w=[pt_])
            P.op("act", lambda e: e.copy(out=oaT[:, :, cs], in_=pt_[:, 0:512].rearrange("p (c t) -> p c t", c=4)[:, :, 0:4]), r=[pt_], w=[oaT])

    def main_program():
        if cfg.get("do_prompt", 1):
            P.op("dve", lambda e: e.memset(HS[:], 0.0), w=[HS])
            for h in range(4):
                P.op("dve", lambda e: e.memset(Sst[h][:], 0.0), w=[Sst[h]])
                P.op("dve", lambda e: e.memset(Sbf[h][:], 0.0), w=[Sbf[h]])
            with P.scope():
                ctx = {"KT": [P.sb("KT%d" % s, [128, 4, 512], BF16) for s in range(NST)],
                       "V": [P.sb("V%d" % s, [128, 4, 8, 66], BF16) for s in range(NST)],
                       "kmT": P.sb("kmT", [128, 4, 16], BF16)}
                P.op("dve", lambda e: e.memset(ctx["kmT"][:], 0.0), w=[ctx["kmT"]])
                for s in range(NST):
                    P.op("dve", lambda e: e.memset(ctx["V"][s][:], 1.0), w=[ctx["V"][s]])
                for s in range(NST):
                    tiles = []
                    for i in range(4):
                        r0 = s * 512 + i * 128
                        tiles.append((0, 128, xp[r0:r0 + 128, :], {"k": kp[r0:r0 + 128, :], "v": vp[r0:r0 + 128, :], "y": yp[r0:r0 + 128, :]}))
                    supertile(tiles, True, s, ctx)
                for h in range(4):
                    P.dma("pool", hgp[h], Sst[h][:], r=[Sst[h]], is_output=True)
                P.dma("pool", cvp, HS[:, 0], r=[HS], is_output=True)
        if cfg.get("do_sample", 1):
            tiles = []
            for j in range(NS):
                tiles.append(((1 + j) if not cfg.get("dbg_seq0") else 0, 128, xs[j * 128:(j + 1) * 128, :], {"k": ksn[j * 128:(j + 1) * 128, :], "v": vsn[j * 128:(j + 1) * 128, :], "y": ys[j * 128:(j + 1) * 128, :]}))
            supertile(tiles, False, 0, None)


    try:
        main_program()
    except StopBuild:
        pass
    P.finish()
    return P


def host_consts():
    bf = ml_dtypes.bfloat16
    c = {}
    c["c_identb"] = np.eye(128, dtype=np.float32).astype(bf)
    c["c_identf"] = np.eye(128, dtype=np.float32)
    s = np.arange(128)
    c["c_tri"] = (s[:, None] <= s[None, :]).astype(np.float32)
    c["c_causal"] = ((s[:, None] <= s[None, :]).astype(np.float32) * MB).astype(bf)
    es = np.zeros((128, 17, 128), np.float32)
    for j in range(17):
        es[j, j, :] = 1.0
    c["c_esel"] = es.reshape(128, 17 * 128).astype(bf)
    cm = np.zeros((4, 8, 4), np.float32)
    for k in range(4):
        for q in range(4):
            cm[k, :, q] = MB if k <= q else 0.0
    c["c_cm4"] = cm.reshape(4, 32)
    c["c_eye4"] = np.eye(4, dtype=np.float32).astype(bf)
    hm = np.zeros((8, 4, 8, 64), np.float32)
    for h in range(8):
        hm[h, :, h, :] = 1.0
    c["c_hmask"] = hm.reshape(32, 512)
    sq = np.zeros((8, 4, 4), np.float32)
    for q in range(4):
        sq[:, q, q] = 1.0
    sqp = np.zeros((128, 4), np.float32)
    sqp[:32] = sq.reshape(32, 4)
    c["c_selq"] = sqp.astype(bf)
    return c


def make_in_maps(cfg, ncores, x_prompt, x_sample, cache_k, cache_v, state_hgrn, state_conv, page_table, c_prompt, c_sample,
                 norm1_g, norm2_g, w_ada, b_ada, w_in, hgrn_lb_logits, hg_norm_g, w_a_out, w_b_out, w_o,
                 w_up, conv_w, conv_b, w_down, final_g):
    T = cfg["T"]; NS = cfg["NS"]; NPG = cfg["NPG"]; NPHYS = cfg["NPHYS"]
    f = np.float32
    A = lambda a: np.ascontiguousarray(np.asarray(a))
    consts = host_consts()
    shared = dict(consts)
    shared["cache_k"] = A(np.asarray(cache_k)[0, :NPHYS]).reshape(NPHYS * 128, 512)
    shared["cache_v"] = A(np.asarray(cache_v)[0, :NPHYS]).reshape(NPHYS * 128, 512)
    shared["w_ada"] = A(w_ada)[0]; shared["w_in"] = A(w_in)[0]; shared["w_a_out"] = A(w_a_out)[0]
    shared["w_b_out"] = A(w_b_out)[0]; shared["w_o"] = A(w_o)[0]; shared["w_up"] = A(w_up)[0]; shared["w_down"] = A(w_down)[0]
    shared["n1gT"] = A(A(norm1_g)[0].reshape(KD, 128).T); shared["n2gT"] = A(A(norm2_g)[0].reshape(KD, 128).T)
    shared["badaT"] = A(A(b_ada)[0].reshape(48, 128).T)
    shared["lbT"] = A(A(hgrn_lb_logits).reshape(2, 4, 128).transpose(2, 1, 0))
    shared["convwT"] = A(A(conv_w)[0].reshape(3, KF, 128).transpose(2, 1, 0))
    shared["convbT"] = A(A(conv_b)[0].reshape(KF, 128).T)
    shared["hgn_bc"] = A(np.broadcast_to(A(hg_norm_g)[0][None, :], (128, 512)))
    shared["fg_bc"] = A(np.broadcast_to(A(final_g)[None, :], (128, D)))
    maps = []
    for c in range(ncores):
        m = dict(shared)
        m["xp"] = A(x_prompt)[c, :T] if T > 0 else np.zeros((0, D), f)
        xsp = np.zeros((NS, 128, D), f)
        xsp[:, :4, :] = A(x_sample)[c * NS:(c + 1) * NS]
        m["xs"] = xsp.reshape(NS * 128, D)
        cc = np.concatenate([A(c_prompt)[c:c + 1], A(c_sample)[c * NS:(c + 1) * NS]], axis=0)
        m["cT"] = A(cc.reshape(1 + NS, KD, 128).transpose(2, 1, 0))
        m["ptab"] = A(page_table)[c * NS:(c + 1) * NS, :NPG].astype(np.int32)
        m["shg"] = A(state_hgrn)[0, c * NS:(c + 1) * NS]
        m["convT"] = A(A(state_conv)[0, c * NS:(c + 1) * NS].reshape(NS, 2, KF, 128).transpose(0, 3, 2, 1))
        maps.append(m)
    return maps


def gather_outputs(cfg, ncores, res):
    T = cfg["T"]; NS = cfg["NS"]
    R = res.results
    y_prompt = np.stack([R[c]["yp"] for c in range(ncores)])
    y_sample = np.concatenate([R[c]["ys"].reshape(NS, 128, D)[:, :4] for c in range(ncores)])
    k_prompt = np.stack([R[c]["kp"].reshape(T, 8, 64) for c in range(ncores)])[None]
    v_prompt = np.stack([R[c]["vp"].reshape(T, 8, 64) for c in range(ncores)])[None]
    k_sample = np.concatenate([R[c]["ksn"].reshape(NS, 128, 8, 64)[:, :4] for c in range(ncores)])[None]
    v_sample = np.concatenate([R[c]["vsn"].reshape(NS, 128, 8, 64)[:, :4] for c in range(ncores)])[None]
    hg_p = np.stack([R[c]["hgp"] for c in range(ncores)])[None]
    hg_s = np.concatenate([R[c]["hgs"] for c in range(ncores)])[None]
    cv_p = np.stack([R[c]["cvp"].transpose(2, 1, 0).reshape(2, FF) for c in range(ncores)])[None]
    cv_s = np.concatenate([R[c]["cvs"].transpose(0, 3, 2, 1).reshape(NS, 2, FF) for c in range(ncores)])[None]
    outs = (y_prompt, y_sample, k_prompt, v_prompt, k_sample, v_sample, hg_p, hg_s, cv_p, cv_s)
    return tuple(np.ascontiguousarray(o.astype(np.float32)) for o in outs)


def kernel(**inputs):
    cfg = {"T": 4096, "NS": 4, "NPG": 128, "NPHYS": 5120, "do_prompt": 1, "do_sample": 1}
    nc = bass.Bass("TRN2", target_bir_lowering=False)
    build(nc, cfg)
    maps = make_in_maps(cfg, 8, **inputs)
    res = run_bass_kernel_spmd(nc, maps, core_ids=list(range(8)))
    return gather_outputs(cfg, 8, res)
```
